# Optimizing a Trainium2 kernel written in Bass

```python
import jax, jax.numpy as jnp
from jax import lax
import numpy as np

D_MODEL = 1024
BATCH = 8
SEQ = 4096
DEPTH = 4

N_MIXERS = 2
N_MAMBA = (DEPTH + 1) // 2
N_HGRN = DEPTH // 2
EPS = 1e-6
D_FF = 2816
M_EXPAND = 2
M_INNER = M_EXPAND * D_MODEL
M_HEADDIM = 64
M_HEADS = M_INNER // M_HEADDIM
M_GROUPS = 4
M_STATE = 128
M_CONV = 4
M_CHUNK = 128
M_CONV_DIM = M_INNER + 2 * M_GROUPS * M_STATE
M_PROJ = 2 * M_INNER + 2 * M_GROUPS * M_STATE + M_HEADS
H_EXPAND = 128
H_HEADS = D_MODEL // H_EXPAND
H_KEY = H_HEADS * H_EXPAND
H_VAL = D_MODEL
H_HEAD_V = H_VAL // H_HEADS
H_CHUNK = 16
H_PROJ = 2 * H_KEY + 2 * H_VAL

kernel_name = 'hybrid_mamba2_hgrn2_macaron'


def rmsnorm(x, w):
    xf = x.astype(jnp.float32)
    xf = xf * lax.rsqrt(jnp.mean(xf * xf, axis=-1, keepdims=True) + EPS)
    return (xf * w.astype(jnp.float32)).astype(x.dtype)


def swiglu(u, w_gate, w_up, w_down):
    return (jax.nn.silu(u @ w_gate) * (u @ w_up)) @ w_down


def causal_dwconv(x, w, b):
    y = lax.conv_general_dilated(
        x, w[:, None, :].astype(x.dtype), window_strides=(1,),
        padding=[(M_CONV - 1, 0)], dimension_numbers=('NWC', 'WIO', 'NWC'),
        feature_group_count=x.shape[-1])
    return y + b.astype(x.dtype)


def ssd_chunked(xh, dA, Bm, Cm):
    b, l, h, p = xh.shape
    g, n = Bm.shape[-2:]
    r = h // g
    c = l // M_CHUNK
    X = xh.reshape(b, c, M_CHUNK, g, r, p)
    Bc = Bm.reshape(b, c, M_CHUNK, g, n)
    Cc = Cm.reshape(b, c, M_CHUNK, g, n)
    dA = dA.reshape(b, c, M_CHUNK, g, r).transpose(0, 3, 4, 1, 2)
    cs = jnp.cumsum(dA, axis=-1)
    tril = jnp.tril(jnp.ones((M_CHUNK, M_CHUNK), dtype=bool))
    seg = cs[..., :, None] - cs[..., None, :]
    Lmat = jnp.exp(jnp.where(tril, seg, -jnp.inf))
    CB = jnp.einsum('bclgn,bcsgn->bcgls', Cc, Bc)
    y_diag = jnp.einsum('bcgls,bgrcls,bcsgrp->bclgrp', CB, Lmat, X)
    decay_states = jnp.exp(cs[..., -1:] - cs)
    states = jnp.einsum('bcsgn,bgrcs,bcsgrp->bcgrpn', Bc, decay_states, X)
    chunk_decay = jnp.exp(cs[..., -1])

    def step(hstate, inp):
        st, dec = inp
        return dec[..., None, None] * hstate + st, hstate

    h0 = jnp.zeros((b, g, r, p, n), dtype=X.dtype)
    _, prev = lax.scan(step, h0, (states.transpose(1, 0, 2, 3, 4, 5), chunk_decay.transpose(3, 0, 1, 2)))
    y_off = jnp.einsum('bclgn,cbgrpn,bgrcl->bclgrp', Cc, prev, jnp.exp(cs))
    return (y_diag + y_off).reshape(b, l, h, p)


def mamba2_mixer(u, w_in, conv_w, conv_b, dt_bias, a_log, d_skip, norm_w, w_out):
    b, l, _ = u.shape
    zxbcdt = u @ w_in
    z, xbc, dt = jnp.split(zxbcdt, [M_INNER, M_INNER + M_CONV_DIM], axis=-1)
    xbc = jax.nn.silu(causal_dwconv(xbc, conv_w, conv_b)).astype(jnp.float32)
    xs, Bm, Cm = jnp.split(xbc, [M_INNER, M_INNER + M_GROUPS * M_STATE], axis=-1)
    dt = jax.nn.softplus(dt.astype(jnp.float32) + dt_bias.astype(jnp.float32))
    A = -jnp.exp(a_log.astype(jnp.float32))
    xh = xs.reshape(b, l, M_HEADS, M_HEADDIM)
    y = ssd_chunked(xh * dt[..., None], dt * A,
                    Bm.reshape(b, l, M_GROUPS, M_STATE), Cm.reshape(b, l, M_GROUPS, M_STATE))
    y = y + d_skip.astype(jnp.float32)[:, None] * xh
    yg = (y.reshape(b, l, M_INNER) * jax.nn.silu(z.astype(jnp.float32))).reshape(b, l, M_GROUPS, -1)
    yg = yg * lax.rsqrt(jnp.mean(yg * yg, axis=-1, keepdims=True) + EPS)
    y = yg.reshape(b, l, M_INNER) * norm_w.astype(jnp.float32)
    return y.astype(u.dtype) @ w_out


def hgrn2_chunked(q, k, v, logf):
    b, l, h, dk = q.shape
    dv = v.shape[-1]
    c = l // H_CHUNK

    def to_chunks(t):
        return t.reshape(b, c, H_CHUNK, h, t.shape[-1]).transpose(1, 0, 3, 2, 4)

    tril = jnp.tril(jnp.ones((H_CHUNK, H_CHUNK), dtype=bool))[:, :, None]

    def step(S, inp):
        qc, kc, vc, gc = inp
        cs = jnp.cumsum(gc, axis=-2)
        o_inter = jnp.einsum('bhld,bhdv->bhlv', qc * jnp.exp(cs), S)
        decay = jnp.exp(jnp.where(tril, cs[:, :, :, None, :] - cs[:, :, None, :, :], -jnp.inf))
        attn = jnp.einsum('bhld,bhlsd,bhsd->bhls', qc, decay, kc)
        o = o_inter + jnp.einsum('bhls,bhsv->bhlv', attn, vc)
        cs_last = cs[:, :, -1:, :]
        S_new = jnp.exp(cs_last[:, :, 0, :])[..., None] * S + jnp.einsum(
            'bhsd,bhsv->bhdv', kc * jnp.exp(cs_last - cs), vc)
        return S_new, o

    S0 = jnp.zeros((b, h, dk, dv), dtype=q.dtype)
    _, o = lax.scan(step, S0, (to_chunks(q), to_chunks(k), to_chunks(v), to_chunks(logf)))
    return o.transpose(1, 0, 3, 2, 4).reshape(b, l, h, dv)


def hgrn2_mixer(u, w_in, lower_bound, norm_w, w_out):
    b, l, _ = u.shape
    q, f, v, g = jnp.split((u @ w_in).astype(jnp.float32), [H_KEY, 2 * H_KEY, 2 * H_KEY + H_VAL], axis=-1)
    q = jax.nn.silu(q)
    forget = lower_bound + (1.0 - lower_bound) * jax.nn.sigmoid(f)
    k = 1.0 - forget
    logf = jnp.log(forget)
    o = hgrn2_chunked(q.reshape(b, l, H_HEADS, H_EXPAND), k.reshape(b, l, H_HEADS, H_EXPAND),
                      v.reshape(b, l, H_HEADS, H_HEAD_V), logf.reshape(b, l, H_HEADS, H_EXPAND))
    o = o * lax.rsqrt(jnp.mean(o * o, axis=-1, keepdims=True) + EPS) * norm_w.astype(jnp.float32)
    o = o.reshape(b, l, H_VAL) * jax.nn.silu(g)
    return o.astype(u.dtype) @ w_out


def setup_inputs(seed: int = 0) -> dict:
    key = jax.random.key(seed)
    ks = jax.random.split(key, 20)
    nrm = jax.random.normal
    x = nrm(ks[0], (BATCH, SEQ, D_MODEL), jnp.float32)
    norm_w = 1.0 + 0.02 * nrm(ks[1], (DEPTH, 3, D_MODEL), jnp.float32)
    ffn_w_gate = nrm(ks[2], (DEPTH, 2, D_MODEL, D_FF), jnp.float32) * D_MODEL ** -0.5
    ffn_w_up = nrm(ks[3], (DEPTH, 2, D_MODEL, D_FF), jnp.float32) * D_MODEL ** -0.5
    ffn_w_down = nrm(ks[4], (DEPTH, 2, D_FF, D_MODEL), jnp.float32) * D_FF ** -0.5
    m_w_in = nrm(ks[5], (N_MAMBA, D_MODEL, M_PROJ), jnp.float32) * D_MODEL ** -0.5
    m_conv_w = nrm(ks[6], (N_MAMBA, M_CONV, M_CONV_DIM), jnp.float32) * M_CONV ** -0.5
    m_conv_b = 0.01 * nrm(ks[7], (N_MAMBA, M_CONV_DIM), jnp.float32)
    u = jax.random.uniform(ks[8], (N_MAMBA, M_HEADS), jnp.float32)
    dt0 = jnp.exp(u * (jnp.log(0.1) - jnp.log(0.001)) + jnp.log(0.001))
    m_dt_bias = dt0 + jnp.log(-jnp.expm1(-dt0))
    m_a_log = jnp.log(jax.random.uniform(ks[9], (N_MAMBA, M_HEADS), jnp.float32, 1.0, 16.0))
    m_d = 1.0 + 0.02 * nrm(ks[10], (N_MAMBA, M_HEADS), jnp.float32)
    m_norm_w = 1.0 + 0.02 * nrm(ks[11], (N_MAMBA, M_INNER), jnp.float32)
    m_w_out = nrm(ks[12], (N_MAMBA, M_INNER, D_MODEL), jnp.float32) * M_INNER ** -0.5
    h_w_in = nrm(ks[13], (N_HGRN, D_MODEL, H_PROJ), jnp.float32) * D_MODEL ** -0.5
    h_lb_logits = 0.1 * nrm(ks[14], (DEPTH, H_KEY), jnp.float32)
    h_norm_w = 1.0 + 0.02 * nrm(ks[15], (N_HGRN, H_HEAD_V), jnp.float32)
    h_w_out = nrm(ks[16], (N_HGRN, H_VAL, D_MODEL), jnp.float32) * H_VAL ** -0.5
    final_norm_w = 1.0 + 0.02 * nrm(ks[17], (D_MODEL,), jnp.float32)
    return {'x': x, 'norm_w': norm_w, 'ffn_w_gate': ffn_w_gate, 'ffn_w_up': ffn_w_up,
            'ffn_w_down': ffn_w_down, 'm_w_in': m_w_in, 'm_conv_w': m_conv_w, 'm_conv_b': m_conv_b,
            'm_dt_bias': m_dt_bias, 'm_a_log': m_a_log, 'm_d': m_d, 'm_norm_w': m_norm_w,
            'm_w_out': m_w_out, 'h_w_in': h_w_in, 'h_lb_logits': h_lb_logits, 'h_norm_w': h_norm_w,
            'h_w_out': h_w_out, 'final_norm_w': final_norm_w}


def reference(x, norm_w, ffn_w_gate, ffn_w_up, ffn_w_down, m_w_in, m_conv_w, m_conv_b,
              m_dt_bias, m_a_log, m_d, m_norm_w, m_w_out, h_w_in, h_lb_logits, h_norm_w,
              h_w_out, final_norm_w):
    s = jax.nn.softmax(h_lb_logits.astype(jnp.float32), axis=0)
    lower_bounds = jnp.cumsum(s, axis=0) - s[0]
    for i in range(DEPTH):
        x = x + 0.5 * swiglu(rmsnorm(x, norm_w[i, 0]), ffn_w_gate[i, 0], ffn_w_up[i, 0], ffn_w_down[i, 0])
        u = rmsnorm(x, norm_w[i, 1])
        j = i // N_MIXERS
        if i % N_MIXERS == 0:
            mix = mamba2_mixer(u, m_w_in[j], m_conv_w[j], m_conv_b[j], m_dt_bias[j], m_a_log[j],
                               m_d[j], m_norm_w[j], m_w_out[j])
        else:
            mix = hgrn2_mixer(u, h_w_in[j], lower_bounds[i], h_norm_w[j], h_w_out[j])
        x = x + mix
        x = x + 0.5 * swiglu(rmsnorm(x, norm_w[i, 2]), ffn_w_gate[i, 1], ffn_w_up[i, 1], ffn_w_down[i, 1])
    return rmsnorm(x, final_norm_w)
```

```python
import numpy as np
import concourse.bass as bass
import concourse.mybir as mybir
from concourse.bass_utils import run_bass_kernel_spmd

F32 = mybir.dt.float32
BF16 = mybir.dt.bfloat16
AF = mybir.ActivationFunctionType
ALU = mybir.AluOpType

ENGS = ("pe", "act", "dve", "pool", "sp")


class Buf:
    __slots__ = ("name", "lw", "rd", "dsem", "dcnt")

    def __init__(self, name):
        self.name = name
        self.lw = None
        self.rd = []
        self.dsem = None
        self.dcnt = 0


class Op:
    __slots__ = ("eng", "fn", "deps", "marked", "dma", "idx")


class Sched:
    SAME_ENG_DIST = 10 ** 9

    def __init__(self, nc):
        self.nc = nc
        self.ops = {e: [] for e in ENGS}
        self.dma_bufs = []

    def _record(self, eng, fn, reads, writes, dma=None):
        op = Op()
        op.eng = eng
        op.fn = fn
        op.marked = False
        op.dma = dma
        op.idx = len(self.ops[eng])
        deps = []
        for b in reads:
            if b.lw is not None:
                deps.append(b.lw)
        for b in writes:
            if b.lw is not None:
                deps.append(b.lw)
            deps.extend(b.rd)
        out = []
        seen = set()
        for d in deps:
            key = (d[0], id(d[1]) if d[0] == "d" else d[1], d[2])
            if key in seen:
                continue
            seen.add(key)
            if d[0] == "e" and d[1] == eng and dma is None:
                if eng == "pe":
                    continue
                if op.idx - d[2] > self.SAME_ENG_DIST:
                    continue
            out.append(d)
        op.deps = out
        self.ops[eng].append(op)
        if dma is not None:
            if dma.dsem is None:
                self.dma_bufs.append(dma)
                dma.dsem = True
            dma.dcnt += 16
            tok = ("d", dma, dma.dcnt)
        else:
            tok = ("e", eng, op.idx)
        for b in reads:
            b.rd.append(tok)
        for b in writes:
            b.lw = tok
            b.rd = []
        return op

    def op(self, eng, fn, reads=(), writes=()):
        return self._record(eng, fn, reads, writes)

    @staticmethod
    def fence(old_bufs, new_bufs):
        toks = []
        for b in old_bufs:
            if b.lw is not None:
                toks.append(b.lw)
            toks.extend(b.rd)
        best = {}
        rest = []
        for t in toks:
            if t[0] == "e":
                if t[1] not in best or best[t[1]][2] < t[2]:
                    best[t[1]] = t
            else:
                rest.append(t)
        toks = list(best.values()) + rest
        for b in new_bufs:
            b.rd.extend(toks)

    def dma(self, eng, out, in_, reads=(), writes=(), sem=None):
        if sem is None:
            sem = (list(writes) + list(reads))[0]
        return self._record(eng, lambda e: e.dma_start(out=out, in_=in_), reads, writes, dma=sem)

    def emit(self):
        nc = self.nc
        for e in ENGS:
            for op in self.ops[e]:
                for d in op.deps:
                    if d[0] == "e":
                        self.ops[d[1]][d[2]].marked = True
        counts = {}
        for e in ENGS:
            c = 0
            arr = []
            for op in self.ops[e]:
                if op.marked:
                    c += 1
                arr.append(c)
            counts[e] = arr
        from contextlib import ExitStack
        with ExitStack() as st:
            esem = {e: st.enter_context(nc.semaphore("sem_" + e)) for e in ENGS}
            for i, b in enumerate(self.dma_bufs):
                b.dsem = st.enter_context(nc.semaphore("dsem%d" % i))
            block = st.enter_context(nc.Block())

            def run(e, eng):
                seen = {}
                for op in self.ops[e]:
                    for d in op.deps:
                        if d[0] == "e":
                            sem = esem[d[1]]
                            val = counts[d[1]][d[2]]
                        else:
                            sem = d[1].dsem
                            val = d[2]
                        k = id(sem)
                        if seen.get(k, 0) >= val:
                            continue
                        seen[k] = val
                        eng.wait_ge(sem, val)
                    ins = op.fn(eng)
                    if op.dma is not None:
                        ins.then_inc(op.dma.dsem, 16)
                    elif op.marked:
                        ins.then_inc(esem[e], 1)
                for b in self.dma_bufs:
                    pass

            final = [(b.dsem, b.dcnt) for b in self.dma_bufs]

            @block.tensor
            def _(eng):
                run("pe", eng)

            @block.scalar
            def _(eng):
                run("act", eng)

            @block.vector
            def _(eng):
                run("dve", eng)

            @block.gpsimd
            def _(eng):
                run("pool", eng)

            @block.sync
            def _(eng):
                run("sp", eng)
                for sem, cnt in final:
                    eng.wait_ge(sem, cnt)


D = 1024
KD = D // 128
DFF = 2816
KF = DFF // 128
SEQ = 4096
DEPTH = 4
EPS = 1e-6
T = 1024
NH = T // 512
TM = 512
NHD = 8
HC = 32


class Ring:
    def __init__(self, name, aps):
        self.aps = aps
        self.bufs = [Buf("%s%d" % (name, i)) for i in range(len(aps))]
        self.i = 0

    def next(self):
        i = self.i
        self.i = (i + 1) % len(self.aps)
        return self.aps[i], self.bufs[i]


class Ctx:
    pass


def _mm(S, out, lhsT, rhs, start, stop, reads, writes):
    S.op("pe", lambda e: e.matmul(out, lhsT=lhsT, rhs=rhs, start=start, stop=stop),
         reads=reads, writes=writes)


def _act(S, out, in_, func, reads, writes, **kw):
    S.op("act", lambda e: e.activation(out=out, in_=in_, func=func, **kw), reads=reads, writes=writes)


def _tt(S, eng, out, in0, in1, op, reads, writes):
    S.op(eng, lambda e: e.tensor_tensor(out=out, in0=in0, in1=in1, op=op), reads=reads, writes=writes)


def emit_rmsnorm(S, c, wcol):
    banks = [c.bank() for _ in range(NH)]
    for k in range(KD):
        sq, bsq = c.sq_ring.next()
        _act(S, sq, c.xT[:, k, :], AF.Square, [c.BxT[k]], [bsq])
        for h in range(NH):
            ps, bps = banks[h]
            _mm(S, ps, c.ones_bf[:, :], sq[:, h * 512:(h + 1) * 512], k == 0, k == KD - 1,
                [bsq, c.Bconst], [bps])
    for h in range(NH):
        ps, bps = banks[h]
        sl = slice(h * 512, (h + 1) * 512)
        _act(S, c.rstd[:, sl], ps, AF.Sqrt, [bps, c.Bconst], [c.Brstd[h]], scale=1.0 / D, bias=c.eps_col[:, 0:1])
        S.op("dve", lambda e, sl=sl: e.reciprocal(out=c.rstd[:, sl], in_=c.rstd[:, sl]),
             reads=[c.Brstd[h]], writes=[c.Brstd[h]])
    for k in range(KD):
        for h in range(NH):
            sl = slice(h * 512, (h + 1) * 512)
            S.op("dve", lambda e, k=k, sl=sl: e.scalar_tensor_tensor(
                out=c.uT[:, k, sl], in0=c.xT[:, k, sl], scalar=c.nw[:, wcol + k:wcol + k + 1],
                in1=c.rstd[:, sl], op0=ALU.mult, op1=ALU.mult),
                reads=[c.BxT[k], c.Brstd[h], c.Bconst], writes=[c.BuT[k][h]])


def enter_phase(S, c, name):
    new = c.phase_bufs[name]
    if c.cur_phase is not None and c.cur_phase != name:
        S.fence(c.phase_bufs[c.cur_phase], new)
    c.cur_phase = name
    if name == "ffn":
        c.bank = c.bank8.next
    else:
        c.bank = c.bank4.next


def emit_ffn(S, c, l, j):
    enter_phase(S, c, "ffn")
    emit_rmsnorm(S, c, (l * 3 + (0 if j == 0 else 2)) * KD)
    for fc in range(KF):
        w, bw = c.gu_ring.next()
        S.dma("pool", w, c.d_wgu[l, j, fc], writes=[bw])
        bk = [[c.bank() for _ in range(NH)] for _ in range(2)]
        for k in range(KD):
            for g in range(2):
                for h in range(NH):
                    ps, bps = bk[g][h]
                    _mm(S, ps, w[:, g, k, :], c.uT[:, k, h * 512:(h + 1) * 512], k == 0, k == KD - 1,
                        [bw, c.BuT[k][h]], [bps])
        for h in range(NH):
            sl = slice(h * 512, (h + 1) * 512)
            sg, bsg = c.sg_ring.next()
            pg, bpg = bk[0][h]
            pu, bpu = bk[1][h]
            _act(S, sg, pg, AF.Silu, [bpg], [bsg])
            _tt(S, "dve", c.actT[:, fc, sl], pu, sg, ALU.mult, [bpu, bsg], [c.Bact[fc][h]])
    for dc in range(KD):
        w, bw = c.dn_ring.next()
        S.dma("pool", w, c.d_wd[l, j, dc], writes=[bw])
        bk = [c.bank() for _ in range(NH)]
        for k in range(KF):
            for h in range(NH):
                ps, bps = bk[h]
                _mm(S, ps, w[:, k, :], c.actT[:, k, h * 512:(h + 1) * 512], k == 0, k == KF - 1,
                    [bw, c.Bact[k][h]], [bps])
        for h in range(NH):
            sl = slice(h * 512, (h + 1) * 512)
            ps, bps = bk[h]
            S.op("dve", lambda e, ps=ps, dc=dc, sl=sl: e.scalar_tensor_tensor(
                out=c.xT[:, dc, sl], in0=ps, scalar=0.5, in1=c.xT[:, dc, sl],
                op0=ALU.mult, op1=ALU.add),
                reads=[bps, c.BxT[dc]], writes=[c.BxT[dc]])


def emit_final(S, c, tile):
    enter_phase(S, c, "ffn")
    wc = DEPTH * 3 * KD
    emit_rmsnorm(S, c, wc)
    for k in range(KD):
        for h in range(NH):
            sl = slice(h * 512, (h + 1) * 512)
            S.op("dve", lambda e, k=k, sl=sl: e.scalar_tensor_tensor(
                out=c.xT[:, k, sl], in0=c.xT[:, k, sl], scalar=c.nw[:, wc + k:wc + k + 1],
                in1=c.rstd[:, sl], op0=ALU.mult, op1=ALU.mult),
                reads=[c.BxT[k], c.Brstd[h], c.Bconst], writes=[c.BxT[k]])
        S.dma("sp", c.d_out[k * 128:(k + 1) * 128, tile * T:(tile + 1) * T], c.xT[:, k, :],
              reads=[c.BxT[k]], sem=c.Bout)


def emit_hgrn(S, c, l, jm):
    enter_phase(S, c, "hgrn")
    emit_rmsnorm(S, c, (l * 3 + 1) * KD)
    h = c.h
    NB = TM // 128
    NCH = TM // HC
    for sub in range(T // TM):
        t0 = sub * TM
        tsl = slice(t0, t0 + TM)
        hh = sub % NH if TM == 512 else None
        rdu = lambda k: [c.BuT[k][(t0 // 512)]]
        for vp in range(NHD // 2):
            w, bw = c.gu_ring.next()
            S.dma("pool", w, c.d_hwv[jm, vp], writes=[bw])
            for b in range(NB):
                pv, bpv = c.bank()
                for i in range(2):
                    for k in range(KD):
                        _mm(S, pv[:, i * 128:(i + 1) * 128], c.uT[:, k, t0 + b * 128:t0 + (b + 1) * 128],
                            w[:, i, k, :], k == 0, k == KD - 1, rdu(k) + [bw], [bpv])
                S.op("act", lambda e, pv=pv, b=b, vp=vp: e.copy(out=h.vall[:, b, vp * 256:(vp + 1) * 256], in_=pv[:, 0:256]),
                     reads=[bpv], writes=[h.Bvall[b]])
        for hd in range(NHD):
            w, bw = c.gu_ring.next()
            S.dma("pool", w, c.d_hwqf[jm, hd], writes=[bw])
            pq, bpq = c.bank()
            pf, bpf = c.bank()
            for k in range(KD):
                _mm(S, pq, w[:, 0, k, :], c.uT[:, k, tsl], k == 0, k == KD - 1, [bw] + rdu(k), [bpq])
            for k in range(KD):
                _mm(S, pf, w[:, 1, k, :], c.uT[:, k, tsl], k == 0, k == KD - 1, [bw] + rdu(k), [bpf])
            qs, bqs = h.tmp.next()
            sg, bsg = h.tmp.next()
            lf, blf = h.tmp.next()
            kk, bkk = h.tmp.next()
            cs, bcs = h.tmp.next()
            eq, beq = h.tmp.next()
            ek, bek = h.tmp.next()
            _act(S, qs, pq, AF.Silu, [bpq], [bqs])
            _act(S, sg, pf, AF.Sigmoid, [bpf], [bsg])
            cb = (jm * 3) * NHD + hd
            _act(S, lf, sg, AF.Ln, [bsg, c.Bconst], [blf],
                 scale=h.hconst[:, cb + NHD:cb + NHD + 1], bias=h.hconst[:, cb:cb + 1])
            S.op("dve", lambda e, kk=kk, sg=sg, cb=cb: e.tensor_scalar(
                out=kk, in0=sg, scalar1=h.hconst[:, cb + 2 * NHD:cb + 2 * NHD + 1],
                scalar2=h.hconst[:, cb + NHD:cb + NHD + 1], op0=ALU.mult, op1=ALU.add),
                reads=[bsg, c.Bconst], writes=[bkk])
            S.op("dve", lambda e, cs=cs, lf=lf: e.tensor_tensor_scan(
                out=cs, data0=h.mask32[:, :], data1=lf, initial=0.0, op0=ALU.mult, op1=ALU.add),
                reads=[blf, c.Bconst], writes=[bcs])
            _act(S, eq, cs, AF.Exp, [bcs], [beq])
            _act(S, ek, cs, AF.Exp, [bcs], [bek], scale=-1.0)
            _tt(S, "dve", h.qT[:, hd, :], qs, eq, ALU.mult, [bqs, beq], [h.BqT[hd]])
            _tt(S, "dve", kk, kk, ek, ALU.mult, [bkk, bek], [bkk])
            S.op("pool", lambda e, kk=kk, hd=hd: e.tensor_copy(out=h.kT[:, hd, :], in_=kk),
                 reads=[bkk], writes=[h.BkT[hd]])
            eq3 = eq.rearrange("p (c j) -> p c j", j=HC)
            kd, bkd = h.kd_ring.next()
            _tt(S, "pool", kd.rearrange("p (c j) -> p c j", j=HC), kk.rearrange("p (c j) -> p c j", j=HC),
                eq3[:, :, HC - 1:HC].broadcast_to([128, NCH, HC]), ALU.mult, [bkk, beq], [bkd])
            S.op("pool", lambda e, eq3=eq3, hd=hd: e.tensor_copy(
                out=h.elast[:, hd, :].rearrange("p (c o) -> p c o", o=1), in_=eq3[:, :, HC - 1:HC]),
                reads=[beq], writes=[h.Bel[hd]])
            for b in range(NB):
                pt, bpt = c.qbank()
                ptb = pt.bitcast(BF16)[:, 0:128]
                S.op("pe", lambda e, ptb=ptb, kd=kd, b=b: e.transpose(ptb, kd[:, b * 128:(b + 1) * 128], c.ident[:, :]),
                     reads=[bkd, c.Bconst], writes=[bpt])
                S.op("act", lambda e, ptb=ptb, b=b, hd=hd: e.copy(out=h.kdtm[:, b, hd, :], in_=ptb),
                     reads=[bpt], writes=[h.Bkdtm[b][hd]])
        for gp in range(NHD // 2):
            w, bw = c.gu_ring.next()
            S.dma("pool", w, c.d_hwg[jm, gp], writes=[bw])
            for i in range(2):
                hd = gp * 2 + i
                pg, bpg = c.bank()
                for k in range(KD):
                    _mm(S, pg, w[:, i, k, :], c.uT[:, k, tsl], k == 0, k == KD - 1, [bw] + rdu(k), [bpg])
                _act(S, h.sgT[:, hd, :], pg, AF.Silu, [bpg], [h.BsgT[hd]])
        for b in range(NB):
            tok = slice(t0 + b * 128, t0 + (b + 1) * 128)
            bsl = slice(b * 128, (b + 1) * 128)
            for cc in range(4):
                for half in range(2):
                    hs = slice(half * 512, (half + 1) * 512)
                    eng = "pool" if (cc + half) % 2 == 0 else "act"
                    if eng == "pool":
                        S.op("pool", lambda e, hs=hs, cc=cc, b=b: e.tensor_scalar(
                            out=h.vm[:, cc, hs], in0=h.vall[:, b, hs], scalar1=h.cmask[:, cc:cc + 1], scalar2=None,
                            op0=ALU.mult), reads=[h.Bvall[b], c.Bconst], writes=[h.Bvm[cc][half]])
                    else:
                        _act(S, h.vm[:, cc, hs], h.vall[:, b, hs], AF.Copy, [h.Bvall[b], c.Bconst], [h.Bvm[cc][half]],
                             scale=h.cmask[:, cc:cc + 1])
            for hb in range(2):
                pa, bpa = c.bank()
                for i in range(4):
                    hd = hb * 4 + i
                    _mm(S, pa[:, i * 128:(i + 1) * 128], h.kT[:, hd, bsl], h.qT[:, hd, bsl], True, True,
                        [h.BkT[hd], h.BqT[hd]], [bpa])
                _tt(S, "dve", h.attn[:, hb * 4:(hb + 1) * 4, :], pa.rearrange("p (i l) -> p i l", i=4),
                    h.maskbd[:, :].unsqueeze(1).broadcast_to([128, 4, 128]), ALU.mult,
                    [bpa, c.Bconst], [h.Battn[hb]])
            pos = []
            for hb in range(2):
                po, bpo = c.bank()
                pos.append((po, bpo))
                for i in range(4):
                    hd = hb * 4 + i
                    _mm(S, po[:, i * 128:(i + 1) * 128], h.vall[:, b, hd * 128:(hd + 1) * 128], h.attn[:, hd, :],
                        True, True, [h.Bvall[b], h.Battn[hb]], [bpo])
            pis = [c.bank() for _ in range(2)]
            for cc in range(4):
                ch = (t0 + b * 128) // HC % (T // HC)
                chunk_in_sub = b * 4 + cc
                for hd in range(NHD):
                    par = h.par[jm][hd]
                    pi, bpi = pis[hd // 4]
                    i = hd % 4
                    _mm(S, pi[:, i * 128 + cc * HC:i * 128 + (cc + 1) * HC], h.sbf[par][:, jm, hd, :],
                        h.qT[:, hd, b * 128 + cc * HC:b * 128 + (cc + 1) * HC], True, True,
                        [h.Bsbf[par][jm][hd], h.BqT[hd]], [bpi])
                    pu, bpu = c.qbank()
                    _mm(S, pu, h.kdtm[:, b, hd, :], h.vm[:, cc, hd * 128:(hd + 1) * 128], True, True,
                        [h.Bkdtm[b][hd], h.Bvm[cc][hd // 4]], [bpu])
                    S.op("dve", lambda e, pu=pu, hd=hd, chunk_in_sub=chunk_in_sub: e.scalar_tensor_tensor(
                        out=h.S[:, jm, hd, :], in0=h.S[:, jm, hd, :],
                        scalar=h.elast[:, hd, chunk_in_sub:chunk_in_sub + 1], in1=pu,
                        op0=ALU.mult, op1=ALU.add),
                        reads=[bpu, h.BS[jm][hd], h.Bel[hd]], writes=[h.BS[jm][hd]])
                    S.op("act", lambda e, hd=hd, par=par: e.copy(out=h.sbf[1 - par][:, jm, hd, :], in_=h.S[:, jm, hd, :]),
                         reads=[h.BS[jm][hd]], writes=[h.Bsbf[1 - par][jm][hd]])
                    h.par[jm][hd] = 1 - par
            for hb in range(2):
                pi, bpi = pis[hb]
                po, bpo = pos[hb]
                oi, boi = h.tmp.next()
                osum, bos = h.tmp.next()
                S.op("act", lambda e, oi=oi, pi=pi: e.copy(out=oi, in_=pi), reads=[bpi], writes=[boi])
                _tt(S, "dve", osum, po, oi, ALU.add, [bpo, boi], [bos])
                osq, bosq = c.sg_ring.next()
                _act(S, osq, osum, AF.Square, [bos], [bosq])
                pss, bpss = c.bank()
                _mm(S, pss, c.ones_bf[:, :], osq, True, True, [bosq, c.Bconst], [bpss])
                rs, brs = h.tmp.next()
                _act(S, rs, pss, AF.Sqrt, [bpss, c.Bconst], [brs], scale=1.0 / 128, bias=c.eps_col[:, 0:1])
                S.op("dve", lambda e, rs=rs: e.reciprocal(out=rs, in_=rs), reads=[brs], writes=[brs])
                _tt(S, "dve", osum, osum, rs, ALU.mult, [bos, brs], [bos])
                S.op("dve", lambda e, osum=osum, hb=hb, bsl=bsl: e.scalar_tensor_tensor(
                    out=h.ogT[:, hb * 4:(hb + 1) * 4, bsl], in0=osum.rearrange("p (i l) -> p i l", i=4),
                    scalar=h.hnw[:, jm:jm + 1], in1=h.sgT[:, hb * 4:(hb + 1) * 4, bsl],
                    op0=ALU.mult, op1=ALU.mult),
                    reads=[bos, c.Bconst] + [h.BsgT[hb * 4 + i] for i in range(4)], writes=[h.BogT[hb][b]])
        for dc in range(KD):
            w, bw = c.dn_ring.next()
            S.dma("pool", w[:, 0:KD, :], c.d_hwo[jm, dc], writes=[bw])
            ps, bps = c.bank()
            for k in range(KD):
                _mm(S, ps, w[:, k, :], h.ogT[:, k, :], k == 0, k == KD - 1,
                    [bw] + [h.BogT[k // 4][b] for b in range(NB)], [bps])
            _tt(S, "dve", c.xT[:, dc, tsl], ps, c.xT[:, dc, tsl], ALU.add, [bps, c.BxT[dc]], [c.BxT[dc]])


def emit_hgrn_consts(S, c):
    h = c.h
    ex = h.lbtmp
    S.dma("sp", ex[:, 0:4 * NHD], c.d_hlb, writes=[c.Bconst])
    _act(S, ex[:, 0:4 * NHD], ex[:, 0:4 * NHD], AF.Exp, [c.Bconst], [c.Bconst])
    den = ex[:, 4 * NHD:5 * NHD]
    e = lambda i: ex[:, i * NHD:(i + 1) * NHD]
    _tt(S, "dve", den, e(0), e(1), ALU.add, [c.Bconst], [c.Bconst])
    _tt(S, "dve", den, den, e(2), ALU.add, [c.Bconst], [c.Bconst])
    _tt(S, "dve", den, den, e(3), ALU.add, [c.Bconst], [c.Bconst])
    rden = ex[:, 5 * NHD:6 * NHD]
    S.op("dve", lambda e_: e_.reciprocal(out=rden, in_=den), reads=[c.Bconst], writes=[c.Bconst])
    s123 = ex[:, 6 * NHD:7 * NHD]
    _tt(S, "dve", s123, e(1), e(2), ALU.add, [c.Bconst], [c.Bconst])
    _tt(S, "dve", s123, s123, e(3), ALU.add, [c.Bconst], [c.Bconst])
    for jm, num in ((0, e(1)), (1, s123)):
        lb = h.hconst[:, (jm * 3) * NHD:(jm * 3 + 1) * NHD]
        oml = h.hconst[:, (jm * 3 + 1) * NHD:(jm * 3 + 2) * NHD]
        noml = h.hconst[:, (jm * 3 + 2) * NHD:(jm * 3 + 3) * NHD]
        _tt(S, "dve", lb, num, rden, ALU.mult, [c.Bconst], [c.Bconst])
        S.op("dve", lambda e_, lb=lb, oml=oml: e_.tensor_scalar(out=oml, in0=lb, scalar1=-1.0, scalar2=1.0,
                                                                 op0=ALU.mult, op1=ALU.add),
             reads=[c.Bconst], writes=[c.Bconst])
        S.op("dve", lambda e_, noml=noml, oml=oml: e_.tensor_scalar(out=noml, in0=oml, scalar1=-1.0, scalar2=None,
                                                                    op0=ALU.mult),
             reads=[c.Bconst], writes=[c.Bconst])
    S.op("pool", lambda e_: e_.memset(h.mask32[:, :], 1.0), writes=[c.Bconst])
    S.op("pool", lambda e_: e_.memset(h.mask32[:, :].rearrange("p (c j) -> p c j", j=HC)[:, :, 0:1], 0.0),
         reads=[c.Bconst], writes=[c.Bconst])
    S.op("pool", lambda e_: e_.memset(h.cmask[:, :], 1.0), writes=[c.Bconst])
    for cc in range(4):
        S.op("pool", lambda e_, cc=cc: e_.affine_select(
            out=h.cmask[:, cc:cc + 1], in_=h.cmask[:, cc:cc + 1], pattern=[[0, 1]], compare_op=ALU.is_ge,
            fill=0.0, base=-32 * cc, channel_multiplier=1), reads=[c.Bconst], writes=[c.Bconst])
        S.op("pool", lambda e_, cc=cc: e_.affine_select(
            out=h.cmask[:, cc:cc + 1], in_=h.cmask[:, cc:cc + 1], pattern=[[0, 1]], compare_op=ALU.is_ge,
            fill=0.0, base=32 * cc + 31, channel_multiplier=-1), reads=[c.Bconst], writes=[c.Bconst])
    S.op("pool", lambda e_: e_.memset(h.maskbd[:, :], 1.0), writes=[c.Bconst])
    S.op("pool", lambda e_: e_.affine_select(out=h.maskbd[:, :], in_=h.maskbd[:, :], pattern=[[1, 128]],
                                             compare_op=ALU.is_ge, fill=0.0, base=0, channel_multiplier=-1),
         reads=[c.Bconst], writes=[c.Bconst])
    for cc in range(4):
        S.op("dve", lambda e_, cc=cc: e_.tensor_scalar(
            out=h.maskbd[:, cc * 32:(cc + 1) * 32], in0=h.maskbd[:, cc * 32:(cc + 1) * 32],
            scalar1=h.cmask[:, cc:cc + 1], scalar2=None, op0=ALU.mult), reads=[c.Bconst], writes=[c.Bconst])
    S.op("pool", lambda e_: e_.memset(c.ident[:, :], 0.0), writes=[c.Bconst])
    S.op("pool", lambda e_: e_.affine_select(out=c.ident[:, :], in_=c.ident[:, :], pattern=[[-1, 128]],
                                             compare_op=ALU.not_equal, fill=1.0, base=0, channel_multiplier=1),
         reads=[c.Bconst], writes=[c.Bconst])
    S.op("pool", lambda e_: e_.memset(h.S[:], 0.0), writes=[b for r in h.BS for b in r])
    for par in range(2):
        S.op("pool", lambda e_, par=par: e_.memset(h.sbf[par][:], 0.0), writes=[b for r in h.Bsbf[par] for b in r])
    S.dma("sp", h.hnw[:, :], c.d_hnw, writes=[c.Bconst])


MT_ = 256
MG = 4
MHG = 8


def _pad_ap(ap2):
    return bass.AP(ap2.tensor, ap2.offset, [[ap2.ap[0][0], 128], [192, 2], [1, 64]])


def emit_mamba(S, c, l, jm):
    enter_phase(S, c, "mamba")
    emit_rmsnorm(S, c, (l * 3 + 1) * KD)
    m = c.m
    NCK = MT_ // 128
    S.op("pool", lambda e: e.memset(m.xpad[:], 0.0), writes=[m.Bxpad])
    S.op("pool", lambda e: e.memset(m.ppad[:], 0.0), writes=m.Bppad)
    for g in range(MG):
        for i in range(MHG // 2):
            ci = g * 4 + i
            S.op("act", lambda e, ci=ci: e.copy(out=_pad_ap(m.ppad[:, 2 * ci:2 * ci + 2, :].rearrange("p a b -> p (a b)")),
                                                  in_=m.prev[:, jm, ci * 128:(ci + 1) * 128].rearrange("p (a b) -> p a b", a=2)),
                 reads=[m.Bprev[jm][g]], writes=[m.Bppad[g]])
    for sub in range(T // MT_):
        t0 = sub * MT_
        tsl = slice(t0, t0 + MT_)
        rdu = lambda k: [c.BuT[k][t0 // 512]]
        for pr in range(20):
            w, bw = c.gu_ring.next()
            S.dma("pool", w, c.d_mwin[jm, pr], writes=[bw])
            for i in range(2):
                ch = pr * 2 + i
                ps, bps = c.bank()
                for k in range(KD):
                    _mm(S, ps[:, 0:MT_], w[:, i, k, :], c.uT[:, k, tsl], k == 0, k == KD - 1, [bw] + rdu(k), [bps])
                if ch < 16:
                    _act(S, m.siluz[:, ch, :], ps[:, 0:MT_], AF.Silu, [bps], [m.Bsz[ch]])
                    continue
                ci = ch - 16
                raw, braw = m.raw_ring.next()
                acc, bacc = m.acc_ring.next()
                S.op("pool", lambda e, raw=raw, ci=ci: e.tensor_copy(out=raw[:, 0:3], in_=m.tails[:, jm, ci, :]),
                     reads=[m.Btail[jm][ci]], writes=[braw])
                S.op("act", lambda e, raw=raw, ps=ps: e.copy(out=raw[:, 3:3 + MT_], in_=ps[:, 0:MT_]),
                     reads=[bps], writes=[braw])
                S.op("pool", lambda e, raw=raw, ci=ci: e.tensor_copy(out=m.tails[:, jm, ci, :], in_=raw[:, MT_:MT_ + 3]),
                     reads=[braw], writes=[m.Btail[jm][ci]])
                cw = lambda tap, ci=ci: m.mconv[:, jm, ci, tap:tap + 1]
                S.op("dve", lambda e, raw=raw, acc=acc, cw=cw: e.tensor_scalar(
                    out=acc, in0=raw[:, 0:MT_], scalar1=cw(0), scalar2=None, op0=ALU.mult),
                    reads=[braw, c.Bconst], writes=[bacc])
                for tap in range(1, 4):
                    S.op("dve", lambda e, raw=raw, acc=acc, cw=cw, tap=tap: e.scalar_tensor_tensor(
                        out=acc, in0=raw[:, tap:tap + MT_], scalar=cw(tap), in1=acc, op0=ALU.mult, op1=ALU.add),
                        reads=[braw, bacc, c.Bconst], writes=[bacc])
                if ci < 16:
                    dst, bdst = m.xc[:, ci, :], m.Bxc[ci]
                elif ci < 20:
                    dst, bdst = m.BT[:, ci - 16, :], m.BBT[ci - 16]
                else:
                    dst, bdst = m.CT[:, ci - 20, :], m.BCT[ci - 20]
                _act(S, dst, acc, AF.Silu, [bacc, c.Bconst], [bdst], bias=cw(4))
        for ck in range(NCK):
            csl = slice(ck * 128, (ck + 1) * 128)
            tok = slice(t0 + ck * 128, t0 + (ck + 1) * 128)
            pd, bpd = c.qbank()
            for k in range(KD):
                _mm(S, pd[:, 0:32], c.uT[:, k, tok], m.wdt[:, jm, k, :], k == 0, k == KD - 1, rdu(k) + [m.Bwdt], [bpd])
            _tt(S, "dve", m.dt[:, :], pd[:, 0:32], m.dtb[:, jm, :], ALU.add, [bpd, c.Bconst], [m.Bdt])
            _act(S, m.dt[:, :], m.dt[:, :], AF.Exp, [m.Bdt], [m.Bdt])
            _act(S, m.dt[:, :], m.dt[:, :], AF.Ln, [m.Bdt], [m.Bdt], bias=1.0)
            _tt(S, "dve", m.dA[:, :], m.dt[:, :], m.Aneg[:, jm, :], ALU.mult, [m.Bdt, c.Bconst], [m.BdA])
            pr_, bpr = c.qbank()
            _mm(S, pr_[:, 0:32], m.Umat[:, :], m.dA[:, :], True, True, [m.BdA, c.Bconst], [bpr])
            _act(S, m.dtdec[:, :], pr_[:, 0:32], AF.Exp, [bpr], [m.Bdtdec])
            _tt(S, "dve", m.dtdec[:, :], m.dtdec[:, :], m.dt[:, :], ALU.mult, [m.Bdtdec, m.Bdt], [m.Bdtdec])
            for ci in range(16):
                pt, bpt = c.qbank()
                ptb = pt.bitcast(BF16)[:, 0:128]
                S.op("pe", lambda e, ptb=ptb, ci=ci, csl=csl: e.transpose(ptb, m.xc[:, ci, csl], c.ident[:, :]),
                     reads=[m.Bxc[ci], c.Bconst], writes=[bpt])
                pt3 = ptb.rearrange("p (a b) -> p a b", a=2)
                _tt(S, "dve", _pad_ap(m.xpad[:, 2 * ci:2 * ci + 2, :].rearrange("p a b -> p (a b)")), pt3,
                    m.dt[:, 2 * ci:2 * ci + 2].unsqueeze(2).broadcast_to([128, 2, 64]), ALU.mult,
                    [bpt, m.Bdt], [m.Bxpad])
                _tt(S, "dve", m.xdd[:, ci * 128:(ci + 1) * 128].rearrange("p (a b) -> p a b", a=2), pt3,
                    m.dtdec[:, 2 * ci:2 * ci + 2].unsqueeze(2).broadcast_to([128, 2, 64]), ALU.mult,
                    [bpt, m.Bdtdec], [m.Bxdd[ci // 4]])
            for g in range(MG):
                pt, bpt = c.qbank()
                ptb = pt.bitcast(BF16)[:, 0:128]
                S.op("pe", lambda e, ptb=ptb, g=g, csl=csl: e.transpose(ptb, m.BT[:, g, csl], c.ident[:, :]),
                     reads=[m.BBT[g], c.Bconst], writes=[bpt])
                S.op("act", lambda e, ptb=ptb, g=g: e.copy(out=m.btm[:, g, :], in_=ptb), reads=[bpt], writes=[m.Bbtm[g]])
            for g in range(MG):
                hs = slice(g * MHG, (g + 1) * MHG)
                R, bR = m.R_ring.next()
                _tt(S, "dve", R, m.Tmat[:, :].unsqueeze(1).broadcast_to([128, MHG, 128]),
                    m.dA[:, hs].unsqueeze(2).broadcast_to([128, MHG, 128]), ALU.mult, [m.BdA, c.Bconst], [bR])
                R2 = R.rearrange("p h l -> p (h l)")
                LT, bLT = m.LT_ring.next()
                ecs, becs = m.ecs_ring.next()
                for half in range(2):
                    hsl = slice(half * 512, (half + 1) * 512)
                    p1, bp1 = c.bank()
                    _mm(S, p1, m.Umat[:, :], R2[:, hsl], True, True, [bR, c.Bconst], [bp1])
                    _act(S, LT.rearrange("p h l -> p (h l)")[:, hsl], p1, AF.Exp, [bp1], [bLT])
                    p2, bp2 = c.bank()
                    _mm(S, p2, m.onesf[:, :], R2[:, hsl], True, True, [bR, c.Bconst], [bp2])
                    _act(S, ecs.rearrange("p h l -> p (h l)")[:, hsl], p2, AF.Exp, [bp2], [becs])
                pg, bpg = c.qbank()
                _mm(S, pg, m.BT[:, g, csl], m.CT[:, g, csl], True, True, [m.BBT[g], m.BCT[g]], [bpg])
                Gm, bGm = m.Gm_ring.next()
                _tt(S, "dve", Gm, pg, m.maskc[:, :], ALU.mult, [bpg, c.Bconst], [bGm])
                MTt, bMT = m.MT_ring.next()
                _tt(S, "pool", MTt, LT, Gm.unsqueeze(1).broadcast_to([128, MHG, 128]), ALU.mult, [bLT, bGm], [bMT])
                Cp, bCp = m.Cp_ring.next()
                _tt(S, "pool", Cp, ecs, m.CT[:, g, csl].unsqueeze(1).broadcast_to([128, MHG, 128]), ALU.mult,
                    [becs, m.BCT[g]], [bCp])
                py, bpy = c.bank()
                for i in range(4):
                    ci = g * 4 + i
                    o = py[:, i * 128:(i + 1) * 128]
                    for a in range(2):
                        _mm(S, o, m.xpad[:, 2 * ci + a, :], MTt[:, 2 * i + a, :], a == 0, False, [m.Bxpad, bMT], [bpy])
                    for a in range(2):
                        _mm(S, o, m.ppad[:, 2 * ci + a, :], Cp[:, 2 * i + a, :], False, a == 1, [m.Bppad[g], bCp], [bpy])
                pu, bpu = c.bank()
                _mm(S, pu, m.btm[:, g, :], m.xdd[:, g * 512:(g + 1) * 512], True, True, [m.Bbtm[g], m.Bxdd[g]], [bpu])
                pv = m.prev[:, jm, g * 512:(g + 1) * 512]
                _tt(S, "dve", pv.rearrange("p (h q) -> p h q", h=MHG), pv.rearrange("p (h q) -> p h q", h=MHG),
                    ecs[:, :, 127:128].broadcast_to([128, MHG, 64]), ALU.mult, [m.Bprev[jm][g], becs], [m.Bprev[jm][g]])
                _tt(S, "dve", pv, pv, pu, ALU.add, [m.Bprev[jm][g], bpu], [m.Bprev[jm][g]])
                for i in range(4):
                    ci = g * 4 + i
                    S.op("act", lambda e, ci=ci: e.copy(
                        out=_pad_ap(m.ppad[:, 2 * ci:2 * ci + 2, :].rearrange("p a b -> p (a b)")),
                        in_=m.prev[:, jm, ci * 128:(ci + 1) * 128].rearrange("p (a b) -> p a b", a=2)),
                        reads=[m.Bprev[jm][g]], writes=[m.Bppad[g]])
                yg, byg = m.yg_ring.next()
                for i in range(4):
                    ci = g * 4 + i
                    S.op("dve", lambda e, i=i, ci=ci, yg=yg, py=py, csl=csl: e.scalar_tensor_tensor(
                        out=yg[:, i, :], in0=m.xc[:, ci, csl], scalar=m.dcol[:, jm, ci:ci + 1],
                        in1=py[:, i * 128:(i + 1) * 128], op0=ALU.mult, op1=ALU.add),
                        reads=[m.Bxc[ci], bpy, c.Bconst], writes=[byg])
                _tt(S, "dve", yg, yg, m.siluz[:, g * 4:(g + 1) * 4, csl], ALU.mult,
                    [byg] + [m.Bsz[g * 4 + i] for i in range(4)], [byg])
                ysq, bysq = c.sg_ring.next()
                _act(S, ysq, yg.rearrange("p i l -> p (i l)"), AF.Square, [byg], [bysq])
                pn, bpn = c.qbank()
                for i in range(4):
                    _mm(S, pn, c.ones_bf[:, :], ysq[:, i * 128:(i + 1) * 128], i == 0, i == 3, [bysq, c.Bconst], [bpn])
                rs, brs = m.rs_ring.next()
                _act(S, rs, pn, AF.Sqrt, [bpn, c.Bconst], [brs], scale=1.0 / 512, bias=c.eps_col[:, 0:1])
                S.op("dve", lambda e, rs=rs: e.reciprocal(out=rs, in_=rs), reads=[brs], writes=[brs])
                for i in range(4):
                    ci = g * 4 + i
                    S.op("dve", lambda e, i=i, ci=ci, yg=yg, rs=rs, csl=csl: e.scalar_tensor_tensor(
                        out=m.ynT[:, ci, csl], in0=yg[:, i, :], scalar=m.mnw[:, jm, ci:ci + 1], in1=rs,
                        op0=ALU.mult, op1=ALU.mult),
                        reads=[byg, brs, c.Bconst], writes=[m.Byn[ci]])
        for dc in range(KD):
            w, bw = c.dn_ring.next()
            S.dma("pool", w[:, 0:16, :], c.d_mwo[jm, dc], writes=[bw])
            ps, bps = c.bank()
            for k in range(16):
                _mm(S, ps[:, 0:MT_], w[:, k, :], m.ynT[:, k, :], k == 0, k == 15, [bw, m.Byn[k]], [bps])
            _tt(S, "dve", c.xT[:, dc, tsl], ps[:, 0:MT_], c.xT[:, dc, tsl], ALU.add, [bps, c.BxT[dc]], [c.BxT[dc]])


def emit_mamba_consts(S, c):
    m = c.m
    B = [c.Bconst]
    S.dma("sp", m.mconv[:], c.d_mconv, writes=B)
    S.dma("sp", m.dcol[:], c.d_mdcol, writes=B)
    S.dma("sp", m.mnw[:], c.d_mnw, writes=B)
    S.dma("pool", m.wdt[:], c.d_mwdt, writes=[m.Bwdt])
    for j in range(2):
        S.dma("sp", m.dtb[:, j, :], c.d_mdtb[j:j + 1, :].partition_broadcast(128), writes=B)
        S.dma("sp", m.Aneg[:, j, :], c.d_malog[j:j + 1, :].partition_broadcast(128), writes=B)
    _act(S, m.Aneg[:], m.Aneg[:], AF.Exp, B, B)
    S.op("dve", lambda e: e.tensor_scalar(out=m.Aneg[:], in0=m.Aneg[:], scalar1=-1.0, scalar2=None, op0=ALU.mult),
         reads=B, writes=B)
    S.op("pool", lambda e: e.memset(m.onesf[:, :], 1.0), writes=B)
    S.op("pool", lambda e: e.memset(m.Tmat[:, :], 1.0), writes=B)
    S.op("pool", lambda e: e.affine_select(out=m.Tmat[:, :], in_=m.Tmat[:, :], pattern=[[1, 128]],
                                           compare_op=ALU.is_ge, fill=0.0, base=0, channel_multiplier=-1),
         reads=B, writes=B)
    S.op("pool", lambda e: e.memset(m.Umat[:, :], 1.0), writes=B)
    S.op("pool", lambda e: e.affine_select(out=m.Umat[:, :], in_=m.Umat[:, :], pattern=[[-1, 128]],
                                           compare_op=ALU.is_gt, fill=0.0, base=0, channel_multiplier=1),
         reads=B, writes=B)
    S.op("pool", lambda e: e.memset(m.prev[:], 0.0), writes=[b for r in m.Bprev for b in r])
    S.op("pool", lambda e: e.memset(m.tails[:], 0.0), writes=[b for r in m.Btail for b in r])


def build_program(seq=SEQ, depth=DEPTH, plan=None):
    nc = bass.Bass("TRN2", target_bir_lowering=False)
    c = Ctx()
    c.nc = nc
    ntiles = seq // T
    dt = lambda name, shape: nc.dram_tensor(name, shape, F32, kind="ExternalInput").ap()
    c.d_x = dt("xT", [D, seq])
    c.d_wgu = dt("wgu", [DEPTH, 2, KF, 128, 2, KD, 128])
    c.d_wd = dt("wd", [DEPTH, 2, KD, 128, KF, 128])
    NWC = (DEPTH * 3 + 1) * KD
    c.d_nw = dt("nw", [128, NWC])
    c.d_hwqf = dt("hwqf", [2, NHD, 128, 2, KD, 128])
    c.d_hwg = dt("hwg", [2, NHD // 2, 128, 2, KD, 128])
    c.d_hwv = dt("hwv", [2, NHD // 2, 128, 2, KD, 128])
    c.d_hwo = dt("hwo", [2, KD, 128, KD, 128])
    c.d_hlb = dt("hlb", [128, 4 * NHD])
    c.d_hnw = dt("hnw", [128, 2])
    c.d_mwin = dt("mwin", [2, 20, 128, 2, KD, 128])
    c.d_mwdt = dt("mwdt", [128, 2, KD, 32])
    c.d_mconv = dt("mconv", [128, 2, 24, 5])
    c.d_mdcol = dt("mdcol", [128, 2, 16])
    c.d_mnw = dt("mnw", [128, 2, 16])
    c.d_mdtb = dt("mdtb", [2, 32])
    c.d_malog = dt("malog", [2, 32])
    c.d_mwo = dt("mwo", [2, KD, 128, 16, 128])
    c.d_out = nc.dram_tensor("outT", [D, seq], F32, kind="ExternalOutput").ap()
    from contextlib import ExitStack
    with ExitStack() as st:
        sb = lambda n, s, d: st.enter_context(nc.sbuf_tensor(n, s, d))
        c.xT = sb("xT_sb", [128, KD, T], F32)
        c.uT = sb("uT_sb", [128, KD, T], BF16)
        c.rstd = sb("rstd_sb", [128, T], F32)
        c.nw = sb("nw_sb", [128, NWC], F32)
        c.ones_bf = sb("ones_bf", [128, 128], BF16)
        c.ident = sb("ident_bf", [128, 128], BF16)
        c.eps_col = sb("eps_col", [128, 1], F32)
        gu = sb("gu_ring", [128, 4, 2, KD, 128], BF16)
        dn = sb("dn_ring", [128, 2, KF, 128], BF16)
        sq = sb("sq_ring", [128, 2, T], BF16)
        sg = sb("sg_ring", [128, 3, 512], BF16)
        c.gu_ring = Ring("gu", [gu[:, i] for i in range(4)])
        c.dn_ring = Ring("dn", [dn[:, i] for i in range(2)])
        c.sq_ring = Ring("sq", [sq[:, i] for i in range(2)])
        c.sg_ring = Ring("sg", [sg[:, i] for i in range(3)])
        h = c.h = Ctx()
        h.S = sb("h_S", [128, 2, NHD, 128], F32)
        h.sbf = [sb("h_sbf%d" % i, [128, 2, NHD, 128], BF16) for i in range(2)]
        h.hconst = sb("h_const", [128, 2 * 3 * NHD], F32)
        h.lbtmp = sb("h_lbtmp", [128, 7 * NHD], F32)
        h.hnw = sb("h_nw", [128, 2], F32)
        h.mask32 = sb("h_mask32", [128, TM], F32)
        h.cmask = sb("h_cmask", [128, 4], F32)
        h.maskbd = sb("h_maskbd", [128, 128], F32)
        h.BS = [[Buf("hS%d_%d" % (j, i)) for i in range(NHD)] for j in range(2)]
        h.Bsbf = [[[Buf("hsbf%d_%d_%d" % (p, j, i)) for i in range(NHD)] for j in range(2)] for p in range(2)]
        h.par = [[0] * NHD for _ in range(2)]
        ARENA = 80 * 1024
        arena = sb("arena", [128, ARENA], mybir.dt.uint8)
        off = [0]

        def carve(shape, dtype, reset=False):
            if reset:
                off[0] = 0
            n = int(np.prod(shape)) * (2 if dtype == BF16 else 4)
            ap = arena[:, off[0]:off[0] + n].bitcast(dtype)
            off[0] += n
            assert off[0] <= ARENA, (off[0], ARENA)
            if len(shape) > 1:
                names = " ".join("a%d" % i for i in range(len(shape)))
                kw = {"a%d" % i: shape[i] for i in range(len(shape))}
                ap = ap.rearrange("p (%s) -> p %s" % (names, names), **kw)
            return ap

        c.actT = carve([KF, T], BF16, reset=True)
        NB = TM // 128
        h.qT = carve([NHD, TM], BF16, reset=True)
        h.kT = carve([NHD, TM], BF16)
        h.sgT = carve([NHD, TM], BF16)
        h.ogT = carve([NHD, TM], BF16)
        h.vall = carve([NB, 1024], BF16)
        h.vm = carve([4, 1024], BF16)
        h.kdtm = carve([NB, NHD, 128], BF16)
        h.attn = carve([NHD, 128], BF16)
        h.elast = carve([NHD, TM // HC], F32)
        tmp = carve([9, TM], F32)
        kd = carve([2, TM], BF16)
        h.tmp = Ring("htmp", [tmp[:, i, :] for i in range(9)])
        h.kd_ring = Ring("hkd", [kd[:, i, :] for i in range(2)])
        h.BqT = [Buf("hq%d" % i) for i in range(NHD)]
        h.BkT = [Buf("hk%d" % i) for i in range(NHD)]
        h.BsgT = [Buf("hsg%d" % i) for i in range(NHD)]
        h.BogT = [[Buf("hog%d_%d" % (i, b)) for b in range(NB)] for i in range(2)]
        h.Bvall = [Buf("hvall%d" % i) for i in range(NB)]
        h.Bvm = [[Buf("hvm%d_%d" % (i, j)) for j in range(2)] for i in range(4)]
        h.Bkdtm = [[Buf("hkdtm%d_%d" % (b, i)) for i in range(NHD)] for b in range(NB)]
        h.Battn = [Buf("hattn%d" % i) for i in range(2)]
        h.Bel = [Buf("hel%d" % i) for i in range(NHD)]


        m = c.m = Ctx()
        m.prev = sb("m_prev", [128, 2, 2048], F32)
        m.tails = sb("m_tails", [128, 2, 24, 3], F32)
        m.mconv = sb("m_conv", [128, 2, 24, 5], F32)
        m.dcol = sb("m_dcol", [128, 2, 16], F32)
        m.mnw = sb("m_nw", [128, 2, 16], F32)
        m.wdt = sb("m_wdt", [128, 2, KD, 32], BF16)
        m.dtb = sb("m_dtb", [128, 2, 32], F32)
        m.Aneg = sb("m_Aneg", [128, 2, 32], F32)
        m.onesf = sb("m_onesf", [128, 128], F32)
        m.Tmat = sb("m_Tmat", [128, 128], F32)
        m.Umat = sb("m_Umat", [128, 128], F32)
        m.maskc = m.Tmat
        m.siluz = carve([16, MT_], BF16, reset=True)
        m.xc = carve([16, MT_], BF16)
        m.BT = carve([MG, MT_], BF16)
        m.CT = carve([MG, MT_], BF16)
        m.ynT = carve([16, MT_], BF16)
        m.xpad = carve([32, 128], BF16)
        m.ppad = carve([32, 128], BF16)
        m.xdd = carve([2048], BF16)
        m.btm = carve([MG, 128], BF16)
        m.dt = carve([32], F32)
        m.dA = carve([32], F32)
        m.dtdec = carve([32], F32)
        mk = lambda name, n, shape, dtype: Ring(name, [carve(shape, dtype) for _ in range(n)])
        m.R_ring = mk("mR", 1, [MHG, 128], F32)
        m.LT_ring = mk("mLT", 2, [MHG, 128], BF16)
        m.ecs_ring = mk("mecs", 1, [MHG, 128], F32)
        m.Gm_ring = mk("mGm", 2, [128], BF16)
        m.MT_ring = mk("mMT", 2, [MHG, 128], BF16)
        m.Cp_ring = mk("mCp", 2, [MHG, 128], BF16)
        m.yg_ring = mk("myg", 2, [4, 128], F32)
        m.rs_ring = mk("mrs", 2, [128], F32)
        m.raw_ring = mk("mraw", 2, [MT_ + 4], F32)
        m.acc_ring = mk("macc", 2, [MT_], F32)
        m.Bsz = [Buf("msz%d" % i) for i in range(16)]
        m.Bxc = [Buf("mxc%d" % i) for i in range(16)]
        m.BBT = [Buf("mBT%d" % i) for i in range(MG)]
        m.BCT = [Buf("mCT%d" % i) for i in range(MG)]
        m.Byn = [Buf("myn%d" % i) for i in range(16)]
        m.Bxpad = Buf("mxpad")
        m.Bwdt = Buf("mwdt")
        m.Bppad = [Buf("mppad%d" % i) for i in range(MG)]
        m.Bprev = [[Buf("mprev%d_%d" % (j, g)) for g in range(MG)] for j in range(2)]
        m.Btail = [[Buf("mtail%d_%d" % (j, i)) for i in range(24)] for j in range(2)]
        m.Bdt, m.BdA, m.Bdtdec = Buf("mdt"), Buf("mdA"), Buf("mdtdec")
        m.Bxdd = [Buf("mxdd%d" % i) for i in range(MG)]
        m.Bbtm = [Buf("mbtm%d" % i) for i in range(MG)]

        psum = st.enter_context(nc.psum_tensor("psum", [128, 8, 512], F32))
        c.bank8 = Ring("bank", [psum[:, i, :] for i in range(8)])
        c.bank4 = Ring("bankm", [psum[:, i, :] for i in range(4)])
        c.qbank_ring = Ring("qbank", [psum[:, 4 + i, 0:128] for i in range(4)])
        c.qbank = c.qbank_ring.next
        c.bank = c.bank8.next
        c.BxT = [Buf("xT%d" % k) for k in range(KD)]
        c.BuT = [[Buf("uT%d_%d" % (k, hh)) for hh in range(NH)] for k in range(KD)]
        c.Bact = [[Buf("act%d_%d" % (k, hh)) for hh in range(NH)] for k in range(KF)]
        c.Brstd = [Buf("rstd%d" % hh) for hh in range(NH)]
        c.Bconst = Buf("const")
        c.Bout = Buf("out")
        flat = lambda x: [b for r in x for b in (flat(r) if isinstance(r, list) else [r])]
        c.phase_bufs = {
            "ffn": flat(c.Bact) + c.bank8.bufs,
            "hgrn": flat([h.BqT, h.BkT, h.BsgT, h.BogT, h.Bvall, h.Bvm, h.Bkdtm, h.Battn, h.Bel])
                    + h.tmp.bufs + h.kd_ring.bufs + c.bank4.bufs + c.qbank_ring.bufs,
            "mamba": flat([m.Bsz, m.Bxc, m.BBT, m.BCT, m.Byn, [m.Bxpad], m.Bppad, [m.Bdt, m.BdA, m.Bdtdec], m.Bxdd, m.Bbtm])
                     + sum([r.bufs for r in (m.R_ring, m.LT_ring, m.ecs_ring, m.Gm_ring, m.MT_ring, m.Cp_ring,
                                             m.yg_ring, m.rs_ring, m.raw_ring, m.acc_ring)], [])
                     + c.bank4.bufs + c.qbank_ring.bufs,
        }
        c.cur_phase = None
        S = Sched(nc)
        S.op("pool", lambda e: e.memset(c.ones_bf[:], 1.0), writes=[c.Bconst])
        S.op("pool", lambda e: e.memset(c.eps_col[:], EPS), writes=[c.Bconst])
        S.dma("sp", c.nw[:], c.d_nw, writes=[c.Bconst])
        emit_hgrn_consts(S, c)
        emit_mamba_consts(S, c)
        if plan is None:
            plan = ["ffn0", "mix", "ffn1"]
        for tile in range(ntiles):
            for k in range(KD):
                S.dma("sp", c.xT[:, k, :], c.d_x[k * 128:(k + 1) * 128, tile * T:(tile + 1) * T],
                      writes=[c.BxT[k]])
            for l in range(depth):
                for stg in plan:
                    if stg == "ffn0":
                        emit_ffn(S, c, l, 0)
                    elif stg == "ffn1":
                        emit_ffn(S, c, l, 1)
                    elif stg == "mix":
                        if l % 2 == 1:
                            emit_hgrn(S, c, l, l // 2)
                        else:
                            emit_mamba(S, c, l, l // 2)
                    elif stg == "mamba":
                        emit_mamba(S, c, l, l // 2)
                    elif stg == "hgrn":
                        emit_hgrn(S, c, l, l // 2)
            emit_final(S, c, tile)
        S.emit()
    return nc


def prep_common(inputs):
    f32 = np.float32
    A = lambda k: np.asarray(inputs[k], f32)
    g = A("ffn_w_gate").reshape(DEPTH, 2, KD, 128, KF, 128)
    u = A("ffn_w_up").reshape(DEPTH, 2, KD, 128, KF, 128)
    wgu = np.stack([g, u], axis=0)
    wgu = np.ascontiguousarray(wgu.transpose(1, 2, 5, 4, 0, 3, 6))
    wd = A("ffn_w_down").reshape(DEPTH, 2, KF, 128, KD, 128)
    wd = np.ascontiguousarray(wd.transpose(0, 1, 4, 3, 2, 5))
    nw = np.concatenate([A("norm_w").reshape(DEPTH * 3, KD, 128),
                         A("final_norm_w").reshape(1, KD, 128)], axis=0)
    nw = np.ascontiguousarray(nw.transpose(2, 0, 1).reshape(128, -1))
    out = {"wgu": wgu, "wd": wd, "nw": nw}
    hw = A("h_w_in").reshape(2, KD, 128, 4, NHD, 128)
    qf = hw[:, :, :, 0:2]
    out["hwqf"] = np.ascontiguousarray(qf.transpose(0, 4, 2, 3, 1, 5))
    gg = hw[:, :, :, 3].reshape(2, KD, 128, NHD // 2, 2, 128)
    out["hwg"] = np.ascontiguousarray(gg.transpose(0, 3, 2, 4, 1, 5))
    vv = hw[:, :, :, 2].reshape(2, KD, 128, NHD // 2, 2, 128)
    out["hwv"] = np.ascontiguousarray(vv.transpose(0, 3, 2, 4, 1, 5))
    wo = A("h_w_out").reshape(2, KD, 128, KD, 128)
    out["hwo"] = np.ascontiguousarray(wo.transpose(0, 3, 2, 1, 4))
    lb = A("h_lb_logits").reshape(DEPTH, NHD, 128)
    out["hlb"] = np.ascontiguousarray(lb.transpose(2, 0, 1).reshape(128, -1))
    out["hnw"] = np.ascontiguousarray(A("h_norm_w").T)
    mw = A("m_w_in")
    win = mw[:, :, 0:5120].reshape(2, KD, 128, 20, 2, 128)
    out["mwin"] = np.ascontiguousarray(win.transpose(0, 3, 2, 4, 1, 5))
    wdt = mw[:, :, 5120:5152].reshape(2, KD, 128, 32)
    out["mwdt"] = np.ascontiguousarray(wdt.transpose(2, 0, 1, 3))
    cw = A("m_conv_w").reshape(2, 4, 24, 128)
    cb = A("m_conv_b").reshape(2, 1, 24, 128)
    out["mconv"] = np.ascontiguousarray(np.concatenate([cw, cb], axis=1).transpose(3, 0, 2, 1))
    dcol = np.repeat(A("m_d"), 64, axis=1).reshape(2, 16, 128)
    out["mdcol"] = np.ascontiguousarray(dcol.transpose(2, 0, 1))
    out["mnw"] = np.ascontiguousarray(A("m_norm_w").reshape(2, 16, 128).transpose(2, 0, 1))
    out["mdtb"] = np.ascontiguousarray(A("m_dt_bias"))
    out["malog"] = np.ascontiguousarray(A("m_a_log"))
    mwo = A("m_w_out").reshape(2, 16, 128, KD, 128)
    out["mwo"] = np.ascontiguousarray(mwo.transpose(0, 3, 2, 1, 4))
    return out


_NC_CACHE = {}


def kernel(**inputs):
    x = np.asarray(inputs["x"], np.float32)
    B = x.shape[0]
    common = prep_common(inputs)
    if "nc" not in _NC_CACHE:
        _NC_CACHE["nc"] = build_program()
    nc = _NC_CACHE["nc"]
    in_maps = []
    for b in range(B):
        m = dict(common)
        m["xT"] = np.ascontiguousarray(x[b].T)
        in_maps.append(m)
    res = run_bass_kernel_spmd(nc, in_maps, core_ids=list(range(B)))
    out = np.stack([np.ascontiguousarray(r["outT"].T) for r in res.results], axis=0)
    return out.astype(np.float32)
```

```python
import heapq
import numpy as np
import concourse.bass as bass
import concourse.mybir as mybir
from concourse.bass_utils import run_bass_kernel_spmd

F32 = mybir.dt.float32
BF16 = mybir.dt.bfloat16
AF = mybir.ActivationFunctionType
ALU = mybir.AluOpType

ENGS = ("pe", "act", "dve", "pool", "sp")
HOP_NS = 250.0
WINDOW = 48


class Buf:
    __slots__ = ("name", "lw", "rd", "dsem", "dcnt", "dlast")

    def __init__(self, name):
        self.name = name
        self.lw = None
        self.rd = []
        self.dsem = None
        self.dcnt = 0
        self.dlast = None


class Op:
    __slots__ = ("eng", "fn", "deps", "marked", "dma", "idx", "cost", "seq", "fin", "pos", "dcount")


class Sched:
    def __init__(self, nc):
        self.nc = nc
        self.ops = {e: [] for e in ENGS}
        self.dma_bufs = []
        self.nseq = 0

    def _record(self, eng, fn, reads, writes, dma=None, cost=500.0):
        op = Op()
        op.eng = eng
        op.fn = fn
        op.marked = False
        op.dma = dma
        op.cost = cost
        op.seq = self.nseq
        self.nseq += 1
        deps = []
        for b in reads:
            if b.lw is not None:
                deps.append(b.lw)
        for b in writes:
            if b.lw is not None:
                deps.append(b.lw)
            deps.extend(b.rd)
        if dma is not None and dma.dlast is not None:
            deps.append(dma.dlast)
        seen = set()
        out = []
        for d in deps:
            if id(d[1]) in seen:
                continue
            seen.add(id(d[1]))
            out.append(d)
        op.deps = out
        self.ops[eng].append(op)
        if dma is not None:
            if dma.dsem is None:
                self.dma_bufs.append(dma)
                dma.dsem = True
            dma.dcnt += 16
            op.dcount = dma.dcnt
            tok = ("d", op)
            dma.dlast = tok
        else:
            op.dcount = 0
            tok = ("e", op)
        for b in reads:
            b.rd.append(tok)
        for b in writes:
            b.lw = tok
            b.rd = []
        return op

    DEFAULT_COST = {"pe": 150.0, "act": 700.0, "dve": 700.0, "pool": 1500.0, "sp": 100.0}

    def op(self, eng, fn, reads=(), writes=(), cost=None):
        if cost is None:
            cost = self.DEFAULT_COST[eng]
        return self._record(eng, fn, reads, writes, cost=cost)

    @staticmethod
    def fence(old_bufs, new_bufs):
        toks = {}
        for b in old_bufs:
            if b.lw is not None:
                toks[id(b.lw[1])] = b.lw
            for t in b.rd:
                toks[id(t[1])] = t
        toks = list(toks.values())
        for b in new_bufs:
            b.rd.extend(toks)

    def dma(self, eng, out, in_, reads=(), writes=(), sem=None, nbytes=65536):
        if sem is None:
            sem = (list(writes) + list(reads))[0]
        return self._record(eng, lambda e: e.dma_start(out=out, in_=in_), reads, writes, dma=sem,
                            cost=2000.0 + nbytes / 100.0)

    def _list_schedule(self):
        pend = {e: list(self.ops[e]) for e in ENGS}
        head = {e: 0 for e in ENGS}
        free = {e: 0.0 for e in ENGS}
        order = {e: [] for e in ENGS}
        for e in ENGS:
            for op in pend[e]:
                op.fin = None
        total = sum(len(v) for v in pend.values())
        done = 0
        issue = {"sp": 60.0, "pool": 700.0}
        while done < total:
            best = None
            for e in ENGS:
                lst = pend[e]
                h = head[e]
                n = len(lst)
                while h < n and lst[h] is None:
                    h += 1
                head[e] = h
                cnt = 0
                i = h
                fe = free[e]
                while i < n and cnt < WINDOW:
                    op = lst[i]
                    if op is not None:
                        cnt += 1
                        rdy = 0.0
                        ok = True
                        for d in op.deps:
                            f = d[1].fin
                            if f is None:
                                ok = False
                                break
                            if d[1].eng != e or d[0] == "d":
                                f += HOP_NS
                            if f > rdy:
                                rdy = f
                        if ok:
                            st = rdy if rdy > fe else fe
                            key = (st, op.seq)
                            if best is None or key < best[0]:
                                best = (key, e, i, op)
                            if rdy <= fe:
                                break
                    i += 1
            key, e, i, op = best
            st = key[0]
            if op.dma is not None:
                free[e] = st + issue.get(e, 100.0)
                op.fin = st + op.cost
            else:
                free[e] = st + op.cost
                op.fin = free[e]
            pend[e][i] = None
            order[e].append(op)
            done += 1
        self.makespan = max(free.values())
        return order

    def emit(self, reorder=True):
        nc = self.nc
        order = self._list_schedule() if reorder else self.ops
        for e in ENGS:
            for i, op in enumerate(order[e]):
                op.pos = i
        def needs_wait(op, d):
            if d[0] == "d":
                return True
            p = d[1]
            if p.eng == op.eng and op.dma is None and op.eng == "pe":
                return False
            return True
        for e in ENGS:
            for op in order[e]:
                for d in op.deps:
                    if d[0] == "e" and needs_wait(op, d):
                        d[1].marked = True
        count = {}
        for e in ENGS:
            cnt = 0
            for op in order[e]:
                if op.marked:
                    cnt += 1
                count[id(op)] = cnt
        from contextlib import ExitStack
        with ExitStack() as st:
            esem = {e: st.enter_context(nc.semaphore("sem_" + e)) for e in ENGS}
            for i, b in enumerate(self.dma_bufs):
                b.dsem = st.enter_context(nc.semaphore("dsem%d" % i))
            block = st.enter_context(nc.Block())

            def run(e, eng):
                seen = {}
                for op in order[e]:
                    for d in op.deps:
                        if not needs_wait(op, d):
                            continue
                        p = d[1]
                        if d[0] == "e":
                            sem = esem[p.eng]
                            val = count[id(p)]
                        else:
                            sem = p.dma.dsem
                            val = p.dcount
                        k = id(sem)
                        if seen.get(k, 0) >= val:
                            continue
                        seen[k] = val
                        eng.wait_ge(sem, val)
                    ins = op.fn(eng)
                    if op.dma is not None:
                        ins.then_inc(op.dma.dsem, 16)
                    elif op.marked:
                        ins.then_inc(esem[e], 1)

            final = [(b.dsem, b.dcnt) for b in self.dma_bufs]

            @block.tensor
            def _(eng):
                run("pe", eng)

            @block.scalar
            def _(eng):
                run("act", eng)

            @block.vector
            def _(eng):
                run("dve", eng)

            @block.gpsimd
            def _(eng):
                run("pool", eng)

            @block.sync
            def _(eng):
                run("sp", eng)
                for sem, cnt in final:
                    eng.wait_ge(sem, cnt)


D = 1024
KD = D // 128
DFF = 2816
KF = DFF // 128
SEQ = 4096
DEPTH = 4
EPS = 1e-6
T = 1024
NH = T // 512
TM = 512
NHD = 8
HC = 32


class Ring:
    def __init__(self, name, aps):
        self.aps = aps
        self.bufs = [Buf("%s%d" % (name, i)) for i in range(len(aps))]
        self.i = 0

    def next(self):
        i = self.i
        self.i = (i + 1) % len(self.aps)
        return self.aps[i], self.bufs[i]


class Ctx:
    pass


def _n(ap):
    n = 1
    for s in ap.shape[1:]:
        n *= int(s)
    return n


def _mm(S, out, lhsT, rhs, start, stop, reads, writes):
    n = _n(rhs)
    cost = 25.0 + max(n, 64) / 2.4 * (4.0 if lhsT.dtype == F32 else 1.0)
    S.op("pe", lambda e: e.matmul(out, lhsT=lhsT, rhs=rhs, start=start, stop=stop),
         reads=reads, writes=writes, cost=cost)


def _act(S, out, in_, func, reads, writes, **kw):
    S.op("act", lambda e: e.activation(out=out, in_=in_, func=func, **kw), reads=reads, writes=writes,
         cost=230.0 + 0.95 * _n(out))


def _tt(S, eng, out, in0, in1, op, reads, writes):
    n = _n(out)
    cost = (120.0 + 1.4 * n) if eng == "dve" else (350.0 + 2.2 * n)
    S.op(eng, lambda e: e.tensor_tensor(out=out, in0=in0, in1=in1, op=op), reads=reads, writes=writes, cost=cost)


def emit_rmsnorm(S, c, wcol):
    banks = [c.bank() for _ in range(NH)]
    for k in range(KD):
        sq, bsq = c.sq_ring.next()
        _act(S, sq, c.xT[:, k, :], AF.Square, [c.BxT[k]], [bsq])
        for h in range(NH):
            ps, bps = banks[h]
            _mm(S, ps, c.ones_bf[:, :], sq[:, h * 512:(h + 1) * 512], k == 0, k == KD - 1,
                [bsq, c.Bconst], [bps])
    for h in range(NH):
        ps, bps = banks[h]
        sl = slice(h * 512, (h + 1) * 512)
        _act(S, c.rstd[:, sl], ps, AF.Ln, [bps, c.Bconst], [c.Brstd[h]], scale=1.0 / D, bias=c.eps_col[:, 0:1])
        _act(S, c.rstd[:, sl], c.rstd[:, sl], AF.Exp, [c.Brstd[h]], [c.Brstd[h]], scale=-0.5)
    for k in range(KD):
        for h in range(NH):
            sl = slice(h * 512, (h + 1) * 512)
            S.op("dve", lambda e, k=k, sl=sl: e.scalar_tensor_tensor(
                out=c.uT[:, k, sl], in0=c.xT[:, k, sl], scalar=c.nw[:, wcol + k:wcol + k + 1],
                in1=c.rstd[:, sl], op0=ALU.mult, op1=ALU.mult),
                reads=[c.BxT[k], c.Brstd[h], c.Bconst], writes=[c.BuT[k][h]])


def enter_phase(S, c, name):
    new = c.phase_bufs[name]
    if c.cur_phase is not None and c.cur_phase != name:
        S.fence(c.phase_bufs[c.cur_phase], new)
    c.cur_phase = name
    if name == "ffn":
        c.bank = c.bank8.next
    else:
        c.bank = c.bank4.next


def emit_ffn(S, c, l, j):
    enter_phase(S, c, "ffn")
    emit_rmsnorm(S, c, (l * 3 + (0 if j == 0 else 2)) * KD)
    for fc in range(KF):
        w, bw = c.gu_ring.next()
        S.dma("pool", w, c.d_wgu[l, j, fc], writes=[bw])
        bk = [[c.bank() for _ in range(NH)] for _ in range(2)]
        for k in range(KD):
            for g in range(2):
                for h in range(NH):
                    ps, bps = bk[g][h]
                    _mm(S, ps, w[:, g, k, :], c.uT[:, k, h * 512:(h + 1) * 512], k == 0, k == KD - 1,
                        [bw, c.BuT[k][h]], [bps])
        for h in range(NH):
            sl = slice(h * 512, (h + 1) * 512)
            sg, bsg = c.sg_ring.next()
            pg, bpg = bk[0][h]
            pu, bpu = bk[1][h]
            _act(S, sg, pg, AF.Silu, [bpg], [bsg])
            _tt(S, "dve", c.actT[:, fc, sl], pu, sg, ALU.mult, [bpu, bsg], [c.Bact[fc][h]])
    for dc in range(KD):
        w, bw = c.dn_ring.next()
        S.dma("pool", w, c.d_wd[l, j, dc], writes=[bw])
        bk = [c.bank() for _ in range(NH)]
        for k in range(KF):
            for h in range(NH):
                ps, bps = bk[h]
                _mm(S, ps, w[:, k, :], c.actT[:, k, h * 512:(h + 1) * 512], k == 0, k == KF - 1,
                    [bw, c.Bact[k][h]], [bps])
        for h in range(NH):
            sl = slice(h * 512, (h + 1) * 512)
            ps, bps = bk[h]
            S.op("dve", lambda e, ps=ps, dc=dc, sl=sl: e.scalar_tensor_tensor(
                out=c.xT[:, dc, sl], in0=ps, scalar=0.5, in1=c.xT[:, dc, sl],
                op0=ALU.mult, op1=ALU.add),
                reads=[bps, c.BxT[dc]], writes=[c.BxT[dc]])


def emit_final(S, c, tile):
    enter_phase(S, c, "ffn")
    wc = DEPTH * 3 * KD
    emit_rmsnorm(S, c, wc)
    for k in range(KD):
        for h in range(NH):
            sl = slice(h * 512, (h + 1) * 512)
            S.op("dve", lambda e, k=k, sl=sl: e.scalar_tensor_tensor(
                out=c.xT[:, k, sl], in0=c.xT[:, k, sl], scalar=c.nw[:, wc + k:wc + k + 1],
                in1=c.rstd[:, sl], op0=ALU.mult, op1=ALU.mult),
                reads=[c.BxT[k], c.Brstd[h], c.Bconst], writes=[c.BxT[k]])
        S.dma("sp", c.d_out[k * 128:(k + 1) * 128, tile * T:(tile + 1) * T], c.xT[:, k, :],
              reads=[c.BxT[k]])


def emit_hgrn(S, c, l, jm):
    enter_phase(S, c, "hgrn")
    emit_rmsnorm(S, c, (l * 3 + 1) * KD)
    h = c.h
    NB = TM // 128
    NCH = TM // HC
    for sub in range(T // TM):
        t0 = sub * TM
        tsl = slice(t0, t0 + TM)
        hh = sub % NH if TM == 512 else None
        rdu = lambda k: [c.BuT[k][(t0 // 512)]]
        for vp in range(NHD // 2):
            w, bw = c.gu_ring.next()
            S.dma("pool", w, c.d_hwv[jm, vp], writes=[bw])
            for b in range(NB):
                pv, bpv = c.bank()
                for i in range(2):
                    for k in range(KD):
                        _mm(S, pv[:, i * 128:(i + 1) * 128], c.uT[:, k, t0 + b * 128:t0 + (b + 1) * 128],
                            w[:, i, k, :], k == 0, k == KD - 1, rdu(k) + [bw], [bpv])
                S.op("act", lambda e, pv=pv, b=b, vp=vp: e.copy(out=h.vall[:, b, vp * 256:(vp + 1) * 256], in_=pv[:, 0:256]),
                     reads=[bpv], writes=[h.Bvall[b]])
        for hd in range(NHD):
            w, bw = c.gu_ring.next()
            S.dma("pool", w, c.d_hwqf[jm, hd], writes=[bw])
            pq, bpq = c.bank()
            pf, bpf = c.bank()
            for k in range(KD):
                _mm(S, pq, w[:, 0, k, :], c.uT[:, k, tsl], k == 0, k == KD - 1, [bw] + rdu(k), [bpq])
            for k in range(KD):
                _mm(S, pf, w[:, 1, k, :], c.uT[:, k, tsl], k == 0, k == KD - 1, [bw] + rdu(k), [bpf])
            qs, bqs = h.tmp.next()
            sg, bsg = h.tmp.next()
            lf, blf = h.tmp.next()
            kk, bkk = h.tmp.next()
            cs, bcs = h.tmp.next()
            eq, beq = h.tmp.next()
            ek, bek = h.tmp.next()
            _act(S, qs, pq, AF.Silu, [bpq], [bqs])
            _act(S, sg, pf, AF.Sigmoid, [bpf], [bsg])
            cb = (jm * 3) * NHD + hd
            _act(S, lf, sg, AF.Ln, [bsg, c.Bconst], [blf],
                 scale=h.hconst[:, cb + NHD:cb + NHD + 1], bias=h.hconst[:, cb:cb + 1])
            S.op("dve", lambda e, kk=kk, sg=sg, cb=cb: e.tensor_scalar(
                out=kk, in0=sg, scalar1=h.hconst[:, cb + 2 * NHD:cb + 2 * NHD + 1],
                scalar2=h.hconst[:, cb + NHD:cb + NHD + 1], op0=ALU.mult, op1=ALU.add),
                reads=[bsg, c.Bconst], writes=[bkk])
            S.op("dve", lambda e, cs=cs, lf=lf: e.tensor_tensor_scan(
                out=cs, data0=h.mask32[:, :], data1=lf, initial=0.0, op0=ALU.mult, op1=ALU.add),
                reads=[blf, c.Bconst], writes=[bcs])
            _act(S, eq, cs, AF.Exp, [bcs], [beq])
            _act(S, ek, cs, AF.Exp, [bcs], [bek], scale=-1.0)
            _tt(S, "dve", h.qT[:, hd, :], qs, eq, ALU.mult, [bqs, beq], [h.BqT[hd]])
            _tt(S, "dve", kk, kk, ek, ALU.mult, [bkk, bek], [bkk])
            S.op("pool", lambda e, kk=kk, hd=hd: e.tensor_copy(out=h.kT[:, hd, :], in_=kk),
                 reads=[bkk], writes=[h.BkT[hd]])
            eq3 = eq.rearrange("p (c j) -> p c j", j=HC)
            kd, bkd = h.kd_ring.next()
            _tt(S, "pool", kd.rearrange("p (c j) -> p c j", j=HC), kk.rearrange("p (c j) -> p c j", j=HC),
                eq3[:, :, HC - 1:HC].broadcast_to([128, NCH, HC]), ALU.mult, [bkk, beq], [bkd])
            S.op("pool", lambda e, eq3=eq3, hd=hd: e.tensor_copy(
                out=h.elast[:, hd, :].rearrange("p (c o) -> p c o", o=1), in_=eq3[:, :, HC - 1:HC]),
                reads=[beq], writes=[h.Bel[hd]])
            for b in range(NB):
                pt, bpt = c.qbank()
                ptb = pt.bitcast(BF16)[:, 0:128]
                S.op("pe", lambda e, ptb=ptb, kd=kd, b=b: e.transpose(ptb, kd[:, b * 128:(b + 1) * 128], c.ident[:, :]),
                     reads=[bkd, c.Bconst], writes=[bpt])
                S.op("act", lambda e, ptb=ptb, b=b, hd=hd: e.copy(out=h.kdtm[:, b, hd, :], in_=ptb),
                     reads=[bpt], writes=[h.Bkdtm[b][hd]])
        for gp in range(NHD // 2):
            w, bw = c.gu_ring.next()
            S.dma("pool", w, c.d_hwg[jm, gp], writes=[bw])
            for i in range(2):
                hd = gp * 2 + i
                pg, bpg = c.bank()
                for k in range(KD):
                    _mm(S, pg, w[:, i, k, :], c.uT[:, k, tsl], k == 0, k == KD - 1, [bw] + rdu(k), [bpg])
                _act(S, h.sgT[:, hd, :], pg, AF.Silu, [bpg], [h.BsgT[hd]])
        for b in range(NB):
            tok = slice(t0 + b * 128, t0 + (b + 1) * 128)
            bsl = slice(b * 128, (b + 1) * 128)
            for cc in range(4):
                for half in range(2):
                    hs = slice(half * 512, (half + 1) * 512)
                    eng = "dve" if (cc + half) % 2 == 0 else "act"
                    if eng == "dve":
                        S.op("dve", lambda e, hs=hs, cc=cc, b=b: e.tensor_scalar(
                            out=h.vm[:, cc, hs], in0=h.vall[:, b, hs], scalar1=h.cmask[:, cc:cc + 1], scalar2=None,
                            op0=ALU.mult), reads=[h.Bvall[b], c.Bconst], writes=[h.Bvm[cc][half]])
                    else:
                        _act(S, h.vm[:, cc, hs], h.vall[:, b, hs], AF.Copy, [h.Bvall[b], c.Bconst], [h.Bvm[cc][half]],
                             scale=h.cmask[:, cc:cc + 1])
            for hb in range(2):
                pa, bpa = c.bank()
                for i in range(4):
                    hd = hb * 4 + i
                    _mm(S, pa[:, i * 128:(i + 1) * 128], h.kT[:, hd, bsl], h.qT[:, hd, bsl], True, True,
                        [h.BkT[hd], h.BqT[hd]], [bpa])
                _tt(S, "dve", h.attn[:, hb * 4:(hb + 1) * 4, :], pa.rearrange("p (i l) -> p i l", i=4),
                    h.maskbd[:, :].unsqueeze(1).broadcast_to([128, 4, 128]), ALU.mult,
                    [bpa, c.Bconst], [h.Battn[hb]])
            pos = []
            for hb in range(2):
                po, bpo = c.bank()
                pos.append((po, bpo))
                for i in range(4):
                    hd = hb * 4 + i
                    _mm(S, po[:, i * 128:(i + 1) * 128], h.vall[:, b, hd * 128:(hd + 1) * 128], h.attn[:, hd, :],
                        True, True, [h.Bvall[b], h.Battn[hb]], [bpo])
            pis = [c.bank() for _ in range(2)]
            for cc in range(4):
                ch = (t0 + b * 128) // HC % (T // HC)
                chunk_in_sub = b * 4 + cc
                for hd in range(NHD):
                    par = h.par[jm][hd]
                    pi, bpi = pis[hd // 4]
                    i = hd % 4
                    _mm(S, pi[:, i * 128 + cc * HC:i * 128 + (cc + 1) * HC], h.sbf[par][:, jm, hd, :],
                        h.qT[:, hd, b * 128 + cc * HC:b * 128 + (cc + 1) * HC], True, True,
                        [h.Bsbf[par][jm][hd], h.BqT[hd]], [bpi])
                    pu, bpu = c.qbank()
                    _mm(S, pu, h.kdtm[:, b, hd, :], h.vm[:, cc, hd * 128:(hd + 1) * 128], True, True,
                        [h.Bkdtm[b][hd], h.Bvm[cc][hd // 4]], [bpu])
                    S.op("dve", lambda e, pu=pu, hd=hd, chunk_in_sub=chunk_in_sub: e.scalar_tensor_tensor(
                        out=h.S[:, jm, hd, :], in0=h.S[:, jm, hd, :],
                        scalar=h.elast[:, hd, chunk_in_sub:chunk_in_sub + 1], in1=pu,
                        op0=ALU.mult, op1=ALU.add),
                        reads=[bpu, h.BS[jm][hd], h.Bel[hd]], writes=[h.BS[jm][hd]])
                    S.op("act", lambda e, hd=hd, par=par: e.copy(out=h.sbf[1 - par][:, jm, hd, :], in_=h.S[:, jm, hd, :]),
                         reads=[h.BS[jm][hd]], writes=[h.Bsbf[1 - par][jm][hd]])
                    h.par[jm][hd] = 1 - par
            for hb in range(2):
                pi, bpi = pis[hb]
                po, bpo = pos[hb]
                oi, boi = h.tmp.next()
                osum, bos = h.tmp.next()
                S.op("act", lambda e, oi=oi, pi=pi: e.copy(out=oi, in_=pi), reads=[bpi], writes=[boi])
                _tt(S, "dve", osum, po, oi, ALU.add, [bpo, boi], [bos])
                osq, bosq = c.sg_ring.next()
                _act(S, osq, osum, AF.Square, [bos], [bosq])
                pss, bpss = c.bank()
                _mm(S, pss, c.ones_bf[:, :], osq, True, True, [bosq, c.Bconst], [bpss])
                rs, brs = h.tmp.next()
                _act(S, rs, pss, AF.Ln, [bpss, c.Bconst], [brs], scale=1.0 / 128, bias=c.eps_col[:, 0:1])
                _act(S, rs, rs, AF.Exp, [brs], [brs], scale=-0.5)
                _tt(S, "dve", osum, osum, rs, ALU.mult, [bos, brs], [bos])
                S.op("dve", lambda e, osum=osum, hb=hb, bsl=bsl: e.scalar_tensor_tensor(
                    out=h.ogT[:, hb * 4:(hb + 1) * 4, bsl], in0=osum.rearrange("p (i l) -> p i l", i=4),
                    scalar=h.hnw[:, jm:jm + 1], in1=h.sgT[:, hb * 4:(hb + 1) * 4, bsl],
                    op0=ALU.mult, op1=ALU.mult),
                    reads=[bos, c.Bconst] + [h.BsgT[hb * 4 + i] for i in range(4)],
                    writes=[h.BsgT[hb * 4 + i] for i in range(4)])
        for dc in range(KD):
            w, bw = c.dn_ring.next()
            S.dma("pool", w[:, 0:KD, :], c.d_hwo[jm, dc], writes=[bw])
            ps, bps = c.bank()
            for k in range(KD):
                _mm(S, ps, w[:, k, :], h.ogT[:, k, :], k == 0, k == KD - 1,
                    [bw, h.BsgT[k]], [bps])
            _tt(S, "dve", c.xT[:, dc, tsl], ps, c.xT[:, dc, tsl], ALU.add, [bps, c.BxT[dc]], [c.BxT[dc]])


def emit_hgrn_consts(S, c):
    h = c.h
    ex = h.lbtmp
    S.dma("sp", ex[:, 0:4 * NHD], c.d_hlb, writes=[c.Bconst])
    _act(S, ex[:, 0:4 * NHD], ex[:, 0:4 * NHD], AF.Exp, [c.Bconst], [c.Bconst])
    den = ex[:, 4 * NHD:5 * NHD]
    e = lambda i: ex[:, i * NHD:(i + 1) * NHD]
    _tt(S, "dve", den, e(0), e(1), ALU.add, [c.Bconst], [c.Bconst])
    _tt(S, "dve", den, den, e(2), ALU.add, [c.Bconst], [c.Bconst])
    _tt(S, "dve", den, den, e(3), ALU.add, [c.Bconst], [c.Bconst])
    rden = ex[:, 5 * NHD:6 * NHD]
    S.op("dve", lambda e_: e_.reciprocal(out=rden, in_=den), reads=[c.Bconst], writes=[c.Bconst])
    s123 = ex[:, 6 * NHD:7 * NHD]
    _tt(S, "dve", s123, e(1), e(2), ALU.add, [c.Bconst], [c.Bconst])
    _tt(S, "dve", s123, s123, e(3), ALU.add, [c.Bconst], [c.Bconst])
    for jm, num in ((0, e(1)), (1, s123)):
        lb = h.hconst[:, (jm * 3) * NHD:(jm * 3 + 1) * NHD]
        oml = h.hconst[:, (jm * 3 + 1) * NHD:(jm * 3 + 2) * NHD]
        noml = h.hconst[:, (jm * 3 + 2) * NHD:(jm * 3 + 3) * NHD]
        _tt(S, "dve", lb, num, rden, ALU.mult, [c.Bconst], [c.Bconst])
        S.op("dve", lambda e_, lb=lb, oml=oml: e_.tensor_scalar(out=oml, in0=lb, scalar1=-1.0, scalar2=1.0,
                                                                 op0=ALU.mult, op1=ALU.add),
             reads=[c.Bconst], writes=[c.Bconst])
        S.op("dve", lambda e_, noml=noml, oml=oml: e_.tensor_scalar(out=noml, in0=oml, scalar1=-1.0, scalar2=None,
                                                                    op0=ALU.mult),
             reads=[c.Bconst], writes=[c.Bconst])
    S.op("pool", lambda e_: e_.memset(h.mask32[:, :], 1.0), writes=[c.Bconst])
    S.op("pool", lambda e_: e_.memset(h.mask32[:, :].rearrange("p (c j) -> p c j", j=HC)[:, :, 0:1], 0.0),
         reads=[c.Bconst], writes=[c.Bconst])
    S.op("pool", lambda e_: e_.memset(h.cmask[:, :], 1.0), writes=[c.Bconst])
    for cc in range(4):
        S.op("pool", lambda e_, cc=cc: e_.affine_select(
            out=h.cmask[:, cc:cc + 1], in_=h.cmask[:, cc:cc + 1], pattern=[[0, 1]], compare_op=ALU.is_ge,
            fill=0.0, base=-32 * cc, channel_multiplier=1), reads=[c.Bconst], writes=[c.Bconst])
        S.op("pool", lambda e_, cc=cc: e_.affine_select(
            out=h.cmask[:, cc:cc + 1], in_=h.cmask[:, cc:cc + 1], pattern=[[0, 1]], compare_op=ALU.is_ge,
            fill=0.0, base=32 * cc + 31, channel_multiplier=-1), reads=[c.Bconst], writes=[c.Bconst])
    S.op("pool", lambda e_: e_.memset(h.maskbd[:, :], 1.0), writes=[c.Bconst])
    S.op("pool", lambda e_: e_.affine_select(out=h.maskbd[:, :], in_=h.maskbd[:, :], pattern=[[1, 128]],
                                             compare_op=ALU.is_ge, fill=0.0, base=0, channel_multiplier=-1),
         reads=[c.Bconst], writes=[c.Bconst])
    for cc in range(4):
        S.op("dve", lambda e_, cc=cc: e_.tensor_scalar(
            out=h.maskbd[:, cc * 32:(cc + 1) * 32], in0=h.maskbd[:, cc * 32:(cc + 1) * 32],
            scalar1=h.cmask[:, cc:cc + 1], scalar2=None, op0=ALU.mult), reads=[c.Bconst], writes=[c.Bconst])
    S.op("pool", lambda e_: e_.memset(c.ident[:, :], 0.0), writes=[c.Bconst])
    S.op("pool", lambda e_: e_.affine_select(out=c.ident[:, :], in_=c.ident[:, :], pattern=[[-1, 128]],
                                             compare_op=ALU.not_equal, fill=1.0, base=0, channel_multiplier=1),
         reads=[c.Bconst], writes=[c.Bconst])
    S.op("pool", lambda e_: e_.memset(h.S[:], 0.0), writes=[b for r in h.BS for b in r])
    for par in range(2):
        S.op("pool", lambda e_, par=par: e_.memset(h.sbf[par][:], 0.0), writes=[b for r in h.Bsbf[par] for b in r])
    S.dma("sp", h.hnw[:, :], c.d_hnw, writes=[c.Bconst])


MT_ = 256
MG = 4
MHG = 8


def _pad_ap(ap2):
    return bass.AP(ap2.tensor, ap2.offset, [[ap2.ap[0][0], 128], [192, 2], [1, 64]])


def emit_mamba(S, c, l, jm):
    enter_phase(S, c, "mamba")
    emit_rmsnorm(S, c, (l * 3 + 1) * KD)
    m = c.m
    NCK = MT_ // 128
    S.op("pool", lambda e: e.memset(m.xpad[:], 0.0), writes=[m.Bxpad])
    S.op("pool", lambda e: e.memset(m.ppad[:], 0.0), writes=m.Bppad)
    for g in range(MG):
        for i in range(MHG // 2):
            ci = g * 4 + i
            S.op("act", lambda e, ci=ci: e.copy(out=_pad_ap(m.ppad[:, 2 * ci:2 * ci + 2, :].rearrange("p a b -> p (a b)")),
                                                  in_=m.prev[:, jm, ci * 128:(ci + 1) * 128].rearrange("p (a b) -> p a b", a=2)),
                 reads=[m.Bprev[jm][g]], writes=[m.Bppad[g]])
    for sub in range(T // MT_):
        t0 = sub * MT_
        tsl = slice(t0, t0 + MT_)
        rdu = lambda k: [c.BuT[k][t0 // 512]]
        for pr in range(20):
            w, bw = c.gu_ring.next()
            S.dma("pool", w, c.d_mwin[jm, pr], writes=[bw])
            for i in range(2):
                ch = pr * 2 + i
                ps, bps = c.bank()
                for k in range(KD):
                    _mm(S, ps[:, 0:MT_], w[:, i, k, :], c.uT[:, k, tsl], k == 0, k == KD - 1, [bw] + rdu(k), [bps])
                if ch < 16:
                    _act(S, m.siluz[:, ch, :], ps[:, 0:MT_], AF.Silu, [bps], [m.Bsz[ch]])
                    continue
                ci = ch - 16
                raw, braw = m.raw_ring.next()
                acc, bacc = m.acc_ring.next()
                S.op("pool", lambda e, raw=raw, ci=ci: e.tensor_copy(out=raw[:, 0:3], in_=m.tails[:, jm, ci, :]),
                     reads=[m.Btail[jm][ci]], writes=[braw])
                S.op("act", lambda e, raw=raw, ps=ps: e.copy(out=raw[:, 3:3 + MT_], in_=ps[:, 0:MT_]),
                     reads=[bps], writes=[braw])
                S.op("pool", lambda e, raw=raw, ci=ci: e.tensor_copy(out=m.tails[:, jm, ci, :], in_=raw[:, MT_:MT_ + 3]),
                     reads=[braw], writes=[m.Btail[jm][ci]])
                cw = lambda tap, ci=ci: m.mconv[:, jm, ci, tap:tap + 1]
                S.op("dve", lambda e, raw=raw, acc=acc, cw=cw: e.tensor_scalar(
                    out=acc, in0=raw[:, 0:MT_], scalar1=cw(0), scalar2=None, op0=ALU.mult),
                    reads=[braw, c.Bconst], writes=[bacc])
                for tap in range(1, 4):
                    S.op("dve", lambda e, raw=raw, acc=acc, cw=cw, tap=tap: e.scalar_tensor_tensor(
                        out=acc, in0=raw[:, tap:tap + MT_], scalar=cw(tap), in1=acc, op0=ALU.mult, op1=ALU.add),
                        reads=[braw, bacc, c.Bconst], writes=[bacc])
                if ci < 16:
                    dst, bdst = m.xc[:, ci, :], m.Bxc[ci]
                elif ci < 20:
                    dst, bdst = m.BT[:, ci - 16, :], m.BBT[ci - 16]
                else:
                    dst, bdst = m.CT[:, ci - 20, :], m.BCT[ci - 20]
                _act(S, dst, acc, AF.Silu, [bacc, c.Bconst], [bdst], bias=cw(4))
        for ck in range(NCK):
            csl = slice(ck * 128, (ck + 1) * 128)
            tok = slice(t0 + ck * 128, t0 + (ck + 1) * 128)
            pd, bpd = c.qbank()
            for k in range(KD):
                _mm(S, pd[:, 0:32], c.uT[:, k, tok], m.wdt[:, jm, k, :], k == 0, k == KD - 1, rdu(k) + [m.Bwdt], [bpd])
            _tt(S, "dve", m.dt[:, :], pd[:, 0:32], m.dtb[:, jm, :], ALU.add, [bpd, c.Bconst], [m.Bdt])
            _act(S, m.dt[:, :], m.dt[:, :], AF.Exp, [m.Bdt], [m.Bdt])
            _act(S, m.dt[:, :], m.dt[:, :], AF.Ln, [m.Bdt], [m.Bdt], bias=1.0)
            _tt(S, "dve", m.dA[:, :], m.dt[:, :], m.Aneg[:, jm, :], ALU.mult, [m.Bdt, c.Bconst], [m.BdA])
            pr_, bpr = c.qbank()
            _mm(S, pr_[:, 0:32], m.Umat[:, :], m.dA[:, :], True, True, [m.BdA, c.Bconst], [bpr])
            _act(S, m.dtdec[:, :], pr_[:, 0:32], AF.Exp, [bpr], [m.Bdtdec])
            _tt(S, "dve", m.dtdec[:, :], m.dtdec[:, :], m.dt[:, :], ALU.mult, [m.Bdtdec, m.Bdt], [m.Bdtdec])
            for ci in range(16):
                pt, bpt = c.qbank()
                ptb = pt.bitcast(BF16)[:, 0:128]
                S.op("pe", lambda e, ptb=ptb, ci=ci, csl=csl: e.transpose(ptb, m.xc[:, ci, csl], c.ident[:, :]),
                     reads=[m.Bxc[ci], c.Bconst], writes=[bpt])
                pt3 = ptb.rearrange("p (a b) -> p a b", a=2)
                _tt(S, "dve", _pad_ap(m.xpad[:, 2 * ci:2 * ci + 2, :].rearrange("p a b -> p (a b)")), pt3,
                    m.dt[:, 2 * ci:2 * ci + 2].unsqueeze(2).broadcast_to([128, 2, 64]), ALU.mult,
                    [bpt, m.Bdt], [m.Bxpad])
                _tt(S, "dve", m.xdd[:, ci * 128:(ci + 1) * 128].rearrange("p (a b) -> p a b", a=2), pt3,
                    m.dtdec[:, 2 * ci:2 * ci + 2].unsqueeze(2).broadcast_to([128, 2, 64]), ALU.mult,
                    [bpt, m.Bdtdec], [m.Bxdd[ci // 4]])
            for g in range(MG):
                pt, bpt = c.qbank()
                ptb = pt.bitcast(BF16)[:, 0:128]
                S.op("pe", lambda e, ptb=ptb, g=g, csl=csl: e.transpose(ptb, m.BT[:, g, csl], c.ident[:, :]),
                     reads=[m.BBT[g], c.Bconst], writes=[bpt])
                S.op("act", lambda e, ptb=ptb, g=g: e.copy(out=m.btm[:, g, :], in_=ptb), reads=[bpt], writes=[m.Bbtm[g]])
            for g in range(MG):
                hs = slice(g * MHG, (g + 1) * MHG)
                R, bR = m.R_ring.next()
                _tt(S, "dve", R, m.Tmat[:, :].unsqueeze(1).broadcast_to([128, MHG, 128]),
                    m.dA[:, hs].unsqueeze(2).broadcast_to([128, MHG, 128]), ALU.mult, [m.BdA, c.Bconst], [bR])
                R2 = R.rearrange("p h l -> p (h l)")
                LT, bLT = m.LT_ring.next()
                ecs, becs = m.ecs_ring.next()
                for half in range(2):
                    hsl = slice(half * 512, (half + 1) * 512)
                    p1, bp1 = c.bank()
                    _mm(S, p1, m.Umat[:, :], R2[:, hsl], True, True, [bR, c.Bconst], [bp1])
                    _act(S, LT.rearrange("p h l -> p (h l)")[:, hsl], p1, AF.Exp, [bp1], [bLT])
                    p2, bp2 = c.bank()
                    _mm(S, p2, m.onesf[:, :], R2[:, hsl], True, True, [bR, c.Bconst], [bp2])
                    _act(S, ecs.rearrange("p h l -> p (h l)")[:, hsl], p2, AF.Exp, [bp2], [becs])
                pg, bpg = c.qbank()
                _mm(S, pg, m.BT[:, g, csl], m.CT[:, g, csl], True, True, [m.BBT[g], m.BCT[g]], [bpg])
                Gm, bGm = m.Gm_ring.next()
                _tt(S, "dve", Gm, pg, m.maskc[:, :], ALU.mult, [bpg, c.Bconst], [bGm])
                MTt, bMT = m.MT_ring.next()
                _tt(S, "pool", MTt, LT, Gm.unsqueeze(1).broadcast_to([128, MHG, 128]), ALU.mult, [bLT, bGm], [bMT])
                Cp, bCp = m.Cp_ring.next()
                _tt(S, "pool", Cp, ecs, m.CT[:, g, csl].unsqueeze(1).broadcast_to([128, MHG, 128]), ALU.mult,
                    [becs, m.BCT[g]], [bCp])
                py, bpy = c.bank()
                for i in range(4):
                    ci = g * 4 + i
                    o = py[:, i * 128:(i + 1) * 128]
                    for a in range(2):
                        _mm(S, o, m.xpad[:, 2 * ci + a, :], MTt[:, 2 * i + a, :], a == 0, False, [m.Bxpad, bMT], [bpy])
                    for a in range(2):
                        _mm(S, o, m.ppad[:, 2 * ci + a, :], Cp[:, 2 * i + a, :], False, a == 1, [m.Bppad[g], bCp], [bpy])
                pu, bpu = c.bank()
                _mm(S, pu, m.btm[:, g, :], m.xdd[:, g * 512:(g + 1) * 512], True, True, [m.Bbtm[g], m.Bxdd[g]], [bpu])
                pv = m.prev[:, jm, g * 512:(g + 1) * 512]
                _tt(S, "dve", pv.rearrange("p (h q) -> p h q", h=MHG), pv.rearrange("p (h q) -> p h q", h=MHG),
                    ecs[:, :, 127:128].broadcast_to([128, MHG, 64]), ALU.mult, [m.Bprev[jm][g], becs], [m.Bprev[jm][g]])
                _tt(S, "dve", pv, pv, pu, ALU.add, [m.Bprev[jm][g], bpu], [m.Bprev[jm][g]])
                for i in range(4):
                    ci = g * 4 + i
                    S.op("act", lambda e, ci=ci: e.copy(
                        out=_pad_ap(m.ppad[:, 2 * ci:2 * ci + 2, :].rearrange("p a b -> p (a b)")),
                        in_=m.prev[:, jm, ci * 128:(ci + 1) * 128].rearrange("p (a b) -> p a b", a=2)),
                        reads=[m.Bprev[jm][g]], writes=[m.Bppad[g]])
                yg, byg = m.yg_ring.next()
                for i in range(4):
                    ci = g * 4 + i
                    S.op("dve", lambda e, i=i, ci=ci, yg=yg, py=py, csl=csl: e.scalar_tensor_tensor(
                        out=yg[:, i, :], in0=m.xc[:, ci, csl], scalar=m.dcol[:, jm, ci:ci + 1],
                        in1=py[:, i * 128:(i + 1) * 128], op0=ALU.mult, op1=ALU.add),
                        reads=[m.Bxc[ci], bpy, c.Bconst], writes=[byg])
                _tt(S, "dve", yg, yg, m.siluz[:, g * 4:(g + 1) * 4, csl], ALU.mult,
                    [byg] + [m.Bsz[g * 4 + i] for i in range(4)], [byg])
                ysq, bysq = c.sg_ring.next()
                _act(S, ysq, yg.rearrange("p i l -> p (i l)"), AF.Square, [byg], [bysq])
                pn, bpn = c.qbank()
                for i in range(4):
                    _mm(S, pn, c.ones_bf[:, :], ysq[:, i * 128:(i + 1) * 128], i == 0, i == 3, [bysq, c.Bconst], [bpn])
                rs, brs = m.rs_ring.next()
                _act(S, rs, pn, AF.Ln, [bpn, c.Bconst], [brs], scale=1.0 / 512, bias=c.eps_col[:, 0:1])
                _act(S, rs, rs, AF.Exp, [brs], [brs], scale=-0.5)
                for i in range(4):
                    ci = g * 4 + i
                    S.op("dve", lambda e, i=i, ci=ci, yg=yg, rs=rs, csl=csl: e.scalar_tensor_tensor(
                        out=m.ynT[:, ci, csl], in0=yg[:, i, :], scalar=m.mnw[:, jm, ci:ci + 1], in1=rs,
                        op0=ALU.mult, op1=ALU.mult),
                        reads=[byg, brs, c.Bconst], writes=[m.Byn[ci]])
        for dc in range(KD):
            w, bw = c.dn_ring.next()
            S.dma("pool", w[:, 0:16, :], c.d_mwo[jm, dc], writes=[bw])
            ps, bps = c.bank()
            for k in range(16):
                _mm(S, ps[:, 0:MT_], w[:, k, :], m.ynT[:, k, :], k == 0, k == 15, [bw, m.Byn[k]], [bps])
            _tt(S, "dve", c.xT[:, dc, tsl], ps[:, 0:MT_], c.xT[:, dc, tsl], ALU.add, [bps, c.BxT[dc]], [c.BxT[dc]])


def emit_mamba_consts(S, c):
    m = c.m
    B = [c.Bconst]
    S.dma("sp", m.mconv[:], c.d_mconv, writes=B)
    S.dma("sp", m.dcol[:], c.d_mdcol, writes=B)
    S.dma("sp", m.mnw[:], c.d_mnw, writes=B)
    S.dma("pool", m.wdt[:], c.d_mwdt, writes=[m.Bwdt])
    for j in range(2):
        S.dma("sp", m.dtb[:, j, :], c.d_mdtb[j:j + 1, :].partition_broadcast(128), writes=B)
        S.dma("sp", m.Aneg[:, j, :], c.d_malog[j:j + 1, :].partition_broadcast(128), writes=B)
    _act(S, m.Aneg[:], m.Aneg[:], AF.Exp, B, B)
    S.op("dve", lambda e: e.tensor_scalar(out=m.Aneg[:], in0=m.Aneg[:], scalar1=-1.0, scalar2=None, op0=ALU.mult),
         reads=B, writes=B)
    S.op("pool", lambda e: e.memset(m.onesf[:, :], 1.0), writes=B)
    S.op("pool", lambda e: e.memset(m.Tmat[:, :], 1.0), writes=B)
    S.op("pool", lambda e: e.affine_select(out=m.Tmat[:, :], in_=m.Tmat[:, :], pattern=[[1, 128]],
                                           compare_op=ALU.is_ge, fill=0.0, base=0, channel_multiplier=-1),
         reads=B, writes=B)
    S.op("pool", lambda e: e.memset(m.Umat[:, :], 1.0), writes=B)
    S.op("pool", lambda e: e.affine_select(out=m.Umat[:, :], in_=m.Umat[:, :], pattern=[[-1, 128]],
                                           compare_op=ALU.is_gt, fill=0.0, base=0, channel_multiplier=1),
         reads=B, writes=B)
    S.op("pool", lambda e: e.memset(m.prev[:], 0.0), writes=[b for r in m.Bprev for b in r])
    S.op("pool", lambda e: e.memset(m.tails[:], 0.0), writes=[b for r in m.Btail for b in r])


def build_program(seq=SEQ, depth=DEPTH, plan=None):
    nc = bass.Bass("TRN2", target_bir_lowering=False)
    c = Ctx()
    c.nc = nc
    ntiles = seq // T
    dt = lambda name, shape: nc.dram_tensor(name, shape, F32, kind="ExternalInput").ap()
    c.d_x = dt("xT", [D, seq])
    c.d_wgu = dt("wgu", [DEPTH, 2, KF, 128, 2, KD, 128])
    c.d_wd = dt("wd", [DEPTH, 2, KD, 128, KF, 128])
    NWC = (DEPTH * 3 + 1) * KD
    c.d_nw = dt("nw", [128, NWC])
    c.d_hwqf = dt("hwqf", [2, NHD, 128, 2, KD, 128])
    c.d_hwg = dt("hwg", [2, NHD // 2, 128, 2, KD, 128])
    c.d_hwv = dt("hwv", [2, NHD // 2, 128, 2, KD, 128])
    c.d_hwo = dt("hwo", [2, KD, 128, KD, 128])
    c.d_hlb = dt("hlb", [128, 4 * NHD])
    c.d_hnw = dt("hnw", [128, 2])
    c.d_mwin = dt("mwin", [2, 20, 128, 2, KD, 128])
    c.d_mwdt = dt("mwdt", [128, 2, KD, 32])
    c.d_mconv = dt("mconv", [128, 2, 24, 5])
    c.d_mdcol = dt("mdcol", [128, 2, 16])
    c.d_mnw = dt("mnw", [128, 2, 16])
    c.d_mdtb = dt("mdtb", [2, 32])
    c.d_malog = dt("malog", [2, 32])
    c.d_mwo = dt("mwo", [2, KD, 128, 16, 128])
    c.d_out = nc.dram_tensor("outT", [D, seq], F32, kind="ExternalOutput").ap()
    from contextlib import ExitStack
    with ExitStack() as st:
        sb = lambda n, s, d: st.enter_context(nc.sbuf_tensor(n, s, d))
        c.xT = sb("xT_sb", [128, KD, T], F32)
        c.uT = sb("uT_sb", [128, KD, T], BF16)
        c.rstd = sb("rstd_sb", [128, T], F32)
        c.nw = sb("nw_sb", [128, NWC], F32)
        c.ones_bf = sb("ones_bf", [128, 128], BF16)
        c.ident = sb("ident_bf", [128, 128], BF16)
        c.eps_col = sb("eps_col", [128, 1], F32)
        gu = sb("gu_ring", [128, 4, 2, KD, 128], BF16)
        dn = sb("dn_ring", [128, 2, KF, 128], BF16)
        sq = sb("sq_ring", [128, 2, T], BF16)
        sg = sb("sg_ring", [128, 3, 512], BF16)
        c.gu_ring = Ring("gu", [gu[:, i] for i in range(4)])
        c.dn_ring = Ring("dn", [dn[:, i] for i in range(2)])
        c.sq_ring = Ring("sq", [sq[:, i] for i in range(2)])
        c.sg_ring = Ring("sg", [sg[:, i] for i in range(3)])
        h = c.h = Ctx()
        h.S = sb("h_S", [128, 2, NHD, 128], F32)
        h.sbf = [sb("h_sbf%d" % i, [128, 2, NHD, 128], BF16) for i in range(2)]
        h.hconst = sb("h_const", [128, 2 * 3 * NHD], F32)
        h.lbtmp = sb("h_lbtmp", [128, 7 * NHD], F32)
        h.hnw = sb("h_nw", [128, 2], F32)
        h.mask32 = sb("h_mask32", [128, TM], F32)
        h.cmask = sb("h_cmask", [128, 4], F32)
        h.maskbd = sb("h_maskbd", [128, 128], F32)
        h.BS = [[Buf("hS%d_%d" % (j, i)) for i in range(NHD)] for j in range(2)]
        h.Bsbf = [[[Buf("hsbf%d_%d_%d" % (p, j, i)) for i in range(NHD)] for j in range(2)] for p in range(2)]
        h.par = [[0] * NHD for _ in range(2)]
        ARENA = 80 * 1024
        arena = sb("arena", [128, ARENA], mybir.dt.uint8)
        off = [0]

        def carve(shape, dtype, reset=False):
            if reset:
                off[0] = 0
            n = int(np.prod(shape)) * (2 if dtype == BF16 else 4)
            ap = arena[:, off[0]:off[0] + n].bitcast(dtype)
            off[0] += n
            assert off[0] <= ARENA, (off[0], ARENA)
            if len(shape) > 1:
                names = " ".join("a%d" % i for i in range(len(shape)))
                kw = {"a%d" % i: shape[i] for i in range(len(shape))}
                ap = ap.rearrange("p (%s) -> p %s" % (names, names), **kw)
            return ap

        c.actT = carve([KF, T], BF16, reset=True)
        NB = TM // 128
        h.qT = carve([NHD, TM], BF16, reset=True)
        h.kT = carve([NHD, TM], BF16)
        h.sgT = carve([NHD, TM], BF16)
        h.ogT = h.sgT
        h.vall = carve([NB, 1024], BF16)
        h.vm = carve([4, 1024], BF16)
        h.kdtm = carve([NB, NHD, 128], BF16)
        h.attn = carve([NHD, 128], BF16)
        h.elast = carve([NHD, TM // HC], F32)
        tmp = carve([13, TM], F32)
        kd = carve([2, TM], BF16)
        h.tmp = Ring("htmp", [tmp[:, i, :] for i in range(13)])
        h.kd_ring = Ring("hkd", [kd[:, i, :] for i in range(2)])
        h.BqT = [Buf("hq%d" % i) for i in range(NHD)]
        h.BkT = [Buf("hk%d" % i) for i in range(NHD)]
        h.BsgT = [Buf("hsg%d" % i) for i in range(NHD)]
        h.Bvall = [Buf("hvall%d" % i) for i in range(NB)]
        h.Bvm = [[Buf("hvm%d_%d" % (i, j)) for j in range(2)] for i in range(4)]
        h.Bkdtm = [[Buf("hkdtm%d_%d" % (b, i)) for i in range(NHD)] for b in range(NB)]
        h.Battn = [Buf("hattn%d" % i) for i in range(2)]
        h.Bel = [Buf("hel%d" % i) for i in range(NHD)]


        m = c.m = Ctx()
        m.prev = sb("m_prev", [128, 2, 2048], F32)
        m.tails = sb("m_tails", [128, 2, 24, 3], F32)
        m.mconv = sb("m_conv", [128, 2, 24, 5], F32)
        m.dcol = sb("m_dcol", [128, 2, 16], F32)
        m.mnw = sb("m_nw", [128, 2, 16], F32)
        m.wdt = sb("m_wdt", [128, 2, KD, 32], BF16)
        m.dtb = sb("m_dtb", [128, 2, 32], F32)
        m.Aneg = sb("m_Aneg", [128, 2, 32], F32)
        m.onesf = sb("m_onesf", [128, 128], F32)
        m.Tmat = sb("m_Tmat", [128, 128], F32)
        m.Umat = sb("m_Umat", [128, 128], F32)
        m.maskc = m.Tmat
        m.siluz = carve([16, MT_], BF16, reset=True)
        m.ynT = m.siluz
        m.xc = carve([16, MT_], BF16)
        m.BT = carve([MG, MT_], BF16)
        m.CT = carve([MG, MT_], BF16)
        m.xpad = carve([32, 128], BF16)
        m.ppad = carve([32, 128], BF16)
        m.xdd = carve([2048], BF16)
        m.btm = carve([MG, 128], BF16)
        m.dt = carve([32], F32)
        m.dA = carve([32], F32)
        m.dtdec = carve([32], F32)
        mk = lambda name, n, shape, dtype: Ring(name, [carve(shape, dtype) for _ in range(n)])
        m.R_ring = mk("mR", 2, [MHG, 128], F32)
        m.LT_ring = mk("mLT", 2, [MHG, 128], BF16)
        m.ecs_ring = mk("mecs", 2, [MHG, 128], F32)
        m.Gm_ring = mk("mGm", 2, [128], BF16)
        m.MT_ring = mk("mMT", 2, [MHG, 128], BF16)
        m.Cp_ring = mk("mCp", 2, [MHG, 128], BF16)
        m.yg_ring = mk("myg", 2, [4, 128], F32)
        m.rs_ring = mk("mrs", 2, [128], F32)
        m.raw_ring = mk("mraw", 2, [MT_ + 4], F32)
        m.acc_ring = mk("macc", 2, [MT_], F32)
        m.Bsz = [Buf("msz%d" % i) for i in range(16)]
        m.Bxc = [Buf("mxc%d" % i) for i in range(16)]
        m.BBT = [Buf("mBT%d" % i) for i in range(MG)]
        m.BCT = [Buf("mCT%d" % i) for i in range(MG)]
        m.Byn = m.Bsz
        m.Bxpad = Buf("mxpad")
        m.Bwdt = Buf("mwdt")
        m.Bppad = [Buf("mppad%d" % i) for i in range(MG)]
        m.Bprev = [[Buf("mprev%d_%d" % (j, g)) for g in range(MG)] for j in range(2)]
        m.Btail = [[Buf("mtail%d_%d" % (j, i)) for i in range(24)] for j in range(2)]
        m.Bdt, m.BdA, m.Bdtdec = Buf("mdt"), Buf("mdA"), Buf("mdtdec")
        m.Bxdd = [Buf("mxdd%d" % i) for i in range(MG)]
        m.Bbtm = [Buf("mbtm%d" % i) for i in range(MG)]

        psum = st.enter_context(nc.psum_tensor("psum", [128, 8, 512], F32))
        c.bank8 = Ring("bank", [psum[:, i, :] for i in range(8)])
        c.bank4 = Ring("bankm", [psum[:, i, :] for i in range(4)])
        c.qbank_ring = Ring("qbank", [psum[:, 4 + i, 0:128] for i in range(4)])
        c.qbank = c.qbank_ring.next
        c.bank = c.bank8.next
        c.BxT = [Buf("xT%d" % k) for k in range(KD)]
        c.BuT = [[Buf("uT%d_%d" % (k, hh)) for hh in range(NH)] for k in range(KD)]
        c.Bact = [[Buf("act%d_%d" % (k, hh)) for hh in range(NH)] for k in range(KF)]
        c.Brstd = [Buf("rstd%d" % hh) for hh in range(NH)]
        c.Bconst = Buf("const")
        c.Bout = Buf("out")
        flat = lambda x: [b for r in x for b in (flat(r) if isinstance(r, list) else [r])]
        c.phase_bufs = {
            "ffn": flat(c.Bact) + c.bank8.bufs,
            "hgrn": flat([h.BqT, h.BkT, h.BsgT, h.Bvall, h.Bvm, h.Bkdtm, h.Battn, h.Bel])
                    + h.tmp.bufs + h.kd_ring.bufs + c.bank4.bufs + c.qbank_ring.bufs,
            "mamba": flat([m.Bsz, m.Bxc, m.BBT, m.BCT, [m.Bxpad], m.Bppad, [m.Bdt, m.BdA, m.Bdtdec], m.Bxdd, m.Bbtm])
                     + sum([r.bufs for r in (m.R_ring, m.LT_ring, m.ecs_ring, m.Gm_ring, m.MT_ring, m.Cp_ring,
                                             m.yg_ring, m.rs_ring, m.raw_ring, m.acc_ring)], [])
                     + c.bank4.bufs + c.qbank_ring.bufs,
        }
        c.cur_phase = None
        S = Sched(nc)
        S.op("pool", lambda e: e.memset(c.ones_bf[:], 1.0), writes=[c.Bconst])
        S.op("pool", lambda e: e.memset(c.eps_col[:], EPS), writes=[c.Bconst])
        S.dma("sp", c.nw[:], c.d_nw, writes=[c.Bconst])
        emit_hgrn_consts(S, c)
        emit_mamba_consts(S, c)
        if plan is None:
            plan = ["ffn0", "mix", "ffn1"]
        for tile in range(ntiles):
            for k in range(KD):
                S.dma("sp", c.xT[:, k, :], c.d_x[k * 128:(k + 1) * 128, tile * T:(tile + 1) * T],
                      writes=[c.BxT[k]])
            for l in range(depth):
                for stg in plan:
                    if stg == "ffn0":
                        emit_ffn(S, c, l, 0)
                    elif stg == "ffn1":
                        emit_ffn(S, c, l, 1)
                    elif stg == "mix":
                        if l % 2 == 1:
                            emit_hgrn(S, c, l, l // 2)
                        else:
                            emit_mamba(S, c, l, l // 2)
                    elif stg == "mamba":
                        emit_mamba(S, c, l, l // 2)
                    elif stg == "hgrn":
                        emit_hgrn(S, c, l, l // 2)
            emit_final(S, c, tile)
        c.sbuf_left = nc.sbuf_bytes_remaining
        S.emit()
        c.makespan = S.makespan
    nc._ctx = c
    return nc


def prep_common(inputs):
    f32 = np.float32
    A = lambda k: np.asarray(inputs[k], f32)
    g = A("ffn_w_gate").reshape(DEPTH, 2, KD, 128, KF, 128)
    u = A("ffn_w_up").reshape(DEPTH, 2, KD, 128, KF, 128)
    wgu = np.stack([g, u], axis=0)
    wgu = np.ascontiguousarray(wgu.transpose(1, 2, 5, 4, 0, 3, 6))
    wd = A("ffn_w_down").reshape(DEPTH, 2, KF, 128, KD, 128)
    wd = np.ascontiguousarray(wd.transpose(0, 1, 4, 3, 2, 5))
    nw = np.concatenate([A("norm_w").reshape(DEPTH * 3, KD, 128),
                         A("final_norm_w").reshape(1, KD, 128)], axis=0)
    nw = np.ascontiguousarray(nw.transpose(2, 0, 1).reshape(128, -1))
    out = {"wgu": wgu, "wd": wd, "nw": nw}
    hw = A("h_w_in").reshape(2, KD, 128, 4, NHD, 128)
    qf = hw[:, :, :, 0:2]
    out["hwqf"] = np.ascontiguousarray(qf.transpose(0, 4, 2, 3, 1, 5))
    gg = hw[:, :, :, 3].reshape(2, KD, 128, NHD // 2, 2, 128)
    out["hwg"] = np.ascontiguousarray(gg.transpose(0, 3, 2, 4, 1, 5))
    vv = hw[:, :, :, 2].reshape(2, KD, 128, NHD // 2, 2, 128)
    out["hwv"] = np.ascontiguousarray(vv.transpose(0, 3, 2, 4, 1, 5))
    wo = A("h_w_out").reshape(2, KD, 128, KD, 128)
    out["hwo"] = np.ascontiguousarray(wo.transpose(0, 3, 2, 1, 4))
    lb = A("h_lb_logits").reshape(DEPTH, NHD, 128)
    out["hlb"] = np.ascontiguousarray(lb.transpose(2, 0, 1).reshape(128, -1))
    out["hnw"] = np.ascontiguousarray(A("h_norm_w").T)
    mw = A("m_w_in")
    win = mw[:, :, 0:5120].reshape(2, KD, 128, 20, 2, 128)
    out["mwin"] = np.ascontiguousarray(win.transpose(0, 3, 2, 4, 1, 5))
    wdt = mw[:, :, 5120:5152].reshape(2, KD, 128, 32)
    out["mwdt"] = np.ascontiguousarray(wdt.transpose(2, 0, 1, 3))
    cw = A("m_conv_w").reshape(2, 4, 24, 128)
    cb = A("m_conv_b").reshape(2, 1, 24, 128)
    out["mconv"] = np.ascontiguousarray(np.concatenate([cw, cb], axis=1).transpose(3, 0, 2, 1))
    dcol = np.repeat(A("m_d"), 64, axis=1).reshape(2, 16, 128)
    out["mdcol"] = np.ascontiguousarray(dcol.transpose(2, 0, 1))
    out["mnw"] = np.ascontiguousarray(A("m_norm_w").reshape(2, 16, 128).transpose(2, 0, 1))
    out["mdtb"] = np.ascontiguousarray(A("m_dt_bias"))
    out["malog"] = np.ascontiguousarray(A("m_a_log"))
    mwo = A("m_w_out").reshape(2, 16, 128, KD, 128)
    out["mwo"] = np.ascontiguousarray(mwo.transpose(0, 3, 2, 1, 4))
    return out


_NC_CACHE = {}


def kernel(**inputs):
    x = np.asarray(inputs["x"], np.float32)
    B = x.shape[0]
    common = prep_common(inputs)
    if "nc" not in _NC_CACHE:
        _NC_CACHE["nc"] = build_program()
    nc = _NC_CACHE["nc"]
    in_maps = []
    for b in range(B):
        m = dict(common)
        m["xT"] = np.ascontiguousarray(x[b].T)
        in_maps.append(m)
    res = run_bass_kernel_spmd(nc, in_maps, core_ids=list(range(B)))
    out = np.stack([np.ascontiguousarray(r["outT"].T) for r in res.results], axis=0)
    return out.astype(np.float32)
```

```python
import heapq
import numpy as np
import concourse.bass as bass
import concourse.mybir as mybir
from concourse.bass_utils import run_bass_kernel_spmd

F32 = mybir.dt.float32
BF16 = mybir.dt.bfloat16
AF = mybir.ActivationFunctionType
ALU = mybir.AluOpType

ENGS = ("pe", "act", "dve", "pool", "sp")
HOP_NS = 250.0
WINDOW = 48


class Buf:
    __slots__ = ("name", "lw", "rd", "dsem", "dcnt", "dlast")

    def __init__(self, name):
        self.name = name
        self.lw = None
        self.rd = []
        self.dsem = None
        self.dcnt = 0
        self.dlast = None


class Op:
    __slots__ = ("eng", "fn", "deps", "marked", "dma", "idx", "cost", "seq", "fin", "pos", "dcount")


class Sched:
    def __init__(self, nc):
        self.nc = nc
        self.ops = {e: [] for e in ENGS}
        self.dma_bufs = []
        self.nseq = 0

    def _record(self, eng, fn, reads, writes, dma=None, cost=500.0):
        op = Op()
        op.eng = eng
        op.fn = fn
        op.marked = False
        op.dma = dma
        op.cost = cost
        op.seq = self.nseq
        self.nseq += 1
        deps = []
        for b in reads:
            if b.lw is not None:
                deps.append(b.lw)
        for b in writes:
            if b.lw is not None:
                deps.append(b.lw)
            deps.extend(b.rd)
        if dma is not None and dma.dlast is not None:
            deps.append(dma.dlast)
        seen = set()
        out = []
        for d in deps:
            if id(d[1]) in seen:
                continue
            seen.add(id(d[1]))
            out.append(d)
        op.deps = out
        self.ops[eng].append(op)
        if dma is not None:
            if dma.dsem is None:
                self.dma_bufs.append(dma)
                dma.dsem = True
            dma.dcnt += 16
            op.dcount = dma.dcnt
            tok = ("d", op)
            dma.dlast = tok
        else:
            op.dcount = 0
            tok = ("e", op)
        for b in reads:
            b.rd.append(tok)
        for b in writes:
            b.lw = tok
            b.rd = []
        return op

    DEFAULT_COST = {"pe": 150.0, "act": 700.0, "dve": 700.0, "pool": 1500.0, "sp": 100.0}

    def op(self, eng, fn, reads=(), writes=(), cost=None):
        if cost is None:
            cost = self.DEFAULT_COST[eng]
        return self._record(eng, fn, reads, writes, cost=cost)

    @staticmethod
    def fence(old_bufs, new_bufs):
        toks = {}
        for b in old_bufs:
            if b.lw is not None:
                toks[id(b.lw[1])] = b.lw
            for t in b.rd:
                toks[id(t[1])] = t
        toks = list(toks.values())
        for b in new_bufs:
            b.rd.extend(toks)

    def dma(self, eng, out, in_, reads=(), writes=(), sem=None, nbytes=65536):
        if sem is None:
            sem = (list(writes) + list(reads))[0]
        return self._record(eng, lambda e: e.dma_start(out=out, in_=in_), reads, writes, dma=sem,
                            cost=2000.0 + nbytes / 100.0)

    def _list_schedule(self):
        pend = {e: list(self.ops[e]) for e in ENGS}
        head = {e: 0 for e in ENGS}
        free = {e: 0.0 for e in ENGS}
        order = {e: [] for e in ENGS}
        for e in ENGS:
            for op in pend[e]:
                op.fin = None
        total = sum(len(v) for v in pend.values())
        done = 0
        issue = {"sp": 60.0, "pool": 700.0}
        while done < total:
            best = None
            for e in ENGS:
                lst = pend[e]
                h = head[e]
                n = len(lst)
                while h < n and lst[h] is None:
                    h += 1
                head[e] = h
                cnt = 0
                i = h
                fe = free[e]
                while i < n and cnt < WINDOW:
                    op = lst[i]
                    if op is not None:
                        cnt += 1
                        rdy = 0.0
                        ok = True
                        for d in op.deps:
                            f = d[1].fin
                            if f is None:
                                ok = False
                                break
                            if d[1].eng != e or d[0] == "d":
                                f += HOP_NS
                            if f > rdy:
                                rdy = f
                        if ok:
                            st = rdy if rdy > fe else fe
                            key = (st, op.seq)
                            if best is None or key < best[0]:
                                best = (key, e, i, op)
                            if rdy <= fe:
                                break
                    i += 1
            key, e, i, op = best
            st = key[0]
            if op.dma is not None:
                free[e] = st + issue.get(e, 100.0)
                op.fin = st + op.cost
            else:
                free[e] = st + op.cost
                op.fin = free[e]
            pend[e][i] = None
            order[e].append(op)
            done += 1
        self.makespan = max(free.values())
        return order

    def emit(self, reorder=True):
        nc = self.nc
        order = self._list_schedule() if reorder else self.ops
        for e in ENGS:
            for i, op in enumerate(order[e]):
                op.pos = i
        def needs_wait(op, d):
            if d[0] == "d":
                return True
            p = d[1]
            if p.eng == op.eng and op.dma is None and op.eng == "pe":
                return False
            return True
        for e in ENGS:
            for op in order[e]:
                for d in op.deps:
                    if d[0] == "e" and needs_wait(op, d):
                        d[1].marked = True
        count = {}
        for e in ENGS:
            cnt = 0
            for op in order[e]:
                if op.marked:
                    cnt += 1
                count[id(op)] = cnt
        from contextlib import ExitStack
        with ExitStack() as st:
            esem = {e: st.enter_context(nc.semaphore("sem_" + e)) for e in ENGS}
            for i, b in enumerate(self.dma_bufs):
                b.dsem = st.enter_context(nc.semaphore("dsem%d" % i))
            block = st.enter_context(nc.Block())

            def run(e, eng):
                seen = {}
                for op in order[e]:
                    for d in op.deps:
                        if not needs_wait(op, d):
                            continue
                        p = d[1]
                        if d[0] == "e":
                            sem = esem[p.eng]
                            val = count[id(p)]
                        else:
                            sem = p.dma.dsem
                            val = p.dcount
                        k = id(sem)
                        if seen.get(k, 0) >= val:
                            continue
                        seen[k] = val
                        eng.wait_ge(sem, val)
                    ins = op.fn(eng)
                    if op.dma is not None:
                        ins.then_inc(op.dma.dsem, 16)
                    elif op.marked:
                        ins.then_inc(esem[e], 1)

            final = [(b.dsem, b.dcnt) for b in self.dma_bufs]

            @block.tensor
            def _(eng):
                run("pe", eng)

            @block.scalar
            def _(eng):
                run("act", eng)

            @block.vector
            def _(eng):
                run("dve", eng)

            @block.gpsimd
            def _(eng):
                run("pool", eng)

            @block.sync
            def _(eng):
                run("sp", eng)
                for sem, cnt in final:
                    eng.wait_ge(sem, cnt)


D = 1024
KD = D // 128
DFF = 2816
KF = DFF // 128
SEQ = 4096
DEPTH = 4
EPS = 1e-6
T = 1024
NH = T // 512
TM = 512
NHD = 8
HC = 32


class Ring:
    def __init__(self, name, aps):
        self.aps = aps
        self.bufs = [Buf("%s%d" % (name, i)) for i in range(len(aps))]
        self.i = 0

    def next(self):
        i = self.i
        self.i = (i + 1) % len(self.aps)
        return self.aps[i], self.bufs[i]


class Ctx:
    pass


def _n(ap):
    n = 1
    for s in ap.shape[1:]:
        n *= int(s)
    return n


def _mm(S, out, lhsT, rhs, start, stop, reads, writes):
    n = _n(rhs)
    cost = 25.0 + max(n, 64) / 2.4 * (4.0 if lhsT.dtype == F32 else 1.0)
    S.op("pe", lambda e: e.matmul(out, lhsT=lhsT, rhs=rhs, start=start, stop=stop),
         reads=reads, writes=writes, cost=cost)


def _act(S, out, in_, func, reads, writes, **kw):
    S.op("act", lambda e: e.activation(out=out, in_=in_, func=func, **kw), reads=reads, writes=writes,
         cost=230.0 + 0.95 * _n(out))


def _tt(S, eng, out, in0, in1, op, reads, writes):
    n = _n(out)
    cost = (120.0 + 1.4 * n) if eng == "dve" else (350.0 + 2.2 * n)
    S.op(eng, lambda e: e.tensor_tensor(out=out, in0=in0, in1=in1, op=op), reads=reads, writes=writes, cost=cost)


def emit_rmsnorm(S, c, wcol):
    banks = [c.bank() for _ in range(NH)]
    for k in range(KD):
        sq, bsq = c.sq_ring.next()
        _act(S, sq, c.xT[:, k, :], AF.Square, [c.BxT[k]], [bsq])
        for h in range(NH):
            ps, bps = banks[h]
            _mm(S, ps, c.ones_bf[:, :], sq[:, h * 512:(h + 1) * 512], k == 0, k == KD - 1,
                [bsq, c.Bconst], [bps])
    for h in range(NH):
        ps, bps = banks[h]
        sl = slice(h * 512, (h + 1) * 512)
        _act(S, c.rstd[:, sl], ps, AF.Ln, [bps, c.Bconst], [c.Brstd[h]], scale=1.0 / D, bias=c.eps_col[:, 0:1])
        _act(S, c.rstd[:, sl], c.rstd[:, sl], AF.Exp, [c.Brstd[h]], [c.Brstd[h]], scale=-0.5)
    for k in range(KD):
        for h in range(NH):
            sl = slice(h * 512, (h + 1) * 512)
            S.op("dve", lambda e, k=k, sl=sl: e.scalar_tensor_tensor(
                out=c.uT[:, k, sl], in0=c.xT[:, k, sl], scalar=c.nw[:, wcol + k:wcol + k + 1],
                in1=c.rstd[:, sl], op0=ALU.mult, op1=ALU.mult),
                reads=[c.BxT[k], c.Brstd[h], c.Bconst], writes=[c.BuT[k][h]])


def enter_phase(S, c, name):
    new = c.phase_bufs[name]
    if c.cur_phase is not None and c.cur_phase != name:
        S.fence(c.phase_bufs[c.cur_phase], new)
    c.cur_phase = name
    if name == "ffn":
        c.bank = c.bank8.next
    else:
        c.bank = c.bank4.next


def emit_ffn(S, c, l, j):
    enter_phase(S, c, "ffn")
    emit_rmsnorm(S, c, (l * 3 + (0 if j == 0 else 2)) * KD)
    for fc in range(KF):
        w, bw = c.gu_ring.next()
        S.dma("pool", w, c.d_wgu[l, j, fc], writes=[bw])
        bk = [[c.bank() for _ in range(NH)] for _ in range(2)]
        for k in range(KD):
            for g in range(2):
                for h in range(NH):
                    ps, bps = bk[g][h]
                    _mm(S, ps, w[:, g, k, :], c.uT[:, k, h * 512:(h + 1) * 512], k == 0, k == KD - 1,
                        [bw, c.BuT[k][h]], [bps])
        for h in range(NH):
            sl = slice(h * 512, (h + 1) * 512)
            sg, bsg = c.sg_ring.next()
            pg, bpg = bk[0][h]
            pu, bpu = bk[1][h]
            _act(S, sg, pg, AF.Silu, [bpg], [bsg])
            _tt(S, "dve", c.actT[:, fc, sl], pu, sg, ALU.mult, [bpu, bsg], [c.Bact[fc][h]])
    for dc in range(KD):
        w, bw = c.dn_ring.next()
        S.dma("pool", w, c.d_wd[l, j, dc], writes=[bw])
        bk = [c.bank() for _ in range(NH)]
        for k in range(KF):
            for h in range(NH):
                ps, bps = bk[h]
                _mm(S, ps, w[:, k, :], c.actT[:, k, h * 512:(h + 1) * 512], k == 0, k == KF - 1,
                    [bw, c.Bact[k][h]], [bps])
        for h in range(NH):
            sl = slice(h * 512, (h + 1) * 512)
            ps, bps = bk[h]
            S.op("dve", lambda e, ps=ps, dc=dc, sl=sl: e.scalar_tensor_tensor(
                out=c.xT[:, dc, sl], in0=ps, scalar=0.5, in1=c.xT[:, dc, sl],
                op0=ALU.mult, op1=ALU.add),
                reads=[bps, c.BxT[dc]], writes=[c.BxT[dc]])


def emit_final(S, c, tile):
    enter_phase(S, c, "ffn")
    wc = DEPTH * 3 * KD
    emit_rmsnorm(S, c, wc)
    for k in range(KD):
        for h in range(NH):
            sl = slice(h * 512, (h + 1) * 512)
            S.op("dve", lambda e, k=k, sl=sl: e.scalar_tensor_tensor(
                out=c.xT[:, k, sl], in0=c.xT[:, k, sl], scalar=c.nw[:, wc + k:wc + k + 1],
                in1=c.rstd[:, sl], op0=ALU.mult, op1=ALU.mult),
                reads=[c.BxT[k], c.Brstd[h], c.Bconst], writes=[c.BxT[k]])
        S.dma("sp", c.d_out[k * 128:(k + 1) * 128, tile * T:(tile + 1) * T], c.xT[:, k, :],
              reads=[c.BxT[k]])


def emit_hgrn(S, c, l, jm):
    enter_phase(S, c, "hgrn")
    emit_rmsnorm(S, c, (l * 3 + 1) * KD)
    h = c.h
    NB = TM // 128
    NCH = TM // HC
    for sub in range(T // TM):
        t0 = sub * TM
        tsl = slice(t0, t0 + TM)
        hh = sub % NH if TM == 512 else None
        rdu = lambda k: [c.BuT[k][(t0 // 512)]]
        for vp in range(NHD // 2):
            w, bw = c.gu_ring.next()
            S.dma("pool", w, c.d_hwv[jm, vp], writes=[bw])
            for b in range(NB):
                pv, bpv = c.bank()
                for i in range(2):
                    for k in range(KD):
                        _mm(S, pv[:, i * 128:(i + 1) * 128], c.uT[:, k, t0 + b * 128:t0 + (b + 1) * 128],
                            w[:, i, k, :], k == 0, k == KD - 1, rdu(k) + [bw], [bpv])
                S.op("act", lambda e, pv=pv, b=b, vp=vp: e.copy(out=h.vall[:, b, vp * 256:(vp + 1) * 256], in_=pv[:, 0:256]),
                     reads=[bpv], writes=[h.Bvall[b]])
        for hd in range(NHD):
            w, bw = c.gu_ring.next()
            S.dma("pool", w, c.d_hwqf[jm, hd], writes=[bw])
            pq, bpq = c.bank()
            pf, bpf = c.bank()
            for k in range(KD):
                _mm(S, pq, w[:, 0, k, :], c.uT[:, k, tsl], k == 0, k == KD - 1, [bw] + rdu(k), [bpq])
            for k in range(KD):
                _mm(S, pf, w[:, 1, k, :], c.uT[:, k, tsl], k == 0, k == KD - 1, [bw] + rdu(k), [bpf])
            qs, bqs = h.tmp.next()
            sg, bsg = h.tmp.next()
            lf, blf = h.tmp.next()
            kk, bkk = h.tmp.next()
            cs, bcs = h.tmp.next()
            eq, beq = h.tmp.next()
            ek, bek = h.tmp.next()
            _act(S, qs, pq, AF.Silu, [bpq], [bqs])
            _act(S, sg, pf, AF.Sigmoid, [bpf], [bsg])
            cb = (jm * 3) * NHD + hd
            _act(S, lf, sg, AF.Ln, [bsg, c.Bconst], [blf],
                 scale=h.hconst[:, cb + NHD:cb + NHD + 1], bias=h.hconst[:, cb:cb + 1])
            S.op("dve", lambda e, kk=kk, sg=sg, cb=cb: e.tensor_scalar(
                out=kk, in0=sg, scalar1=h.hconst[:, cb + 2 * NHD:cb + 2 * NHD + 1],
                scalar2=h.hconst[:, cb + NHD:cb + NHD + 1], op0=ALU.mult, op1=ALU.add),
                reads=[bsg, c.Bconst], writes=[bkk])
            S.op("dve", lambda e, cs=cs, lf=lf: e.tensor_tensor_scan(
                out=cs, data0=h.mask32[:, :], data1=lf, initial=0.0, op0=ALU.mult, op1=ALU.add),
                reads=[blf, c.Bconst], writes=[bcs])
            _act(S, eq, cs, AF.Exp, [bcs], [beq])
            _act(S, ek, cs, AF.Exp, [bcs], [bek], scale=-1.0)
            _tt(S, "dve", h.qT[:, hd, :], qs, eq, ALU.mult, [bqs, beq], [h.BqT[hd]])
            _tt(S, "dve", kk, kk, ek, ALU.mult, [bkk, bek], [bkk])
            S.op("pool", lambda e, kk=kk, hd=hd: e.tensor_copy(out=h.kT[:, hd, :], in_=kk),
                 reads=[bkk], writes=[h.BkT[hd]])
            eq3 = eq.rearrange("p (c j) -> p c j", j=HC)
            kd, bkd = h.kd_ring.next()
            _tt(S, "pool", kd.rearrange("p (c j) -> p c j", j=HC), kk.rearrange("p (c j) -> p c j", j=HC),
                eq3[:, :, HC - 1:HC].broadcast_to([128, NCH, HC]), ALU.mult, [bkk, beq], [bkd])
            S.op("pool", lambda e, eq3=eq3, hd=hd: e.tensor_copy(
                out=h.elast[:, hd, :].rearrange("p (c o) -> p c o", o=1), in_=eq3[:, :, HC - 1:HC]),
                reads=[beq], writes=[h.Bel[hd]])
            for b in range(NB):
                pt, bpt = c.qbank()
                ptb = pt.bitcast(BF16)[:, 0:128]
                S.op("pe", lambda e, ptb=ptb, kd=kd, b=b: e.transpose(ptb, kd[:, b * 128:(b + 1) * 128], c.ident[:, :]),
                     reads=[bkd, c.Bconst], writes=[bpt])
                S.op("act", lambda e, ptb=ptb, b=b, hd=hd: e.copy(out=h.kdtm[:, b, hd, :], in_=ptb),
                     reads=[bpt], writes=[h.Bkdtm[b][hd]])
        for gp in range(NHD // 2):
            w, bw = c.gu_ring.next()
            S.dma("pool", w, c.d_hwg[jm, gp], writes=[bw])
            for i in range(2):
                hd = gp * 2 + i
                pg, bpg = c.bank()
                for k in range(KD):
                    _mm(S, pg, w[:, i, k, :], c.uT[:, k, tsl], k == 0, k == KD - 1, [bw] + rdu(k), [bpg])
                _act(S, h.sgT[:, hd, :], pg, AF.Silu, [bpg], [h.BsgT[hd]])
        for b in range(NB):
            tok = slice(t0 + b * 128, t0 + (b + 1) * 128)
            bsl = slice(b * 128, (b + 1) * 128)
            for cc in range(4):
                for half in range(2):
                    hs = slice(half * 512, (half + 1) * 512)
                    eng = "dve" if (cc + half) % 2 == 0 else "act"
                    if eng == "dve":
                        S.op("dve", lambda e, hs=hs, cc=cc, b=b: e.tensor_scalar(
                            out=h.vm[:, cc, hs], in0=h.vall[:, b, hs], scalar1=h.cmask[:, cc:cc + 1], scalar2=None,
                            op0=ALU.mult), reads=[h.Bvall[b], c.Bconst], writes=[h.Bvm[cc][half]])
                    else:
                        _act(S, h.vm[:, cc, hs], h.vall[:, b, hs], AF.Copy, [h.Bvall[b], c.Bconst], [h.Bvm[cc][half]],
                             scale=h.cmask[:, cc:cc + 1])
            for hb in range(2):
                pa, bpa = c.bank()
                for i in range(4):
                    hd = hb * 4 + i
                    _mm(S, pa[:, i * 128:(i + 1) * 128], h.kT[:, hd, bsl], h.qT[:, hd, bsl], True, True,
                        [h.BkT[hd], h.BqT[hd]], [bpa])
                _tt(S, "dve", h.attn[:, hb * 4:(hb + 1) * 4, :], pa.rearrange("p (i l) -> p i l", i=4),
                    h.maskbd[:, :].unsqueeze(1).broadcast_to([128, 4, 128]), ALU.mult,
                    [bpa, c.Bconst], [h.Battn[hb]])
            pos = []
            for hb in range(2):
                po, bpo = c.bank()
                pos.append((po, bpo))
                for i in range(4):
                    hd = hb * 4 + i
                    _mm(S, po[:, i * 128:(i + 1) * 128], h.vall[:, b, hd * 128:(hd + 1) * 128], h.attn[:, hd, :],
                        True, True, [h.Bvall[b], h.Battn[hb]], [bpo])
            pis = [c.bank() for _ in range(2)]
            for cc in range(4):
                ch = (t0 + b * 128) // HC % (T // HC)
                chunk_in_sub = b * 4 + cc
                for hd in range(NHD):
                    par = h.par[jm][hd]
                    pi, bpi = pis[hd // 4]
                    i = hd % 4
                    _mm(S, pi[:, i * 128 + cc * HC:i * 128 + (cc + 1) * HC], h.sbf[par][:, jm, hd, :],
                        h.qT[:, hd, b * 128 + cc * HC:b * 128 + (cc + 1) * HC], True, True,
                        [h.Bsbf[par][jm][hd], h.BqT[hd]], [bpi])
                    pu, bpu = c.qbank()
                    _mm(S, pu, h.kdtm[:, b, hd, :], h.vm[:, cc, hd * 128:(hd + 1) * 128], True, True,
                        [h.Bkdtm[b][hd], h.Bvm[cc][hd // 4]], [bpu])
                    S.op("dve", lambda e, pu=pu, hd=hd, chunk_in_sub=chunk_in_sub: e.scalar_tensor_tensor(
                        out=h.S[:, jm, hd, :], in0=h.S[:, jm, hd, :],
                        scalar=h.elast[:, hd, chunk_in_sub:chunk_in_sub + 1], in1=pu,
                        op0=ALU.mult, op1=ALU.add),
                        reads=[bpu, h.BS[jm][hd], h.Bel[hd]], writes=[h.BS[jm][hd]])
                    S.op("act", lambda e, hd=hd, par=par: e.copy(out=h.sbf[1 - par][:, jm, hd, :], in_=h.S[:, jm, hd, :]),
                         reads=[h.BS[jm][hd]], writes=[h.Bsbf[1 - par][jm][hd]])
                    h.par[jm][hd] = 1 - par
            for hb in range(2):
                pi, bpi = pis[hb]
                po, bpo = pos[hb]
                oi, boi = h.tmp.next()
                osum, bos = h.tmp.next()
                S.op("act", lambda e, oi=oi, pi=pi: e.copy(out=oi, in_=pi), reads=[bpi], writes=[boi])
                _tt(S, "dve", osum, po, oi, ALU.add, [bpo, boi], [bos])
                osq, bosq = c.sg_ring.next()
                _act(S, osq, osum, AF.Square, [bos], [bosq])
                pss, bpss = c.bank()
                _mm(S, pss, c.ones_bf[:, :], osq, True, True, [bosq, c.Bconst], [bpss])
                rs, brs = h.tmp.next()
                _act(S, rs, pss, AF.Ln, [bpss, c.Bconst], [brs], scale=1.0 / 128, bias=c.eps_col[:, 0:1])
                _act(S, rs, rs, AF.Exp, [brs], [brs], scale=-0.5)
                _tt(S, "dve", osum, osum, rs, ALU.mult, [bos, brs], [bos])
                S.op("dve", lambda e, osum=osum, hb=hb, bsl=bsl: e.scalar_tensor_tensor(
                    out=h.ogT[:, hb * 4:(hb + 1) * 4, bsl], in0=osum.rearrange("p (i l) -> p i l", i=4),
                    scalar=h.hnw[:, jm:jm + 1], in1=h.sgT[:, hb * 4:(hb + 1) * 4, bsl],
                    op0=ALU.mult, op1=ALU.mult),
                    reads=[bos, c.Bconst] + [h.BsgT[hb * 4 + i] for i in range(4)],
                    writes=[h.BsgT[hb * 4 + i] for i in range(4)])
        for dc in range(KD):
            w, bw = c.dn_ring.next()
            S.dma("pool", w[:, 0:KD, :], c.d_hwo[jm, dc], writes=[bw])
            ps, bps = c.bank()
            for k in range(KD):
                _mm(S, ps, w[:, k, :], h.ogT[:, k, :], k == 0, k == KD - 1,
                    [bw, h.BsgT[k]], [bps])
            _tt(S, "dve", c.xT[:, dc, tsl], ps, c.xT[:, dc, tsl], ALU.add, [bps, c.BxT[dc]], [c.BxT[dc]])


def emit_hgrn_consts(S, c):
    h = c.h
    ex = h.lbtmp
    S.dma("sp", ex[:, 0:4 * NHD], c.d_hlb, writes=[c.Bconst])
    _act(S, ex[:, 0:4 * NHD], ex[:, 0:4 * NHD], AF.Exp, [c.Bconst], [c.Bconst])
    den = ex[:, 4 * NHD:5 * NHD]
    e = lambda i: ex[:, i * NHD:(i + 1) * NHD]
    _tt(S, "dve", den, e(0), e(1), ALU.add, [c.Bconst], [c.Bconst])
    _tt(S, "dve", den, den, e(2), ALU.add, [c.Bconst], [c.Bconst])
    _tt(S, "dve", den, den, e(3), ALU.add, [c.Bconst], [c.Bconst])
    rden = ex[:, 5 * NHD:6 * NHD]
    S.op("dve", lambda e_: e_.reciprocal(out=rden, in_=den), reads=[c.Bconst], writes=[c.Bconst])
    s123 = ex[:, 6 * NHD:7 * NHD]
    _tt(S, "dve", s123, e(1), e(2), ALU.add, [c.Bconst], [c.Bconst])
    _tt(S, "dve", s123, s123, e(3), ALU.add, [c.Bconst], [c.Bconst])
    for jm, num in ((0, e(1)), (1, s123)):
        lb = h.hconst[:, (jm * 3) * NHD:(jm * 3 + 1) * NHD]
        oml = h.hconst[:, (jm * 3 + 1) * NHD:(jm * 3 + 2) * NHD]
        noml = h.hconst[:, (jm * 3 + 2) * NHD:(jm * 3 + 3) * NHD]
        _tt(S, "dve", lb, num, rden, ALU.mult, [c.Bconst], [c.Bconst])
        S.op("dve", lambda e_, lb=lb, oml=oml: e_.tensor_scalar(out=oml, in0=lb, scalar1=-1.0, scalar2=1.0,
                                                                 op0=ALU.mult, op1=ALU.add),
             reads=[c.Bconst], writes=[c.Bconst])
        S.op("dve", lambda e_, noml=noml, oml=oml: e_.tensor_scalar(out=noml, in0=oml, scalar1=-1.0, scalar2=None,
                                                                    op0=ALU.mult),
             reads=[c.Bconst], writes=[c.Bconst])
    S.op("pool", lambda e_: e_.memset(h.mask32[:, :], 1.0), writes=[c.Bconst])
    S.op("pool", lambda e_: e_.memset(h.mask32[:, :].rearrange("p (c j) -> p c j", j=HC)[:, :, 0:1], 0.0),
         reads=[c.Bconst], writes=[c.Bconst])
    S.op("pool", lambda e_: e_.memset(h.cmask[:, :], 1.0), writes=[c.Bconst])
    for cc in range(4):
        S.op("pool", lambda e_, cc=cc: e_.affine_select(
            out=h.cmask[:, cc:cc + 1], in_=h.cmask[:, cc:cc + 1], pattern=[[0, 1]], compare_op=ALU.is_ge,
            fill=0.0, base=-32 * cc, channel_multiplier=1), reads=[c.Bconst], writes=[c.Bconst])
        S.op("pool", lambda e_, cc=cc: e_.affine_select(
            out=h.cmask[:, cc:cc + 1], in_=h.cmask[:, cc:cc + 1], pattern=[[0, 1]], compare_op=ALU.is_ge,
            fill=0.0, base=32 * cc + 31, channel_multiplier=-1), reads=[c.Bconst], writes=[c.Bconst])
    S.op("pool", lambda e_: e_.memset(h.maskbd[:, :], 1.0), writes=[c.Bconst])
    S.op("pool", lambda e_: e_.affine_select(out=h.maskbd[:, :], in_=h.maskbd[:, :], pattern=[[1, 128]],
                                             compare_op=ALU.is_ge, fill=0.0, base=0, channel_multiplier=-1),
         reads=[c.Bconst], writes=[c.Bconst])
    for cc in range(4):
        S.op("dve", lambda e_, cc=cc: e_.tensor_scalar(
            out=h.maskbd[:, cc * 32:(cc + 1) * 32], in0=h.maskbd[:, cc * 32:(cc + 1) * 32],
            scalar1=h.cmask[:, cc:cc + 1], scalar2=None, op0=ALU.mult), reads=[c.Bconst], writes=[c.Bconst])
    S.op("pool", lambda e_: e_.memset(c.ident[:, :], 0.0), writes=[c.Bconst])
    S.op("pool", lambda e_: e_.affine_select(out=c.ident[:, :], in_=c.ident[:, :], pattern=[[-1, 128]],
                                             compare_op=ALU.not_equal, fill=1.0, base=0, channel_multiplier=1),
         reads=[c.Bconst], writes=[c.Bconst])
    S.op("pool", lambda e_: e_.memset(h.S[:], 0.0), writes=[b for r in h.BS for b in r])
    for par in range(2):
        S.op("pool", lambda e_, par=par: e_.memset(h.sbf[par][:], 0.0), writes=[b for r in h.Bsbf[par] for b in r])
    S.dma("sp", h.hnw[:, :], c.d_hnw, writes=[c.Bconst])


MT_ = 256
MG = 4
MHG = 8


def _pad_ap(ap2):
    return bass.AP(ap2.tensor, ap2.offset, [[ap2.ap[0][0], 128], [192, 2], [1, 64]])


def emit_mamba(S, c, l, jm):
    enter_phase(S, c, "mamba")
    emit_rmsnorm(S, c, (l * 3 + 1) * KD)
    m = c.m
    NCK = MT_ // 128
    for xpg_, bx_ in zip(m.xpad_ring.aps, m.xpad_ring.bufs):
        S.op("pool", lambda e, xpg_=xpg_: e.memset(xpg_, 0.0), writes=[bx_])
    S.op("pool", lambda e: e.memset(m.ppad[:], 0.0), writes=m.Bppad)
    for g in range(MG):
        for i in range(MHG // 2):
            ci = g * 4 + i
            S.op("act", lambda e, ci=ci: e.copy(out=_pad_ap(m.ppad[:, 2 * ci:2 * ci + 2, :].rearrange("p a b -> p (a b)")),
                                                  in_=m.prev[:, jm, ci * 128:(ci + 1) * 128].rearrange("p (a b) -> p a b", a=2)),
                 reads=[m.Bprev[jm][g]], writes=[m.Bppad[g]])
    for sub in range(T // MT_):
        t0 = sub * MT_
        tsl = slice(t0, t0 + MT_)
        rdu = lambda k: [c.BuT[k][t0 // 512]]
        for ck in range(NCK):
            tok = slice(t0 + ck * 128, t0 + (ck + 1) * 128)
            pd, bpd = c.qbank()
            for k in range(KD):
                _mm(S, pd[:, 0:32], c.uT[:, k, tok], m.wdt[:, jm, k, :], k == 0, k == KD - 1, rdu(k) + [m.Bwdt], [bpd])
            dtc, dAc, ddc = m.dt[:, ck, :], m.dA[:, ck, :], m.dtdec[:, ck, :]
            _tt(S, "dve", dtc, pd[:, 0:32], m.dtb[:, jm, :], ALU.add, [bpd, c.Bconst], [m.Bdt[ck]])
            _act(S, dtc, dtc, AF.Exp, [m.Bdt[ck]], [m.Bdt[ck]])
            _act(S, dtc, dtc, AF.Ln, [m.Bdt[ck]], [m.Bdt[ck]], bias=1.0)
            _tt(S, "dve", dAc, dtc, m.Aneg[:, jm, :], ALU.mult, [m.Bdt[ck], c.Bconst], [m.BdA[ck]])
            pr_, bpr = c.qbank()
            _mm(S, pr_[:, 0:32], m.Umat[:, :], dAc, True, True, [m.BdA[ck], c.Bconst], [bpr])
            _act(S, ddc, pr_[:, 0:32], AF.Exp, [bpr], [m.Bdtdec[ck]])
            _tt(S, "dve", ddc, ddc, dtc, ALU.mult, [m.Bdtdec[ck], m.Bdt[ck]], [m.Bdtdec[ck]])
        for pr in range(20):
            w, bw = c.gu_ring.next()
            S.dma("pool", w, c.d_mwin[jm, pr], writes=[bw])
            for i in range(2):
                ch = pr * 2 + i
                ps, bps = c.bank()
                for k in range(KD):
                    _mm(S, ps[:, 0:MT_], w[:, i, k, :], c.uT[:, k, tsl], k == 0, k == KD - 1, [bw] + rdu(k), [bps])
                if ch < 16:
                    _act(S, m.siluz[:, ch, :], ps[:, 0:MT_], AF.Silu, [bps], [m.Bsz[ch]])
                    continue
                ci = ch - 16
                raw, braw = m.raw_ring.next()
                acc, bacc = m.acc_ring.next()
                S.op("pool", lambda e, raw=raw, ci=ci: e.tensor_copy(out=raw[:, 0:3], in_=m.tails[:, jm, ci, :]),
                     reads=[m.Btail[jm][ci]], writes=[braw])
                S.op("act", lambda e, raw=raw, ps=ps: e.copy(out=raw[:, 3:3 + MT_], in_=ps[:, 0:MT_]),
                     reads=[bps], writes=[braw])
                S.op("pool", lambda e, raw=raw, ci=ci: e.tensor_copy(out=m.tails[:, jm, ci, :], in_=raw[:, MT_:MT_ + 3]),
                     reads=[braw], writes=[m.Btail[jm][ci]])
                cw = lambda tap, ci=ci: m.mconv[:, jm, ci, tap:tap + 1]
                S.op("dve", lambda e, raw=raw, acc=acc, cw=cw: e.tensor_scalar(
                    out=acc, in0=raw[:, 0:MT_], scalar1=cw(0), scalar2=None, op0=ALU.mult),
                    reads=[braw, c.Bconst], writes=[bacc])
                for tap in range(1, 4):
                    S.op("dve", lambda e, raw=raw, acc=acc, cw=cw, tap=tap: e.scalar_tensor_tensor(
                        out=acc, in0=raw[:, tap:tap + MT_], scalar=cw(tap), in1=acc, op0=ALU.mult, op1=ALU.add),
                        reads=[braw, bacc, c.Bconst], writes=[bacc])
                if ci < 16:
                    dst, bdst = m.xc[:, ci, :], m.Bxc[ci]
                elif ci < 20:
                    dst, bdst = m.BT[:, ci - 16, :], m.BBT[ci - 16]
                else:
                    dst, bdst = m.CT[:, ci - 20, :], m.BCT[ci - 20]
                _act(S, dst, acc, AF.Silu, [bacc, c.Bconst], [bdst], bias=cw(4))
        for ck in range(NCK):
            csl = slice(ck * 128, (ck + 1) * 128)
            for g in range(MG):
                hs = slice(g * MHG, (g + 1) * MHG)
                pt, bpt = c.bank()
                ptb = pt.bitcast(BF16)[:, 0:512]
                for i in range(4):
                    ci = g * 4 + i
                    S.op("pe", lambda e, ptb=ptb, ci=ci, i=i, csl=csl: e.transpose(
                        ptb[:, i * 128:(i + 1) * 128], m.xc[:, ci, csl], c.ident[:, :]),
                        reads=[m.Bxc[ci], c.Bconst], writes=[bpt], cost=120.0)
                pt4 = ptb.rearrange("p (c a b) -> p c a b", c=4, a=2)
                xpg, bxpg = m.xpad_ring.next()
                xp = xpg.rearrange("p a b -> p (a b)")
                xp4 = bass.AP(xp.tensor, xp.offset, [[xp.ap[0][0], 128], [256, 4], [192, 2], [1, 64]])
                _tt(S, "dve", xp4, pt4,
                    m.dt[:, ck, hs].rearrange("p (c a) -> p c a", a=2).unsqueeze(3).broadcast_to([128, 4, 2, 64]),
                    ALU.mult, [bpt, m.Bdt[ck]], [bxpg])
                xddg, bxddg = m.xdd_ring.next()
                _tt(S, "dve", xddg.rearrange("p (c a b) -> p c a b", c=4, a=2), pt4,
                    m.dtdec[:, ck, hs].rearrange("p (c a) -> p c a", a=2).unsqueeze(3).broadcast_to([128, 4, 2, 64]),
                    ALU.mult, [bpt, m.Bdtdec[ck]], [bxddg])
                ptB, bptB = c.qbank()
                ptBb = ptB.bitcast(BF16)[:, 0:128]
                S.op("pe", lambda e, ptBb=ptBb, g=g, csl=csl: e.transpose(ptBb, m.BT[:, g, csl], c.ident[:, :]),
                     reads=[m.BBT[g], c.Bconst], writes=[bptB], cost=120.0)
                btmg, bbtmg = m.btm_ring.next()
                S.op("act", lambda e, ptBb=ptBb, btmg=btmg: e.copy(out=btmg, in_=ptBb), reads=[bptB], writes=[bbtmg],
                     cost=350.0)
                R, bR = m.R_ring.next()
                _tt(S, "pool", R, m.Tmat[:, :].unsqueeze(1).broadcast_to([128, MHG, 128]),
                    m.dA[:, ck, hs].unsqueeze(2).broadcast_to([128, MHG, 128]), ALU.mult, [m.BdA[ck], c.Bconst], [bR])
                R2 = R.rearrange("p h l -> p (h l)")
                LT, bLT = m.LT_ring.next()
                ecs, becs = m.ecs_ring.next()
                for half in range(2):
                    hsl = slice(half * 512, (half + 1) * 512)
                    p1, bp1 = c.bank()
                    _mm(S, p1, m.Umat[:, :], R2[:, hsl], True, True, [bR, c.Bconst], [bp1])
                    _act(S, LT.rearrange("p h l -> p (h l)")[:, hsl], p1, AF.Exp, [bp1], [bLT])
                    p2, bp2 = c.bank()
                    _mm(S, p2, m.onesf[:, :], R2[:, hsl], True, True, [bR, c.Bconst], [bp2])
                    _act(S, ecs.rearrange("p h l -> p (h l)")[:, hsl], p2, AF.Exp, [bp2], [becs])
                pg, bpg = c.qbank()
                _mm(S, pg, m.BT[:, g, csl], m.CT[:, g, csl], True, True, [m.BBT[g], m.BCT[g]], [bpg])
                Gm, bGm = m.Gm_ring.next()
                _tt(S, "dve", Gm, pg, m.maskc[:, :], ALU.mult, [bpg, c.Bconst], [bGm])
                MTt, bMT = m.MT_ring.next()
                _tt(S, "pool", MTt, LT, Gm.unsqueeze(1).broadcast_to([128, MHG, 128]), ALU.mult, [bLT, bGm], [bMT])
                Cp, bCp = m.Cp_ring.next()
                _tt(S, "pool", Cp, ecs, m.CT[:, g, csl].unsqueeze(1).broadcast_to([128, MHG, 128]), ALU.mult,
                    [becs, m.BCT[g]], [bCp])
                py, bpy = c.bank()
                for i in range(4):
                    ci = g * 4 + i
                    o = py[:, i * 128:(i + 1) * 128]
                    for a in range(2):
                        _mm(S, o, xpg[:, 2 * i + a, :], MTt[:, 2 * i + a, :], a == 0, False, [bxpg, bMT], [bpy])
                    for a in range(2):
                        _mm(S, o, m.ppad[:, 2 * ci + a, :], Cp[:, 2 * i + a, :], False, a == 1, [m.Bppad[g], bCp], [bpy])
                pu, bpu = c.bank()
                _mm(S, pu, btmg, xddg, True, True, [bbtmg, bxddg], [bpu])
                pv = m.prev[:, jm, g * 512:(g + 1) * 512]
                _tt(S, "dve", pv.rearrange("p (h q) -> p h q", h=MHG), pv.rearrange("p (h q) -> p h q", h=MHG),
                    ecs[:, :, 127:128].broadcast_to([128, MHG, 64]), ALU.mult, [m.Bprev[jm][g], becs], [m.Bprev[jm][g]])
                _tt(S, "dve", pv, pv, pu, ALU.add, [m.Bprev[jm][g], bpu], [m.Bprev[jm][g]])
                for i in range(4):
                    ci = g * 4 + i
                    S.op("act", lambda e, ci=ci: e.copy(
                        out=_pad_ap(m.ppad[:, 2 * ci:2 * ci + 2, :].rearrange("p a b -> p (a b)")),
                        in_=m.prev[:, jm, ci * 128:(ci + 1) * 128].rearrange("p (a b) -> p a b", a=2)),
                        reads=[m.Bprev[jm][g]], writes=[m.Bppad[g]])
                yg, byg = m.yg_ring.next()
                for i in range(4):
                    ci = g * 4 + i
                    S.op("dve", lambda e, i=i, ci=ci, yg=yg, py=py, csl=csl: e.scalar_tensor_tensor(
                        out=yg[:, i, :], in0=m.xc[:, ci, csl], scalar=m.dcol[:, jm, ci:ci + 1],
                        in1=py[:, i * 128:(i + 1) * 128], op0=ALU.mult, op1=ALU.add),
                        reads=[m.Bxc[ci], bpy, c.Bconst], writes=[byg])
                _tt(S, "dve", yg, yg, m.siluz[:, g * 4:(g + 1) * 4, csl], ALU.mult,
                    [byg] + [m.Bsz[g * 4 + i] for i in range(4)], [byg])
                ysq, bysq = c.sg_ring.next()
                _act(S, ysq, yg.rearrange("p i l -> p (i l)"), AF.Square, [byg], [bysq])
                pn, bpn = c.qbank()
                for i in range(4):
                    _mm(S, pn, c.ones_bf[:, :], ysq[:, i * 128:(i + 1) * 128], i == 0, i == 3, [bysq, c.Bconst], [bpn])
                rs, brs = m.rs_ring.next()
                _act(S, rs, pn, AF.Ln, [bpn, c.Bconst], [brs], scale=1.0 / 512, bias=c.eps_col[:, 0:1])
                _act(S, rs, rs, AF.Exp, [brs], [brs], scale=-0.5)
                for i in range(4):
                    ci = g * 4 + i
                    S.op("dve", lambda e, i=i, ci=ci, yg=yg, rs=rs, csl=csl: e.scalar_tensor_tensor(
                        out=m.ynT[:, ci, csl], in0=yg[:, i, :], scalar=m.mnw[:, jm, ci:ci + 1], in1=rs,
                        op0=ALU.mult, op1=ALU.mult),
                        reads=[byg, brs, c.Bconst], writes=[m.Byn[ci]])
        for dc in range(KD):
            w, bw = c.dn_ring.next()
            S.dma("pool", w[:, 0:16, :], c.d_mwo[jm, dc], writes=[bw])
            ps, bps = c.bank()
            for k in range(16):
                _mm(S, ps[:, 0:MT_], w[:, k, :], m.ynT[:, k, :], k == 0, k == 15, [bw, m.Byn[k]], [bps])
            _tt(S, "dve", c.xT[:, dc, tsl], ps[:, 0:MT_], c.xT[:, dc, tsl], ALU.add, [bps, c.BxT[dc]], [c.BxT[dc]])


def emit_mamba_consts(S, c):
    m = c.m
    B = [c.Bconst]
    S.dma("sp", m.mconv[:], c.d_mconv, writes=B)
    S.dma("sp", m.dcol[:], c.d_mdcol, writes=B)
    S.dma("sp", m.mnw[:], c.d_mnw, writes=B)
    S.dma("pool", m.wdt[:], c.d_mwdt, writes=[m.Bwdt])
    for j in range(2):
        S.dma("sp", m.dtb[:, j, :], c.d_mdtb[j:j + 1, :].partition_broadcast(128), writes=B)
        S.dma("sp", m.Aneg[:, j, :], c.d_malog[j:j + 1, :].partition_broadcast(128), writes=B)
    _act(S, m.Aneg[:], m.Aneg[:], AF.Exp, B, B)
    S.op("dve", lambda e: e.tensor_scalar(out=m.Aneg[:], in0=m.Aneg[:], scalar1=-1.0, scalar2=None, op0=ALU.mult),
         reads=B, writes=B)
    S.op("pool", lambda e: e.memset(m.onesf[:, :], 1.0), writes=B)
    S.op("pool", lambda e: e.memset(m.Tmat[:, :], 1.0), writes=B)
    S.op("pool", lambda e: e.affine_select(out=m.Tmat[:, :], in_=m.Tmat[:, :], pattern=[[1, 128]],
                                           compare_op=ALU.is_ge, fill=0.0, base=0, channel_multiplier=-1),
         reads=B, writes=B)
    S.op("pool", lambda e: e.memset(m.Umat[:, :], 1.0), writes=B)
    S.op("pool", lambda e: e.affine_select(out=m.Umat[:, :], in_=m.Umat[:, :], pattern=[[-1, 128]],
                                           compare_op=ALU.is_gt, fill=0.0, base=0, channel_multiplier=1),
         reads=B, writes=B)
    S.op("pool", lambda e: e.memset(m.prev[:], 0.0), writes=[b for r in m.Bprev for b in r])
    S.op("pool", lambda e: e.memset(m.tails[:], 0.0), writes=[b for r in m.Btail for b in r])


def build_program(seq=SEQ, depth=DEPTH, plan=None):
    nc = bass.Bass("TRN2", target_bir_lowering=False)
    c = Ctx()
    c.nc = nc
    ntiles = seq // T
    dt = lambda name, shape: nc.dram_tensor(name, shape, F32, kind="ExternalInput").ap()
    c.d_x = dt("xT", [D, seq])
    c.d_wgu = dt("wgu", [DEPTH, 2, KF, 128, 2, KD, 128])
    c.d_wd = dt("wd", [DEPTH, 2, KD, 128, KF, 128])
    NWC = (DEPTH * 3 + 1) * KD
    c.d_nw = dt("nw", [128, NWC])
    c.d_hwqf = dt("hwqf", [2, NHD, 128, 2, KD, 128])
    c.d_hwg = dt("hwg", [2, NHD // 2, 128, 2, KD, 128])
    c.d_hwv = dt("hwv", [2, NHD // 2, 128, 2, KD, 128])
    c.d_hwo = dt("hwo", [2, KD, 128, KD, 128])
    c.d_hlb = dt("hlb", [128, 4 * NHD])
    c.d_hnw = dt("hnw", [128, 2])
    c.d_mwin = dt("mwin", [2, 20, 128, 2, KD, 128])
    c.d_mwdt = dt("mwdt", [128, 2, KD, 32])
    c.d_mconv = dt("mconv", [128, 2, 24, 5])
    c.d_mdcol = dt("mdcol", [128, 2, 16])
    c.d_mnw = dt("mnw", [128, 2, 16])
    c.d_mdtb = dt("mdtb", [2, 32])
    c.d_malog = dt("malog", [2, 32])
    c.d_mwo = dt("mwo", [2, KD, 128, 16, 128])
    c.d_out = nc.dram_tensor("outT", [D, seq], F32, kind="ExternalOutput").ap()
    from contextlib import ExitStack
    with ExitStack() as st:
        sb = lambda n, s, d: st.enter_context(nc.sbuf_tensor(n, s, d))
        c.xT = sb("xT_sb", [128, KD, T], F32)
        c.uT = sb("uT_sb", [128, KD, T], BF16)
        c.rstd = sb("rstd_sb", [128, T], F32)
        c.nw = sb("nw_sb", [128, NWC], F32)
        c.ones_bf = sb("ones_bf", [128, 128], BF16)
        c.ident = sb("ident_bf", [128, 128], BF16)
        c.eps_col = sb("eps_col", [128, 1], F32)
        gu = sb("gu_ring", [128, 4, 2, KD, 128], BF16)
        dn = sb("dn_ring", [128, 2, KF, 128], BF16)
        sq = sb("sq_ring", [128, 2, T], BF16)
        sg = sb("sg_ring", [128, 3, 512], BF16)
        c.gu_ring = Ring("gu", [gu[:, i] for i in range(4)])
        c.dn_ring = Ring("dn", [dn[:, i] for i in range(2)])
        c.sq_ring = Ring("sq", [sq[:, i] for i in range(2)])
        c.sg_ring = Ring("sg", [sg[:, i] for i in range(3)])
        h = c.h = Ctx()
        h.S = sb("h_S", [128, 2, NHD, 128], F32)
        h.sbf = [sb("h_sbf%d" % i, [128, 2, NHD, 128], BF16) for i in range(2)]
        h.hconst = sb("h_const", [128, 2 * 3 * NHD], F32)
        h.lbtmp = sb("h_lbtmp", [128, 7 * NHD], F32)
        h.hnw = sb("h_nw", [128, 2], F32)
        h.mask32 = sb("h_mask32", [128, TM], F32)
        h.cmask = sb("h_cmask", [128, 4], F32)
        h.maskbd = sb("h_maskbd", [128, 128], F32)
        h.BS = [[Buf("hS%d_%d" % (j, i)) for i in range(NHD)] for j in range(2)]
        h.Bsbf = [[[Buf("hsbf%d_%d_%d" % (p, j, i)) for i in range(NHD)] for j in range(2)] for p in range(2)]
        h.par = [[0] * NHD for _ in range(2)]
        ARENA = 80 * 1024
        arena = sb("arena", [128, ARENA], mybir.dt.uint8)
        off = [0]

        def carve(shape, dtype, reset=False):
            if reset:
                off[0] = 0
            n = int(np.prod(shape)) * (2 if dtype == BF16 else 4)
            ap = arena[:, off[0]:off[0] + n].bitcast(dtype)
            off[0] += n
            assert off[0] <= ARENA, (off[0], ARENA)
            if len(shape) > 1:
                names = " ".join("a%d" % i for i in range(len(shape)))
                kw = {"a%d" % i: shape[i] for i in range(len(shape))}
                ap = ap.rearrange("p (%s) -> p %s" % (names, names), **kw)
            return ap

        c.actT = carve([KF, T], BF16, reset=True)
        NB = TM // 128
        h.qT = carve([NHD, TM], BF16, reset=True)
        h.kT = carve([NHD, TM], BF16)
        h.sgT = carve([NHD, TM], BF16)
        h.ogT = h.sgT
        h.vall = carve([NB, 1024], BF16)
        h.vm = carve([4, 1024], BF16)
        h.kdtm = carve([NB, NHD, 128], BF16)
        h.attn = carve([NHD, 128], BF16)
        h.elast = carve([NHD, TM // HC], F32)
        tmp = carve([13, TM], F32)
        kd = carve([2, TM], BF16)
        h.tmp = Ring("htmp", [tmp[:, i, :] for i in range(13)])
        h.kd_ring = Ring("hkd", [kd[:, i, :] for i in range(2)])
        h.BqT = [Buf("hq%d" % i) for i in range(NHD)]
        h.BkT = [Buf("hk%d" % i) for i in range(NHD)]
        h.BsgT = [Buf("hsg%d" % i) for i in range(NHD)]
        h.Bvall = [Buf("hvall%d" % i) for i in range(NB)]
        h.Bvm = [[Buf("hvm%d_%d" % (i, j)) for j in range(2)] for i in range(4)]
        h.Bkdtm = [[Buf("hkdtm%d_%d" % (b, i)) for i in range(NHD)] for b in range(NB)]
        h.Battn = [Buf("hattn%d" % i) for i in range(2)]
        h.Bel = [Buf("hel%d" % i) for i in range(NHD)]


        m = c.m = Ctx()
        m.prev = sb("m_prev", [128, 2, 2048], F32)
        m.tails = sb("m_tails", [128, 2, 24, 3], F32)
        m.mconv = sb("m_conv", [128, 2, 24, 5], F32)
        m.dcol = sb("m_dcol", [128, 2, 16], F32)
        m.mnw = sb("m_nw", [128, 2, 16], F32)
        m.wdt = sb("m_wdt", [128, 2, KD, 32], BF16)
        m.dtb = sb("m_dtb", [128, 2, 32], F32)
        m.Aneg = sb("m_Aneg", [128, 2, 32], F32)
        m.onesf = sb("m_onesf", [128, 128], F32)
        m.Tmat = sb("m_Tmat", [128, 128], F32)
        m.Umat = sb("m_Umat", [128, 128], F32)
        m.maskc = m.Tmat
        m.siluz = carve([16, MT_], BF16, reset=True)
        m.ynT = m.siluz
        m.xc = carve([16, MT_], BF16)
        m.BT = carve([MG, MT_], BF16)
        m.CT = carve([MG, MT_], BF16)
        m.ppad = carve([32, 128], BF16)
        NCK_ = MT_ // 128
        m.dt = carve([NCK_, 32], F32)
        m.dA = carve([NCK_, 32], F32)
        m.dtdec = carve([NCK_, 32], F32)
        mk = lambda name, n, shape, dtype: Ring(name, [carve(shape, dtype) for _ in range(n)])
        m.xpad_ring = mk("mxpad", 3, [MHG, 128], BF16)
        m.xdd_ring = mk("mxdd", 3, [512], BF16)
        m.btm_ring = mk("mbtm", 3, [128], BF16)
        m.R_ring = mk("mR", 2, [MHG, 128], F32)
        m.LT_ring = mk("mLT", 2, [MHG, 128], BF16)
        m.ecs_ring = mk("mecs", 2, [MHG, 128], F32)
        m.Gm_ring = mk("mGm", 2, [128], BF16)
        m.MT_ring = mk("mMT", 2, [MHG, 128], BF16)
        m.Cp_ring = mk("mCp", 2, [MHG, 128], BF16)
        m.yg_ring = mk("myg", 2, [4, 128], F32)
        m.rs_ring = mk("mrs", 2, [128], F32)
        m.raw_ring = mk("mraw", 2, [MT_ + 4], F32)
        m.acc_ring = mk("macc", 2, [MT_], F32)
        m.Bsz = [Buf("msz%d" % i) for i in range(16)]
        m.Bxc = [Buf("mxc%d" % i) for i in range(16)]
        m.BBT = [Buf("mBT%d" % i) for i in range(MG)]
        m.BCT = [Buf("mCT%d" % i) for i in range(MG)]
        m.Byn = m.Bsz
        m.Bwdt = Buf("mwdt")
        m.Bppad = [Buf("mppad%d" % i) for i in range(MG)]
        m.Bprev = [[Buf("mprev%d_%d" % (j, g)) for g in range(MG)] for j in range(2)]
        m.Btail = [[Buf("mtail%d_%d" % (j, i)) for i in range(24)] for j in range(2)]
        m.Bdt = [Buf("mdt%d" % i) for i in range(NCK_)]
        m.BdA = [Buf("mdA%d" % i) for i in range(NCK_)]
        m.Bdtdec = [Buf("mdtdec%d" % i) for i in range(NCK_)]

        psum = st.enter_context(nc.psum_tensor("psum", [128, 8, 512], F32))
        c.bank8 = Ring("bank", [psum[:, i, :] for i in range(8)])
        c.bank4 = Ring("bankm", [psum[:, i, :] for i in range(4)])
        c.qbank_ring = Ring("qbank", [psum[:, 4 + i, 0:128] for i in range(4)])
        c.qbank = c.qbank_ring.next
        c.bank = c.bank8.next
        c.BxT = [Buf("xT%d" % k) for k in range(KD)]
        c.BuT = [[Buf("uT%d_%d" % (k, hh)) for hh in range(NH)] for k in range(KD)]
        c.Bact = [[Buf("act%d_%d" % (k, hh)) for hh in range(NH)] for k in range(KF)]
        c.Brstd = [Buf("rstd%d" % hh) for hh in range(NH)]
        c.Bconst = Buf("const")
        c.Bout = Buf("out")
        flat = lambda x: [b for r in x for b in (flat(r) if isinstance(r, list) else [r])]
        c.phase_bufs = {
            "ffn": flat(c.Bact) + c.bank8.bufs,
            "hgrn": flat([h.BqT, h.BkT, h.BsgT, h.Bvall, h.Bvm, h.Bkdtm, h.Battn, h.Bel])
                    + h.tmp.bufs + h.kd_ring.bufs + c.bank4.bufs + c.qbank_ring.bufs,
            "mamba": flat([m.Bsz, m.Bxc, m.BBT, m.BCT, m.Bppad, m.Bdt, m.BdA, m.Bdtdec])
                     + sum([r.bufs for r in (m.xpad_ring, m.xdd_ring, m.btm_ring,
                                             m.R_ring, m.LT_ring, m.ecs_ring, m.Gm_ring, m.MT_ring, m.Cp_ring,
                                             m.yg_ring, m.rs_ring, m.raw_ring, m.acc_ring)], [])
                     + c.bank4.bufs + c.qbank_ring.bufs,
        }
        c.cur_phase = None
        S = Sched(nc)
        S.op("pool", lambda e: e.memset(c.ones_bf[:], 1.0), writes=[c.Bconst])
        S.op("pool", lambda e: e.memset(c.eps_col[:], EPS), writes=[c.Bconst])
        S.dma("sp", c.nw[:], c.d_nw, writes=[c.Bconst])
        emit_hgrn_consts(S, c)
        emit_mamba_consts(S, c)
        if plan is None:
            plan = ["ffn0", "mix", "ffn1"]
        for tile in range(ntiles):
            for k in range(KD):
                S.dma("sp", c.xT[:, k, :], c.d_x[k * 128:(k + 1) * 128, tile * T:(tile + 1) * T],
                      writes=[c.BxT[k]])
            for l in range(depth):
                for stg in plan:
                    if stg == "ffn0":
                        emit_ffn(S, c, l, 0)
                    elif stg == "ffn1":
                        emit_ffn(S, c, l, 1)
                    elif stg == "mix":
                        if l % 2 == 1:
                            emit_hgrn(S, c, l, l // 2)
                        else:
                            emit_mamba(S, c, l, l // 2)
                    elif stg == "mamba":
                        emit_mamba(S, c, l, l // 2)
                    elif stg == "hgrn":
                        emit_hgrn(S, c, l, l // 2)
            emit_final(S, c, tile)
        c.sbuf_left = nc.sbuf_bytes_remaining
        S.emit()
        c.makespan = S.makespan
    nc._ctx = c
    return nc


def prep_common(inputs):
    f32 = np.float32
    A = lambda k: np.asarray(inputs[k], f32)
    g = A("ffn_w_gate").reshape(DEPTH, 2, KD, 128, KF, 128)
    u = A("ffn_w_up").reshape(DEPTH, 2, KD, 128, KF, 128)
    wgu = np.stack([g, u], axis=0)
    wgu = np.ascontiguousarray(wgu.transpose(1, 2, 5, 4, 0, 3, 6))
    wd = A("ffn_w_down").reshape(DEPTH, 2, KF, 128, KD, 128)
    wd = np.ascontiguousarray(wd.transpose(0, 1, 4, 3, 2, 5))
    nw = np.concatenate([A("norm_w").reshape(DEPTH * 3, KD, 128),
                         A("final_norm_w").reshape(1, KD, 128)], axis=0)
    nw = np.ascontiguousarray(nw.transpose(2, 0, 1).reshape(128, -1))
    out = {"wgu": wgu, "wd": wd, "nw": nw}
    hw = A("h_w_in").reshape(2, KD, 128, 4, NHD, 128)
    qf = hw[:, :, :, 0:2]
    out["hwqf"] = np.ascontiguousarray(qf.transpose(0, 4, 2, 3, 1, 5))
    gg = hw[:, :, :, 3].reshape(2, KD, 128, NHD // 2, 2, 128)
    out["hwg"] = np.ascontiguousarray(gg.transpose(0, 3, 2, 4, 1, 5))
    vv = hw[:, :, :, 2].reshape(2, KD, 128, NHD // 2, 2, 128)
    out["hwv"] = np.ascontiguousarray(vv.transpose(0, 3, 2, 4, 1, 5))
    wo = A("h_w_out").reshape(2, KD, 128, KD, 128)
    out["hwo"] = np.ascontiguousarray(wo.transpose(0, 3, 2, 1, 4))
    lb = A("h_lb_logits").reshape(DEPTH, NHD, 128)
    out["hlb"] = np.ascontiguousarray(lb.transpose(2, 0, 1).reshape(128, -1))
    out["hnw"] = np.ascontiguousarray(A("h_norm_w").T)
    mw = A("m_w_in")
    win = mw[:, :, 0:5120].reshape(2, KD, 128, 20, 2, 128)
    out["mwin"] = np.ascontiguousarray(win.transpose(0, 3, 2, 4, 1, 5))
    wdt = mw[:, :, 5120:5152].reshape(2, KD, 128, 32)
    out["mwdt"] = np.ascontiguousarray(wdt.transpose(2, 0, 1, 3))
    cw = A("m_conv_w").reshape(2, 4, 24, 128)
    cb = A("m_conv_b").reshape(2, 1, 24, 128)
    out["mconv"] = np.ascontiguousarray(np.concatenate([cw, cb], axis=1).transpose(3, 0, 2, 1))
    dcol = np.repeat(A("m_d"), 64, axis=1).reshape(2, 16, 128)
    out["mdcol"] = np.ascontiguousarray(dcol.transpose(2, 0, 1))
    out["mnw"] = np.ascontiguousarray(A("m_norm_w").reshape(2, 16, 128).transpose(2, 0, 1))
    out["mdtb"] = np.ascontiguousarray(A("m_dt_bias"))
    out["malog"] = np.ascontiguousarray(A("m_a_log"))
    mwo = A("m_w_out").reshape(2, 16, 128, KD, 128)
    out["mwo"] = np.ascontiguousarray(mwo.transpose(0, 3, 2, 1, 4))
    return out


_NC_CACHE = {}


def kernel(**inputs):
    x = np.asarray(inputs["x"], np.float32)
    B = x.shape[0]
    common = prep_common(inputs)
    if "nc" not in _NC_CACHE:
        _NC_CACHE["nc"] = build_program()
    nc = _NC_CACHE["nc"]
    in_maps = []
    for b in range(B):
        m = dict(common)
        m["xT"] = np.ascontiguousarray(x[b].T)
        in_maps.append(m)
    res = run_bass_kernel_spmd(nc, in_maps, core_ids=list(range(B)))
    out = np.stack([np.ascontiguousarray(r["outT"].T) for r in res.results], axis=0)
    return out.astype(np.float32)
```

```python
import heapq
import numpy as np
import concourse.bass as bass
import concourse.mybir as mybir
from concourse.bass_utils import run_bass_kernel_spmd

F32 = mybir.dt.float32
BF16 = mybir.dt.bfloat16
AF = mybir.ActivationFunctionType
ALU = mybir.AluOpType

ENGS = ("pe", "act", "dve", "pool", "sp")
HOP_NS = 250.0
WINDOW = 48


class Buf:
    __slots__ = ("name", "lw", "rd", "dsem", "dcnt", "dlast")

    def __init__(self, name):
        self.name = name
        self.lw = None
        self.rd = []
        self.dsem = None
        self.dcnt = 0
        self.dlast = None


class Op:
    __slots__ = ("eng", "fn", "deps", "marked", "dma", "idx", "cost", "seq", "fin", "pos", "dcount")


class Sched:
    def __init__(self, nc):
        self.nc = nc
        self.ops = {e: [] for e in ENGS}
        self.dma_bufs = []
        self.nseq = 0

    def _record(self, eng, fn, reads, writes, dma=None, cost=500.0):
        op = Op()
        op.eng = eng
        op.fn = fn
        op.marked = False
        op.dma = dma
        op.cost = cost
        op.seq = self.nseq
        self.nseq += 1
        deps = []
        for b in reads:
            if b.lw is not None:
                deps.append(b.lw)
        for b in writes:
            if b.lw is not None:
                deps.append(b.lw)
            deps.extend(b.rd)
        if dma is not None and dma.dlast is not None:
            deps.append(dma.dlast)
        seen = set()
        out = []
        for d in deps:
            if id(d[1]) in seen:
                continue
            seen.add(id(d[1]))
            out.append(d)
        op.deps = out
        self.ops[eng].append(op)
        if dma is not None:
            if dma.dsem is None:
                self.dma_bufs.append(dma)
                dma.dsem = True
            dma.dcnt += 16
            op.dcount = dma.dcnt
            tok = ("d", op)
            dma.dlast = tok
        else:
            op.dcount = 0
            tok = ("e", op)
        for b in reads:
            b.rd.append(tok)
        for b in writes:
            b.lw = tok
            b.rd = []
        return op

    DEFAULT_COST = {"pe": 150.0, "act": 700.0, "dve": 700.0, "pool": 1500.0, "sp": 100.0}

    def op(self, eng, fn, reads=(), writes=(), cost=None):
        if cost is None:
            cost = self.DEFAULT_COST[eng]
        return self._record(eng, fn, reads, writes, cost=cost)

    @staticmethod
    def fence(old_bufs, new_bufs):
        toks = {}
        for b in old_bufs:
            if b.lw is not None:
                toks[id(b.lw[1])] = b.lw
            for t in b.rd:
                toks[id(t[1])] = t
        toks = list(toks.values())
        for b in new_bufs:
            b.rd.extend(toks)

    def dma(self, eng, out, in_, reads=(), writes=(), sem=None, nbytes=65536):
        if sem is None:
            sem = (list(writes) + list(reads))[0]
        return self._record(eng, lambda e: e.dma_start(out=out, in_=in_), reads, writes, dma=sem,
                            cost=2000.0 + nbytes / 100.0)

    def _list_schedule(self):
        pend = {e: list(self.ops[e]) for e in ENGS}
        head = {e: 0 for e in ENGS}
        free = {e: 0.0 for e in ENGS}
        order = {e: [] for e in ENGS}
        for e in ENGS:
            for op in pend[e]:
                op.fin = None
        total = sum(len(v) for v in pend.values())
        done = 0
        issue = {"sp": 60.0, "pool": 700.0}
        while done < total:
            best = None
            for e in ENGS:
                lst = pend[e]
                h = head[e]
                n = len(lst)
                while h < n and lst[h] is None:
                    h += 1
                head[e] = h
                cnt = 0
                i = h
                fe = free[e]
                while i < n and cnt < WINDOW:
                    op = lst[i]
                    if op is not None:
                        cnt += 1
                        rdy = 0.0
                        ok = True
                        for d in op.deps:
                            f = d[1].fin
                            if f is None:
                                ok = False
                                break
                            if d[1].eng != e or d[0] == "d":
                                f += HOP_NS
                            if f > rdy:
                                rdy = f
                        if ok:
                            st = rdy if rdy > fe else fe
                            key = (st, op.seq)
                            if best is None or key < best[0]:
                                best = (key, e, i, op)
                            if rdy <= fe:
                                break
                    i += 1
            key, e, i, op = best
            st = key[0]
            if op.dma is not None:
                free[e] = st + issue.get(e, 100.0)
                op.fin = st + op.cost
            else:
                free[e] = st + op.cost
                op.fin = free[e]
            pend[e][i] = None
            order[e].append(op)
            done += 1
        self.makespan = max(free.values())
        return order

    def emit(self, reorder=True):
        nc = self.nc
        order = self._list_schedule() if reorder else self.ops
        for e in ENGS:
            for i, op in enumerate(order[e]):
                op.pos = i
        def needs_wait(op, d):
            if d[0] == "d":
                return True
            p = d[1]
            if p.eng == op.eng and op.dma is None and op.eng == "pe":
                return False
            return True
        for e in ENGS:
            for op in order[e]:
                for d in op.deps:
                    if d[0] == "e" and needs_wait(op, d):
                        d[1].marked = True
        count = {}
        for e in ENGS:
            cnt = 0
            for op in order[e]:
                if op.marked:
                    cnt += 1
                count[id(op)] = cnt
        from contextlib import ExitStack
        with ExitStack() as st:
            esem = {e: st.enter_context(nc.semaphore("sem_" + e)) for e in ENGS}
            for i, b in enumerate(self.dma_bufs):
                b.dsem = st.enter_context(nc.semaphore("dsem%d" % i))
            block = st.enter_context(nc.Block())

            def run(e, eng):
                seen = {}
                for op in order[e]:
                    for d in op.deps:
                        if not needs_wait(op, d):
                            continue
                        p = d[1]
                        if d[0] == "e":
                            sem = esem[p.eng]
                            val = count[id(p)]
                        else:
                            sem = p.dma.dsem
                            val = p.dcount
                        k = id(sem)
                        if seen.get(k, 0) >= val:
                            continue
                        seen[k] = val
                        eng.wait_ge(sem, val)
                    ins = op.fn(eng)
                    if op.dma is not None:
                        ins.then_inc(op.dma.dsem, 16)
                    elif op.marked:
                        ins.then_inc(esem[e], 1)

            final = [(b.dsem, b.dcnt) for b in self.dma_bufs]

            @block.tensor
            def _(eng):
                run("pe", eng)

            @block.scalar
            def _(eng):
                run("act", eng)

            @block.vector
            def _(eng):
                run("dve", eng)

            @block.gpsimd
            def _(eng):
                run("pool", eng)

            @block.sync
            def _(eng):
                run("sp", eng)
                for sem, cnt in final:
                    eng.wait_ge(sem, cnt)


D = 1024
KD = D // 128
DFF = 2816
KF = DFF // 128
SEQ = 4096
DEPTH = 4
EPS = 1e-6
T = 1024
NH = T // 512
TM = 512
F32R = mybir.dt.float32r
NHD = 8
HC = 32


class Ring:
    def __init__(self, name, aps):
        self.aps = aps
        self.bufs = [Buf("%s%d" % (name, i)) for i in range(len(aps))]
        self.i = 0

    def next(self):
        i = self.i
        self.i = (i + 1) % len(self.aps)
        return self.aps[i], self.bufs[i]


class Ctx:
    pass


def _n(ap):
    n = 1
    for s in ap.shape[1:]:
        n *= int(s)
    return n


def _mm(S, out, lhsT, rhs, start, stop, reads, writes):
    n = _n(rhs)
    cost = 25.0 + max(n, 64) / 2.4 * (4.0 if lhsT.dtype == F32 else 1.0)
    S.op("pe", lambda e: e.matmul(out, lhsT=lhsT, rhs=rhs, start=start, stop=stop),
         reads=reads, writes=writes, cost=cost)


def _act(S, out, in_, func, reads, writes, **kw):
    S.op("act", lambda e: e.activation(out=out, in_=in_, func=func, **kw), reads=reads, writes=writes,
         cost=230.0 + 0.95 * _n(out))


def _tt(S, eng, out, in0, in1, op, reads, writes):
    n = _n(out)
    cost = (120.0 + 1.4 * n) if eng == "dve" else (350.0 + 2.2 * n)
    S.op(eng, lambda e: e.tensor_tensor(out=out, in0=in0, in1=in1, op=op), reads=reads, writes=writes, cost=cost)


def emit_rmsnorm(S, c, wcol):
    banks = [c.bank() for _ in range(NH)]
    for k in range(KD):
        sq, bsq = c.sq_ring.next()
        _act(S, sq, c.xT[:, k, :], AF.Square, [c.BxT[k]], [bsq])
        for h in range(NH):
            ps, bps = banks[h]
            _mm(S, ps, c.ones_bf[:, :], sq[:, h * 512:(h + 1) * 512], k == 0, k == KD - 1,
                [bsq, c.Bconst], [bps])
    for h in range(NH):
        ps, bps = banks[h]
        sl = slice(h * 512, (h + 1) * 512)
        _act(S, c.rstd[:, sl], ps, AF.Ln, [bps, c.Bconst], [c.Brstd[h]], scale=1.0 / D, bias=c.eps_col[:, 0:1])
        _act(S, c.rstd[:, sl], c.rstd[:, sl], AF.Exp, [c.Brstd[h]], [c.Brstd[h]], scale=-0.5)
    for k in range(KD):
        for h in range(NH):
            sl = slice(h * 512, (h + 1) * 512)
            S.op("dve", lambda e, k=k, sl=sl: e.scalar_tensor_tensor(
                out=c.uT[:, k, sl], in0=c.xT[:, k, sl], scalar=c.nw[:, wcol + k:wcol + k + 1],
                in1=c.rstd[:, sl], op0=ALU.mult, op1=ALU.mult),
                reads=[c.BxT[k], c.Brstd[h], c.Bconst], writes=[c.BuT[k][h]])


def enter_phase(S, c, name):
    new = c.phase_bufs[name]
    if c.cur_phase is not None and c.cur_phase != name:
        S.fence(c.phase_bufs[c.cur_phase], new)
    c.cur_phase = name
    if name == "ffn":
        c.bank = c.bank8.next
    else:
        c.bank = c.bank4.next


def emit_ffn(S, c, l, j):
    enter_phase(S, c, "ffn")
    emit_rmsnorm(S, c, (l * 3 + (0 if j == 0 else 2)) * KD)
    for fc in range(KF):
        w, bw = c.gu_ring.next()
        S.dma("pool", w, c.d_wgu[l, j, fc], writes=[bw])
        bk = [[c.bank() for _ in range(NH)] for _ in range(2)]
        for k in range(KD):
            for g in range(2):
                for h in range(NH):
                    ps, bps = bk[g][h]
                    _mm(S, ps, w[:, g, k, :], c.uT[:, k, h * 512:(h + 1) * 512], k == 0, k == KD - 1,
                        [bw, c.BuT[k][h]], [bps])
        for h in range(NH):
            sl = slice(h * 512, (h + 1) * 512)
            sg, bsg = c.sg_ring.next()
            pg, bpg = bk[0][h]
            pu, bpu = bk[1][h]
            _act(S, sg, pg, AF.Silu, [bpg], [bsg])
            _tt(S, "dve", c.actT[:, fc, sl], pu, sg, ALU.mult, [bpu, bsg], [c.Bact[fc][h]])
    for dc in range(KD):
        w, bw = c.dn_ring.next()
        S.dma("pool", w, c.d_wd[l, j, dc], writes=[bw])
        bk = [c.bank() for _ in range(NH)]
        for k in range(KF):
            for h in range(NH):
                ps, bps = bk[h]
                _mm(S, ps, w[:, k, :], c.actT[:, k, h * 512:(h + 1) * 512], k == 0, k == KF - 1,
                    [bw, c.Bact[k][h]], [bps])
        for h in range(NH):
            sl = slice(h * 512, (h + 1) * 512)
            ps, bps = bk[h]
            S.op("dve", lambda e, ps=ps, dc=dc, sl=sl: e.scalar_tensor_tensor(
                out=c.xT[:, dc, sl], in0=ps, scalar=0.5, in1=c.xT[:, dc, sl],
                op0=ALU.mult, op1=ALU.add),
                reads=[bps, c.BxT[dc]], writes=[c.BxT[dc]])


def emit_final(S, c, tile):
    enter_phase(S, c, "ffn")
    wc = DEPTH * 3 * KD
    emit_rmsnorm(S, c, wc)
    for k in range(KD):
        for h in range(NH):
            sl = slice(h * 512, (h + 1) * 512)
            S.op("dve", lambda e, k=k, sl=sl: e.scalar_tensor_tensor(
                out=c.xT[:, k, sl], in0=c.xT[:, k, sl], scalar=c.nw[:, wc + k:wc + k + 1],
                in1=c.rstd[:, sl], op0=ALU.mult, op1=ALU.mult),
                reads=[c.BxT[k], c.Brstd[h], c.Bconst], writes=[c.BxT[k]])
        S.dma("sp", c.d_out[k * 128:(k + 1) * 128, tile * T:(tile + 1) * T], c.xT[:, k, :],
              reads=[c.BxT[k]])


def emit_hgrn(S, c, l, jm):
    enter_phase(S, c, "hgrn")
    emit_rmsnorm(S, c, (l * 3 + 1) * KD)
    h = c.h
    NB = TM // 128
    NCH = TM // HC
    for sub in range(T // TM):
        t0 = sub * TM
        tsl = slice(t0, t0 + TM)
        hh = sub % NH if TM == 512 else None
        rdu = lambda k: [c.BuT[k][(t0 // 512)]]
        for vp in range(NHD // 2):
            w, bw = c.gu_ring.next()
            S.dma("pool", w, c.d_hwv[jm, vp], writes=[bw])
            for b in range(NB):
                pv, bpv = c.bank()
                for i in range(2):
                    for k in range(KD):
                        _mm(S, pv[:, i * 128:(i + 1) * 128], c.uT[:, k, t0 + b * 128:t0 + (b + 1) * 128],
                            w[:, i, k, :], k == 0, k == KD - 1, rdu(k) + [bw], [bpv])
                S.op("act", lambda e, pv=pv, b=b, vp=vp: e.copy(out=h.vall[:, b, vp * 256:(vp + 1) * 256], in_=pv[:, 0:256]),
                     reads=[bpv], writes=[h.Bvall[b]])
        for hd in range(NHD):
            w, bw = c.gu_ring.next()
            S.dma("pool", w, c.d_hwqf[jm, hd], writes=[bw])
            pq, bpq = c.bank()
            pf, bpf = c.bank()
            for k in range(KD):
                _mm(S, pq, w[:, 0, k, :], c.uT[:, k, tsl], k == 0, k == KD - 1, [bw] + rdu(k), [bpq])
            for k in range(KD):
                _mm(S, pf, w[:, 1, k, :], c.uT[:, k, tsl], k == 0, k == KD - 1, [bw] + rdu(k), [bpf])
            qs, bqs = h.tmp.next()
            sg, bsg = h.tmp.next()
            lf, blf = h.tmp.next()
            kk, bkk = h.tmp.next()
            cs, bcs = h.tmp.next()
            eq, beq = h.tmp.next()
            ek, bek = h.tmp.next()
            _act(S, qs, pq, AF.Silu, [bpq], [bqs])
            _act(S, sg, pf, AF.Sigmoid, [bpf], [bsg])
            cb = (jm * 3) * NHD + hd
            _act(S, lf, sg, AF.Ln, [bsg, c.Bconst], [blf],
                 scale=h.hconst[:, cb + NHD:cb + NHD + 1], bias=h.hconst[:, cb:cb + 1])
            S.op("dve", lambda e, kk=kk, sg=sg, cb=cb: e.tensor_scalar(
                out=kk, in0=sg, scalar1=h.hconst[:, cb + 2 * NHD:cb + 2 * NHD + 1],
                scalar2=h.hconst[:, cb + NHD:cb + NHD + 1], op0=ALU.mult, op1=ALU.add),
                reads=[bsg, c.Bconst], writes=[bkk])
            S.op("dve", lambda e, cs=cs, lf=lf: e.tensor_tensor_scan(
                out=cs, data0=h.mask32[:, :], data1=lf, initial=0.0, op0=ALU.mult, op1=ALU.add),
                reads=[blf, c.Bconst], writes=[bcs])
            _act(S, eq, cs, AF.Exp, [bcs], [beq])
            _act(S, ek, cs, AF.Exp, [bcs], [bek], scale=-1.0)
            _tt(S, "dve", h.qT[:, hd, :], qs, eq, ALU.mult, [bqs, beq], [h.BqT[hd]])
            _tt(S, "dve", kk, kk, ek, ALU.mult, [bkk, bek], [bkk])
            S.op("pool", lambda e, kk=kk, hd=hd: e.tensor_copy(out=h.kT[:, hd, :], in_=kk),
                 reads=[bkk], writes=[h.BkT[hd]])
            eq3 = eq.rearrange("p (c j) -> p c j", j=HC)
            kd, bkd = h.kd_ring.next()
            _tt(S, "pool", kd.rearrange("p (c j) -> p c j", j=HC), kk.rearrange("p (c j) -> p c j", j=HC),
                eq3[:, :, HC - 1:HC].broadcast_to([128, NCH, HC]), ALU.mult, [bkk, beq], [bkd])
            S.op("pool", lambda e, eq3=eq3, hd=hd: e.tensor_copy(
                out=h.elast[:, hd, :].rearrange("p (c o) -> p c o", o=1), in_=eq3[:, :, HC - 1:HC]),
                reads=[beq], writes=[h.Bel[hd]])
            for b in range(NB):
                pt, bpt = c.qbank()
                ptb = pt.bitcast(BF16)[:, 0:128]
                S.op("pe", lambda e, ptb=ptb, kd=kd, b=b: e.transpose(ptb, kd[:, b * 128:(b + 1) * 128], c.ident[:, :]),
                     reads=[bkd, c.Bconst], writes=[bpt])
                S.op("act", lambda e, ptb=ptb, b=b, hd=hd: e.copy(out=h.kdtm[:, b, hd, :], in_=ptb),
                     reads=[bpt], writes=[h.Bkdtm[b][hd]])
        for gp in range(NHD // 2):
            w, bw = c.gu_ring.next()
            S.dma("pool", w, c.d_hwg[jm, gp], writes=[bw])
            for i in range(2):
                hd = gp * 2 + i
                pg, bpg = c.bank()
                for k in range(KD):
                    _mm(S, pg, w[:, i, k, :], c.uT[:, k, tsl], k == 0, k == KD - 1, [bw] + rdu(k), [bpg])
                _act(S, h.sgT[:, hd, :], pg, AF.Silu, [bpg], [h.BsgT[hd]])
        for b in range(NB):
            tok = slice(t0 + b * 128, t0 + (b + 1) * 128)
            bsl = slice(b * 128, (b + 1) * 128)
            for cc in range(4):
                for half in range(2):
                    hs = slice(half * 512, (half + 1) * 512)
                    eng = "dve" if (cc + half) % 2 == 0 else "act"
                    if eng == "dve":
                        S.op("dve", lambda e, hs=hs, cc=cc, b=b: e.tensor_scalar(
                            out=h.vm[:, cc, hs], in0=h.vall[:, b, hs], scalar1=h.cmask[:, cc:cc + 1], scalar2=None,
                            op0=ALU.mult), reads=[h.Bvall[b], c.Bconst], writes=[h.Bvm[cc][half]])
                    else:
                        _act(S, h.vm[:, cc, hs], h.vall[:, b, hs], AF.Copy, [h.Bvall[b], c.Bconst], [h.Bvm[cc][half]],
                             scale=h.cmask[:, cc:cc + 1])
            for hb in range(2):
                pa, bpa = c.bank()
                for i in range(4):
                    hd = hb * 4 + i
                    _mm(S, pa[:, i * 128:(i + 1) * 128], h.kT[:, hd, bsl], h.qT[:, hd, bsl], True, True,
                        [h.BkT[hd], h.BqT[hd]], [bpa])
                _tt(S, "dve", h.attn[:, hb * 4:(hb + 1) * 4, :], pa.rearrange("p (i l) -> p i l", i=4),
                    h.maskbd[:, :].unsqueeze(1).broadcast_to([128, 4, 128]), ALU.mult,
                    [bpa, c.Bconst], [h.Battn[hb]])
            pos = []
            for hb in range(2):
                po, bpo = c.bank()
                pos.append((po, bpo))
                for i in range(4):
                    hd = hb * 4 + i
                    _mm(S, po[:, i * 128:(i + 1) * 128], h.vall[:, b, hd * 128:(hd + 1) * 128], h.attn[:, hd, :],
                        True, True, [h.Bvall[b], h.Battn[hb]], [bpo])
            pis = [c.bank() for _ in range(2)]
            for cc in range(4):
                ch = (t0 + b * 128) // HC % (T // HC)
                chunk_in_sub = b * 4 + cc
                for hd in range(NHD):
                    par = h.par[jm][hd]
                    pi, bpi = pis[hd // 4]
                    i = hd % 4
                    _mm(S, pi[:, i * 128 + cc * HC:i * 128 + (cc + 1) * HC], h.sbf[par][:, jm, hd, :],
                        h.qT[:, hd, b * 128 + cc * HC:b * 128 + (cc + 1) * HC], True, True,
                        [h.Bsbf[par][jm][hd], h.BqT[hd]], [bpi])
                    pu, bpu = c.qbank()
                    _mm(S, pu, h.kdtm[:, b, hd, :], h.vm[:, cc, hd * 128:(hd + 1) * 128], True, True,
                        [h.Bkdtm[b][hd], h.Bvm[cc][hd // 4]], [bpu])
                    S.op("dve", lambda e, pu=pu, hd=hd, chunk_in_sub=chunk_in_sub: e.scalar_tensor_tensor(
                        out=h.S[:, jm, hd, :], in0=h.S[:, jm, hd, :],
                        scalar=h.elast[:, hd, chunk_in_sub:chunk_in_sub + 1], in1=pu,
                        op0=ALU.mult, op1=ALU.add),
                        reads=[bpu, h.BS[jm][hd], h.Bel[hd]], writes=[h.BS[jm][hd]])
                    S.op("act", lambda e, hd=hd, par=par: e.copy(out=h.sbf[1 - par][:, jm, hd, :], in_=h.S[:, jm, hd, :]),
                         reads=[h.BS[jm][hd]], writes=[h.Bsbf[1 - par][jm][hd]])
                    h.par[jm][hd] = 1 - par
            for hb in range(2):
                pi, bpi = pis[hb]
                po, bpo = pos[hb]
                oi, boi = h.tmp.next()
                osum, bos = h.tmp.next()
                S.op("act", lambda e, oi=oi, pi=pi: e.copy(out=oi, in_=pi), reads=[bpi], writes=[boi])
                _tt(S, "dve", osum, po, oi, ALU.add, [bpo, boi], [bos])
                osq, bosq = c.sg_ring.next()
                _act(S, osq, osum, AF.Square, [bos], [bosq])
                pss, bpss = c.bank()
                _mm(S, pss, c.ones_bf[:, :], osq, True, True, [bosq, c.Bconst], [bpss])
                rs, brs = h.tmp.next()
                _act(S, rs, pss, AF.Ln, [bpss, c.Bconst], [brs], scale=1.0 / 128, bias=c.eps_col[:, 0:1])
                _act(S, rs, rs, AF.Exp, [brs], [brs], scale=-0.5)
                _tt(S, "dve", osum, osum, rs, ALU.mult, [bos, brs], [bos])
                S.op("dve", lambda e, osum=osum, hb=hb, bsl=bsl: e.scalar_tensor_tensor(
                    out=h.ogT[:, hb * 4:(hb + 1) * 4, bsl], in0=osum.rearrange("p (i l) -> p i l", i=4),
                    scalar=h.hnw[:, jm:jm + 1], in1=h.sgT[:, hb * 4:(hb + 1) * 4, bsl],
                    op0=ALU.mult, op1=ALU.mult),
                    reads=[bos, c.Bconst] + [h.BsgT[hb * 4 + i] for i in range(4)],
                    writes=[h.BsgT[hb * 4 + i] for i in range(4)])
        for dc in range(KD):
            w, bw = c.dn_ring.next()
            S.dma("pool", w[:, 0:KD, :], c.d_hwo[jm, dc], writes=[bw])
            ps, bps = c.bank()
            for k in range(KD):
                _mm(S, ps, w[:, k, :], h.ogT[:, k, :], k == 0, k == KD - 1,
                    [bw, h.BsgT[k]], [bps])
            _tt(S, "dve", c.xT[:, dc, tsl], ps, c.xT[:, dc, tsl], ALU.add, [bps, c.BxT[dc]], [c.BxT[dc]])


def emit_hgrn_consts(S, c):
    h = c.h
    ex = h.lbtmp
    S.dma("sp", ex[:, 0:4 * NHD], c.d_hlb, writes=[c.Bconst])
    _act(S, ex[:, 0:4 * NHD], ex[:, 0:4 * NHD], AF.Exp, [c.Bconst], [c.Bconst])
    den = ex[:, 4 * NHD:5 * NHD]
    e = lambda i: ex[:, i * NHD:(i + 1) * NHD]
    _tt(S, "dve", den, e(0), e(1), ALU.add, [c.Bconst], [c.Bconst])
    _tt(S, "dve", den, den, e(2), ALU.add, [c.Bconst], [c.Bconst])
    _tt(S, "dve", den, den, e(3), ALU.add, [c.Bconst], [c.Bconst])
    rden = ex[:, 5 * NHD:6 * NHD]
    S.op("dve", lambda e_: e_.reciprocal(out=rden, in_=den), reads=[c.Bconst], writes=[c.Bconst])
    s123 = ex[:, 6 * NHD:7 * NHD]
    _tt(S, "dve", s123, e(1), e(2), ALU.add, [c.Bconst], [c.Bconst])
    _tt(S, "dve", s123, s123, e(3), ALU.add, [c.Bconst], [c.Bconst])
    for jm, num in ((0, e(1)), (1, s123)):
        lb = h.hconst[:, (jm * 3) * NHD:(jm * 3 + 1) * NHD]
        oml = h.hconst[:, (jm * 3 + 1) * NHD:(jm * 3 + 2) * NHD]
        noml = h.hconst[:, (jm * 3 + 2) * NHD:(jm * 3 + 3) * NHD]
        _tt(S, "dve", lb, num, rden, ALU.mult, [c.Bconst], [c.Bconst])
        S.op("dve", lambda e_, lb=lb, oml=oml: e_.tensor_scalar(out=oml, in0=lb, scalar1=-1.0, scalar2=1.0,
                                                                 op0=ALU.mult, op1=ALU.add),
             reads=[c.Bconst], writes=[c.Bconst])
        S.op("dve", lambda e_, noml=noml, oml=oml: e_.tensor_scalar(out=noml, in0=oml, scalar1=-1.0, scalar2=None,
                                                                    op0=ALU.mult),
             reads=[c.Bconst], writes=[c.Bconst])
    S.op("pool", lambda e_: e_.memset(h.mask32[:, :], 1.0), writes=[c.Bconst])
    S.op("pool", lambda e_: e_.memset(h.mask32[:, :].rearrange("p (c j) -> p c j", j=HC)[:, :, 0:1], 0.0),
         reads=[c.Bconst], writes=[c.Bconst])
    S.op("pool", lambda e_: e_.memset(h.cmask[:, :], 1.0), writes=[c.Bconst])
    for cc in range(4):
        S.op("pool", lambda e_, cc=cc: e_.affine_select(
            out=h.cmask[:, cc:cc + 1], in_=h.cmask[:, cc:cc + 1], pattern=[[0, 1]], compare_op=ALU.is_ge,
            fill=0.0, base=-32 * cc, channel_multiplier=1), reads=[c.Bconst], writes=[c.Bconst])
        S.op("pool", lambda e_, cc=cc: e_.affine_select(
            out=h.cmask[:, cc:cc + 1], in_=h.cmask[:, cc:cc + 1], pattern=[[0, 1]], compare_op=ALU.is_ge,
            fill=0.0, base=32 * cc + 31, channel_multiplier=-1), reads=[c.Bconst], writes=[c.Bconst])
    S.op("pool", lambda e_: e_.memset(h.maskbd[:, :], 1.0), writes=[c.Bconst])
    S.op("pool", lambda e_: e_.affine_select(out=h.maskbd[:, :], in_=h.maskbd[:, :], pattern=[[1, 128]],
                                             compare_op=ALU.is_ge, fill=0.0, base=0, channel_multiplier=-1),
         reads=[c.Bconst], writes=[c.Bconst])
    for cc in range(4):
        S.op("dve", lambda e_, cc=cc: e_.tensor_scalar(
            out=h.maskbd[:, cc * 32:(cc + 1) * 32], in0=h.maskbd[:, cc * 32:(cc + 1) * 32],
            scalar1=h.cmask[:, cc:cc + 1], scalar2=None, op0=ALU.mult), reads=[c.Bconst], writes=[c.Bconst])
    S.op("pool", lambda e_: e_.memset(c.ident[:, :], 0.0), writes=[c.Bconst])
    S.op("pool", lambda e_: e_.affine_select(out=c.ident[:, :], in_=c.ident[:, :], pattern=[[-1, 128]],
                                             compare_op=ALU.not_equal, fill=1.0, base=0, channel_multiplier=1),
         reads=[c.Bconst], writes=[c.Bconst])
    S.op("pool", lambda e_: e_.memset(h.S[:], 0.0), writes=[b for r in h.BS for b in r])
    for par in range(2):
        S.op("pool", lambda e_, par=par: e_.memset(h.sbf[par][:], 0.0), writes=[b for r in h.Bsbf[par] for b in r])
    S.dma("sp", h.hnw[:, :], c.d_hnw, writes=[c.Bconst])


MT_ = 256
MG = 4
MHG = 8


def _pad_ap(ap2):
    return bass.AP(ap2.tensor, ap2.offset, [[ap2.ap[0][0], 128], [192, 2], [1, 64]])


def emit_mamba(S, c, l, jm):
    enter_phase(S, c, "mamba")
    emit_rmsnorm(S, c, (l * 3 + 1) * KD)
    m = c.m
    NCK = MT_ // 128
    for xpg_, bx_ in zip(m.xpad_ring.aps, m.xpad_ring.bufs):
        S.op("pool", lambda e, xpg_=xpg_: e.memset(xpg_, 0.0), writes=[bx_])
    S.op("pool", lambda e: e.memset(m.ppad[:], 0.0), writes=m.Bppad)
    for g in range(MG):
        for i in range(MHG // 2):
            ci = g * 4 + i
            S.op("act", lambda e, ci=ci: e.copy(out=_pad_ap(m.ppad[:, 2 * ci:2 * ci + 2, :].rearrange("p a b -> p (a b)")),
                                                  in_=m.prev[:, jm, ci * 128:(ci + 1) * 128].rearrange("p (a b) -> p a b", a=2)),
                 reads=[m.Bprev[jm][g]], writes=[m.Bppad[g]])
    for sub in range(T // MT_):
        t0 = sub * MT_
        tsl = slice(t0, t0 + MT_)
        rdu = lambda k: [c.BuT[k][t0 // 512]]
        for ck in range(NCK):
            tok = slice(t0 + ck * 128, t0 + (ck + 1) * 128)
            pd, bpd = c.qbank()
            for k in range(KD):
                _mm(S, pd[:, 0:32], c.uT[:, k, tok], m.wdt[:, jm, k, :], k == 0, k == KD - 1, rdu(k) + [m.Bwdt], [bpd])
            dtc, dAc, ddc = m.dt[:, ck, :], m.dA[:, ck, :], m.dtdec[:, ck, :]
            _tt(S, "dve", dtc, pd[:, 0:32], m.dtb[:, jm, :], ALU.add, [bpd, c.Bconst], [m.Bdt[ck]])
            _act(S, dtc, dtc, AF.Exp, [m.Bdt[ck]], [m.Bdt[ck]])
            _act(S, dtc, dtc, AF.Ln, [m.Bdt[ck]], [m.Bdt[ck]], bias=1.0)
            _tt(S, "dve", dAc, dtc, m.Aneg[:, jm, :], ALU.mult, [m.Bdt[ck], c.Bconst], [m.BdA[ck]])
            pr_, bpr = c.qbank()
            _mm(S, pr_[:, 0:32], m.Umat[:, :], dAc, True, True, [m.BdA[ck], c.Bconst], [bpr])
            _act(S, ddc, pr_[:, 0:32], AF.Exp, [bpr], [m.Bdtdec[ck]])
            _tt(S, "dve", ddc, ddc, dtc, ALU.mult, [m.Bdtdec[ck], m.Bdt[ck]], [m.Bdtdec[ck]])
        for pr in range(20):
            w, bw = c.gu_ring.next()
            S.dma("pool", w, c.d_mwin[jm, pr], writes=[bw])
            for i in range(2):
                ch = pr * 2 + i
                ps, bps = c.bank()
                for k in range(KD):
                    _mm(S, ps[:, 0:MT_], w[:, i, k, :], c.uT[:, k, tsl], k == 0, k == KD - 1, [bw] + rdu(k), [bps])
                if ch < 16:
                    _act(S, m.siluz[:, ch, :], ps[:, 0:MT_], AF.Silu, [bps], [m.Bsz[ch]])
                    continue
                ci = ch - 16
                raw, braw = m.raw_ring.next()
                acc, bacc = m.acc_ring.next()
                S.op("pool", lambda e, raw=raw, ci=ci: e.tensor_copy(out=raw[:, 0:3], in_=m.tails[:, jm, ci, :]),
                     reads=[m.Btail[jm][ci]], writes=[braw])
                S.op("act", lambda e, raw=raw, ps=ps: e.copy(out=raw[:, 3:3 + MT_], in_=ps[:, 0:MT_]),
                     reads=[bps], writes=[braw])
                S.op("pool", lambda e, raw=raw, ci=ci: e.tensor_copy(out=m.tails[:, jm, ci, :], in_=raw[:, MT_:MT_ + 3]),
                     reads=[braw], writes=[m.Btail[jm][ci]])
                cw = lambda tap, ci=ci: m.mconv[:, jm, ci, tap:tap + 1]
                S.op("dve", lambda e, raw=raw, acc=acc, cw=cw: e.tensor_scalar(
                    out=acc, in0=raw[:, 0:MT_], scalar1=cw(0), scalar2=None, op0=ALU.mult),
                    reads=[braw, c.Bconst], writes=[bacc])
                for tap in range(1, 4):
                    S.op("dve", lambda e, raw=raw, acc=acc, cw=cw, tap=tap: e.scalar_tensor_tensor(
                        out=acc, in0=raw[:, tap:tap + MT_], scalar=cw(tap), in1=acc, op0=ALU.mult, op1=ALU.add),
                        reads=[braw, bacc, c.Bconst], writes=[bacc])
                if ci < 16:
                    dst, bdst = m.xc[:, ci, :], m.Bxc[ci]
                elif ci < 20:
                    dst, bdst = m.BT[:, ci - 16, :], m.BBT[ci - 16]
                else:
                    dst, bdst = m.CT[:, ci - 20, :], m.BCT[ci - 20]
                _act(S, dst, acc, AF.Silu, [bacc, c.Bconst], [bdst], bias=cw(4))
        for ck in range(NCK):
            csl = slice(ck * 128, (ck + 1) * 128)
            for g in range(MG):
                hs = slice(g * MHG, (g + 1) * MHG)
                pt, bpt = c.bank()
                ptb = pt.bitcast(BF16)[:, 0:512]
                for i in range(4):
                    ci = g * 4 + i
                    S.op("pe", lambda e, ptb=ptb, ci=ci, i=i, csl=csl: e.transpose(
                        ptb[:, i * 128:(i + 1) * 128], m.xc[:, ci, csl], c.ident[:, :]),
                        reads=[m.Bxc[ci], c.Bconst], writes=[bpt], cost=120.0)
                pt4 = ptb.rearrange("p (c a b) -> p c a b", c=4, a=2)
                xpg, bxpg = m.xpad_ring.next()
                xp = xpg.rearrange("p a b -> p (a b)")
                xp4 = bass.AP(xp.tensor, xp.offset, [[xp.ap[0][0], 128], [256, 4], [192, 2], [1, 64]])
                _tt(S, "dve", xp4, pt4,
                    m.dt[:, ck, hs].rearrange("p (c a) -> p c a", a=2).unsqueeze(3).broadcast_to([128, 4, 2, 64]),
                    ALU.mult, [bpt, m.Bdt[ck]], [bxpg])
                xddg, bxddg = m.xdd_ring.next()
                _tt(S, "dve", xddg.rearrange("p (c a b) -> p c a b", c=4, a=2), pt4,
                    m.dtdec[:, ck, hs].rearrange("p (c a) -> p c a", a=2).unsqueeze(3).broadcast_to([128, 4, 2, 64]),
                    ALU.mult, [bpt, m.Bdtdec[ck]], [bxddg])
                ptB, bptB = c.qbank()
                ptBb = ptB.bitcast(BF16)[:, 0:128]
                S.op("pe", lambda e, ptBb=ptBb, g=g, csl=csl: e.transpose(ptBb, m.BT[:, g, csl], c.ident[:, :]),
                     reads=[m.BBT[g], c.Bconst], writes=[bptB], cost=120.0)
                btmg, bbtmg = m.btm_ring.next()
                S.op("act", lambda e, ptBb=ptBb, btmg=btmg: e.copy(out=btmg, in_=ptBb), reads=[bptB], writes=[bbtmg],
                     cost=350.0)
                R, bR = m.R_ring.next()
                _tt(S, "dve", R, m.Tmat[:, :].unsqueeze(1).broadcast_to([128, MHG, 128]),
                    m.dA[:, ck, hs].unsqueeze(2).broadcast_to([128, MHG, 128]), ALU.mult, [m.BdA[ck], c.Bconst], [bR])
                R2 = R.rearrange("p h l -> p (h l)")
                LT, bLT = m.LT_ring.next()
                ecs, becs = m.ecs_ring.next()
                for half in range(2):
                    hsl = slice(half * 512, (half + 1) * 512)
                    p1, bp1 = c.bank()
                    _mm(S, p1, m.UmatR[:, :], R2[:, hsl], True, True, [bR, c.Bconst], [bp1])
                    _act(S, LT.rearrange("p h l -> p (h l)")[:, hsl], p1, AF.Exp, [bp1], [bLT])
                    p2, bp2 = c.bank()
                    _mm(S, p2, m.onesR[:, :], R2[:, hsl], True, True, [bR, c.Bconst], [bp2])
                    _act(S, ecs.rearrange("p h l -> p (h l)")[:, hsl], p2, AF.Exp, [bp2], [becs])
                pg, bpg = c.qbank()
                _mm(S, pg, m.BT[:, g, csl], m.CT[:, g, csl], True, True, [m.BBT[g], m.BCT[g]], [bpg])
                Gm, bGm = m.Gm_ring.next()
                _tt(S, "dve", Gm, pg, m.maskc[:, :], ALU.mult, [bpg, c.Bconst], [bGm])
                MTt, bMT = m.MT_ring.next()
                _tt(S, "pool", MTt, LT, Gm.unsqueeze(1).broadcast_to([128, MHG, 128]), ALU.mult, [bLT, bGm], [bMT])
                Cp, bCp = m.Cp_ring.next()
                _tt(S, "pool", Cp, ecs, m.CT[:, g, csl].unsqueeze(1).broadcast_to([128, MHG, 128]), ALU.mult,
                    [becs, m.BCT[g]], [bCp])
                py, bpy = c.bank()
                for i in range(4):
                    ci = g * 4 + i
                    o = py[:, i * 128:(i + 1) * 128]
                    for a in range(2):
                        _mm(S, o, xpg[:, 2 * i + a, :], MTt[:, 2 * i + a, :], a == 0, False, [bxpg, bMT], [bpy])
                    for a in range(2):
                        _mm(S, o, m.ppad[:, 2 * ci + a, :], Cp[:, 2 * i + a, :], False, a == 1, [m.Bppad[g], bCp], [bpy])
                pu, bpu = c.bank()
                _mm(S, pu, btmg, xddg, True, True, [bbtmg, bxddg], [bpu])
                pv = m.prev[:, jm, g * 512:(g + 1) * 512]
                _tt(S, "dve", pv.rearrange("p (h q) -> p h q", h=MHG), pv.rearrange("p (h q) -> p h q", h=MHG),
                    ecs[:, :, 127:128].broadcast_to([128, MHG, 64]), ALU.mult, [m.Bprev[jm][g], becs], [m.Bprev[jm][g]])
                _tt(S, "dve", pv, pv, pu, ALU.add, [m.Bprev[jm][g], bpu], [m.Bprev[jm][g]])
                for i in range(4):
                    ci = g * 4 + i
                    S.op("act", lambda e, ci=ci: e.copy(
                        out=_pad_ap(m.ppad[:, 2 * ci:2 * ci + 2, :].rearrange("p a b -> p (a b)")),
                        in_=m.prev[:, jm, ci * 128:(ci + 1) * 128].rearrange("p (a b) -> p a b", a=2)),
                        reads=[m.Bprev[jm][g]], writes=[m.Bppad[g]])
                yg, byg = m.yg_ring.next()
                for i in range(4):
                    ci = g * 4 + i
                    S.op("dve", lambda e, i=i, ci=ci, yg=yg, py=py, csl=csl: e.scalar_tensor_tensor(
                        out=yg[:, i, :], in0=m.xc[:, ci, csl], scalar=m.dcol[:, jm, ci:ci + 1],
                        in1=py[:, i * 128:(i + 1) * 128], op0=ALU.mult, op1=ALU.add),
                        reads=[m.Bxc[ci], bpy, c.Bconst], writes=[byg])
                _tt(S, "dve", yg, yg, m.siluz[:, g * 4:(g + 1) * 4, csl], ALU.mult,
                    [byg] + [m.Bsz[g * 4 + i] for i in range(4)], [byg])
                ysq, bysq = c.sg_ring.next()
                _act(S, ysq, yg.rearrange("p i l -> p (i l)"), AF.Square, [byg], [bysq])
                pn, bpn = c.qbank()
                for i in range(4):
                    _mm(S, pn, c.ones_bf[:, :], ysq[:, i * 128:(i + 1) * 128], i == 0, i == 3, [bysq, c.Bconst], [bpn])
                rs, brs = m.rs_ring.next()
                _act(S, rs, pn, AF.Ln, [bpn, c.Bconst], [brs], scale=1.0 / 512, bias=c.eps_col[:, 0:1])
                _act(S, rs, rs, AF.Exp, [brs], [brs], scale=-0.5)
                for i in range(4):
                    ci = g * 4 + i
                    S.op("dve", lambda e, i=i, ci=ci, yg=yg, rs=rs, csl=csl: e.scalar_tensor_tensor(
                        out=m.ynT[:, ci, csl], in0=yg[:, i, :], scalar=m.mnw[:, jm, ci:ci + 1], in1=rs,
                        op0=ALU.mult, op1=ALU.mult),
                        reads=[byg, brs, c.Bconst], writes=[m.Byn[ci]])
        for dc in range(KD):
            w, bw = c.dn_ring.next()
            S.dma("pool", w[:, 0:16, :], c.d_mwo[jm, dc], writes=[bw])
            ps, bps = c.bank()
            for k in range(16):
                _mm(S, ps[:, 0:MT_], w[:, k, :], m.ynT[:, k, :], k == 0, k == 15, [bw, m.Byn[k]], [bps])
            _tt(S, "dve", c.xT[:, dc, tsl], ps[:, 0:MT_], c.xT[:, dc, tsl], ALU.add, [bps, c.BxT[dc]], [c.BxT[dc]])


def emit_mamba_consts(S, c):
    m = c.m
    B = [c.Bconst]
    S.dma("sp", m.mconv[:], c.d_mconv, writes=B)
    S.dma("sp", m.dcol[:], c.d_mdcol, writes=B)
    S.dma("sp", m.mnw[:], c.d_mnw, writes=B)
    S.dma("pool", m.wdt[:], c.d_mwdt, writes=[m.Bwdt])
    for j in range(2):
        S.dma("sp", m.dtb[:, j, :], c.d_mdtb[j:j + 1, :].partition_broadcast(128), writes=B)
        S.dma("sp", m.Aneg[:, j, :], c.d_malog[j:j + 1, :].partition_broadcast(128), writes=B)
    _act(S, m.Aneg[:], m.Aneg[:], AF.Exp, B, B)
    S.op("dve", lambda e: e.tensor_scalar(out=m.Aneg[:], in0=m.Aneg[:], scalar1=-1.0, scalar2=None, op0=ALU.mult),
         reads=B, writes=B)
    S.op("pool", lambda e: e.memset(m.onesf[:, :], 1.0), writes=B)
    S.op("pool", lambda e: e.memset(m.Tmat[:, :], 1.0), writes=B)
    S.op("pool", lambda e: e.affine_select(out=m.Tmat[:, :], in_=m.Tmat[:, :], pattern=[[1, 128]],
                                           compare_op=ALU.is_ge, fill=0.0, base=0, channel_multiplier=-1),
         reads=B, writes=B)
    S.op("pool", lambda e: e.memset(m.Umat[:, :], 1.0), writes=B)
    S.op("pool", lambda e: e.affine_select(out=m.Umat[:, :], in_=m.Umat[:, :], pattern=[[-1, 128]],
                                           compare_op=ALU.is_gt, fill=0.0, base=0, channel_multiplier=1),
         reads=B, writes=B)
    S.op("dve", lambda e: e.tensor_copy(out=m.UmatR[:, :], in_=m.Umat[:, :]), reads=B, writes=B)
    S.op("dve", lambda e: e.tensor_copy(out=m.onesR[:, :], in_=m.onesf[:, :]), reads=B, writes=B)
    S.op("pool", lambda e: e.memset(m.prev[:], 0.0), writes=[b for r in m.Bprev for b in r])
    S.op("pool", lambda e: e.memset(m.tails[:], 0.0), writes=[b for r in m.Btail for b in r])


def build_program(seq=SEQ, depth=DEPTH, plan=None):
    nc = bass.Bass("TRN2", target_bir_lowering=False)
    c = Ctx()
    c.nc = nc
    ntiles = seq // T
    dt = lambda name, shape: nc.dram_tensor(name, shape, F32, kind="ExternalInput").ap()
    c.d_x = dt("xT", [D, seq])
    c.d_wgu = dt("wgu", [DEPTH, 2, KF, 128, 2, KD, 128])
    c.d_wd = dt("wd", [DEPTH, 2, KD, 128, KF, 128])
    NWC = (DEPTH * 3 + 1) * KD
    c.d_nw = dt("nw", [128, NWC])
    c.d_hwqf = dt("hwqf", [2, NHD, 128, 2, KD, 128])
    c.d_hwg = dt("hwg", [2, NHD // 2, 128, 2, KD, 128])
    c.d_hwv = dt("hwv", [2, NHD // 2, 128, 2, KD, 128])
    c.d_hwo = dt("hwo", [2, KD, 128, KD, 128])
    c.d_hlb = dt("hlb", [128, 4 * NHD])
    c.d_hnw = dt("hnw", [128, 2])
    c.d_mwin = dt("mwin", [2, 20, 128, 2, KD, 128])
    c.d_mwdt = dt("mwdt", [128, 2, KD, 32])
    c.d_mconv = dt("mconv", [128, 2, 24, 5])
    c.d_mdcol = dt("mdcol", [128, 2, 16])
    c.d_mnw = dt("mnw", [128, 2, 16])
    c.d_mdtb = dt("mdtb", [2, 32])
    c.d_malog = dt("malog", [2, 32])
    c.d_mwo = dt("mwo", [2, KD, 128, 16, 128])
    c.d_out = nc.dram_tensor("outT", [D, seq], F32, kind="ExternalOutput").ap()
    from contextlib import ExitStack
    with ExitStack() as st:
        sb = lambda n, s, d: st.enter_context(nc.sbuf_tensor(n, s, d))
        c.xT = sb("xT_sb", [128, KD, T], F32)
        c.uT = sb("uT_sb", [128, KD, T], BF16)
        c.rstd = sb("rstd_sb", [128, T], F32)
        c.nw = sb("nw_sb", [128, NWC], F32)
        c.ones_bf = sb("ones_bf", [128, 128], BF16)
        c.ident = sb("ident_bf", [128, 128], BF16)
        c.eps_col = sb("eps_col", [128, 1], F32)
        gu = sb("gu_ring", [128, 4, 2, KD, 128], BF16)
        dn = sb("dn_ring", [128, 2, KF, 128], BF16)
        sq = sb("sq_ring", [128, 2, T], BF16)
        sg = sb("sg_ring", [128, 3, 512], BF16)
        c.gu_ring = Ring("gu", [gu[:, i] for i in range(4)])
        c.dn_ring = Ring("dn", [dn[:, i] for i in range(2)])
        c.sq_ring = Ring("sq", [sq[:, i] for i in range(2)])
        c.sg_ring = Ring("sg", [sg[:, i] for i in range(3)])
        h = c.h = Ctx()
        h.S = sb("h_S", [128, 2, NHD, 128], F32)
        h.sbf = [sb("h_sbf%d" % i, [128, 2, NHD, 128], BF16) for i in range(2)]
        h.hconst = sb("h_const", [128, 2 * 3 * NHD], F32)
        h.lbtmp = sb("h_lbtmp", [128, 7 * NHD], F32)
        h.hnw = sb("h_nw", [128, 2], F32)
        h.mask32 = sb("h_mask32", [128, TM], F32)
        h.cmask = sb("h_cmask", [128, 4], F32)
        h.maskbd = sb("h_maskbd", [128, 128], F32)
        h.BS = [[Buf("hS%d_%d" % (j, i)) for i in range(NHD)] for j in range(2)]
        h.Bsbf = [[[Buf("hsbf%d_%d_%d" % (p, j, i)) for i in range(NHD)] for j in range(2)] for p in range(2)]
        h.par = [[0] * NHD for _ in range(2)]
        ARENA = 76 * 1024
        arena = sb("arena", [128, ARENA], mybir.dt.uint8)
        off = [0]

        def carve(shape, dtype, reset=False):
            if reset:
                off[0] = 0
            n = int(np.prod(shape)) * (2 if dtype == BF16 else 4)
            ap = arena[:, off[0]:off[0] + n].bitcast(dtype)
            off[0] += n
            assert off[0] <= ARENA, (off[0], ARENA)
            if len(shape) > 1:
                names = " ".join("a%d" % i for i in range(len(shape)))
                kw = {"a%d" % i: shape[i] for i in range(len(shape))}
                ap = ap.rearrange("p (%s) -> p %s" % (names, names), **kw)
            return ap

        c.actT = carve([KF, T], BF16, reset=True)
        NB = TM // 128
        h.qT = carve([NHD, TM], BF16, reset=True)
        h.kT = carve([NHD, TM], BF16)
        h.sgT = carve([NHD, TM], BF16)
        h.ogT = h.sgT
        h.vall = carve([NB, 1024], BF16)
        h.vm = carve([4, 1024], BF16)
        h.kdtm = carve([NB, NHD, 128], BF16)
        h.attn = carve([NHD, 128], BF16)
        h.elast = carve([NHD, TM // HC], F32)
        tmp = carve([11, TM], F32)
        kd = carve([2, TM], BF16)
        h.tmp = Ring("htmp", [tmp[:, i, :] for i in range(11)])
        h.kd_ring = Ring("hkd", [kd[:, i, :] for i in range(2)])
        h.BqT = [Buf("hq%d" % i) for i in range(NHD)]
        h.BkT = [Buf("hk%d" % i) for i in range(NHD)]
        h.BsgT = [Buf("hsg%d" % i) for i in range(NHD)]
        h.Bvall = [Buf("hvall%d" % i) for i in range(NB)]
        h.Bvm = [[Buf("hvm%d_%d" % (i, j)) for j in range(2)] for i in range(4)]
        h.Bkdtm = [[Buf("hkdtm%d_%d" % (b, i)) for i in range(NHD)] for b in range(NB)]
        h.Battn = [Buf("hattn%d" % i) for i in range(2)]
        h.Bel = [Buf("hel%d" % i) for i in range(NHD)]


        m = c.m = Ctx()
        m.prev = sb("m_prev", [128, 2, 2048], F32)
        m.tails = sb("m_tails", [128, 2, 24, 3], F32)
        m.mconv = sb("m_conv", [128, 2, 24, 5], F32)
        m.dcol = sb("m_dcol", [128, 2, 16], F32)
        m.mnw = sb("m_nw", [128, 2, 16], F32)
        m.wdt = sb("m_wdt", [128, 2, KD, 32], BF16)
        m.dtb = sb("m_dtb", [128, 2, 32], F32)
        m.Aneg = sb("m_Aneg", [128, 2, 32], F32)
        m.onesf = sb("m_onesf", [128, 128], F32)
        m.Tmat = sb("m_Tmat", [128, 128], F32)
        m.Umat = sb("m_Umat", [128, 128], F32)
        m.UmatR = sb("m_UmatR", [128, 128], F32R)
        m.onesR = sb("m_onesR", [128, 128], F32R)
        m.Rsb = sb("m_R", [128, 1, MHG, 128], F32R)
        m.maskc = m.Tmat
        m.siluz = carve([16, MT_], BF16, reset=True)
        m.ynT = m.siluz
        m.xc = carve([16, MT_], BF16)
        m.BT = carve([MG, MT_], BF16)
        m.CT = carve([MG, MT_], BF16)
        m.ppad = carve([32, 128], BF16)
        NCK_ = MT_ // 128
        m.dt = carve([NCK_, 32], F32)
        m.dA = carve([NCK_, 32], F32)
        m.dtdec = carve([NCK_, 32], F32)
        mk = lambda name, n, shape, dtype: Ring(name, [carve(shape, dtype) for _ in range(n)])
        m.xpad_ring = mk("mxpad", 3, [MHG, 128], BF16)
        m.xdd_ring = mk("mxdd", 3, [512], BF16)
        m.btm_ring = mk("mbtm", 3, [128], BF16)
        m.R_ring = Ring("mR", [m.Rsb[:, 0]])
        m.LT_ring = mk("mLT", 2, [MHG, 128], BF16)
        m.ecs_ring = mk("mecs", 2, [MHG, 128], F32)
        m.Gm_ring = mk("mGm", 2, [128], BF16)
        m.MT_ring = mk("mMT", 2, [MHG, 128], BF16)
        m.Cp_ring = mk("mCp", 2, [MHG, 128], BF16)
        m.yg_ring = mk("myg", 2, [4, 128], F32)
        m.rs_ring = mk("mrs", 2, [128], F32)
        m.raw_ring = mk("mraw", 2, [MT_ + 4], F32)
        m.acc_ring = mk("macc", 2, [MT_], F32)
        m.Bsz = [Buf("msz%d" % i) for i in range(16)]
        m.Bxc = [Buf("mxc%d" % i) for i in range(16)]
        m.BBT = [Buf("mBT%d" % i) for i in range(MG)]
        m.BCT = [Buf("mCT%d" % i) for i in range(MG)]
        m.Byn = m.Bsz
        m.Bwdt = Buf("mwdt")
        m.Bppad = [Buf("mppad%d" % i) for i in range(MG)]
        m.Bprev = [[Buf("mprev%d_%d" % (j, g)) for g in range(MG)] for j in range(2)]
        m.Btail = [[Buf("mtail%d_%d" % (j, i)) for i in range(24)] for j in range(2)]
        m.Bdt = [Buf("mdt%d" % i) for i in range(NCK_)]
        m.BdA = [Buf("mdA%d" % i) for i in range(NCK_)]
        m.Bdtdec = [Buf("mdtdec%d" % i) for i in range(NCK_)]

        psum = st.enter_context(nc.psum_tensor("psum", [128, 8, 512], F32))
        c.bank8 = Ring("bank", [psum[:, i, :] for i in range(8)])
        c.bank4 = Ring("bankm", [psum[:, i, :] for i in range(4)])
        c.qbank_ring = Ring("qbank", [psum[:, 4 + i, 0:128] for i in range(4)])
        c.qbank = c.qbank_ring.next
        c.bank = c.bank8.next
        c.BxT = [Buf("xT%d" % k) for k in range(KD)]
        c.BuT = [[Buf("uT%d_%d" % (k, hh)) for hh in range(NH)] for k in range(KD)]
        c.Bact = [[Buf("act%d_%d" % (k, hh)) for hh in range(NH)] for k in range(KF)]
        c.Brstd = [Buf("rstd%d" % hh) for hh in range(NH)]
        c.Bconst = Buf("const")
        c.Bout = Buf("out")
        flat = lambda x: [b for r in x for b in (flat(r) if isinstance(r, list) else [r])]
        c.phase_bufs = {
            "ffn": flat(c.Bact) + c.bank8.bufs,
            "hgrn": flat([h.BqT, h.BkT, h.BsgT, h.Bvall, h.Bvm, h.Bkdtm, h.Battn, h.Bel])
                    + h.tmp.bufs + h.kd_ring.bufs + c.bank4.bufs + c.qbank_ring.bufs,
            "mamba": flat([m.Bsz, m.Bxc, m.BBT, m.BCT, m.Bppad, m.Bdt, m.BdA, m.Bdtdec])
                     + sum([r.bufs for r in (m.xpad_ring, m.xdd_ring, m.btm_ring,
                                             m.R_ring, m.LT_ring, m.ecs_ring, m.Gm_ring, m.MT_ring, m.Cp_ring,
                                             m.yg_ring, m.rs_ring, m.raw_ring, m.acc_ring)], [])
                     + c.bank4.bufs + c.qbank_ring.bufs,
        }
        c.cur_phase = None
        S = Sched(nc)
        S.op("pool", lambda e: e.memset(c.ones_bf[:], 1.0), writes=[c.Bconst])
        S.op("pool", lambda e: e.memset(c.eps_col[:], EPS), writes=[c.Bconst])
        S.dma("sp", c.nw[:], c.d_nw, writes=[c.Bconst])
        emit_hgrn_consts(S, c)
        emit_mamba_consts(S, c)
        if plan is None:
            plan = ["ffn0", "mix", "ffn1"]
        for tile in range(ntiles):
            for k in range(KD):
                S.dma("sp", c.xT[:, k, :], c.d_x[k * 128:(k + 1) * 128, tile * T:(tile + 1) * T],
                      writes=[c.BxT[k]])
            for l in range(depth):
                for stg in plan:
                    if stg == "ffn0":
                        emit_ffn(S, c, l, 0)
                    elif stg == "ffn1":
                        emit_ffn(S, c, l, 1)
                    elif stg == "mix":
                        if l % 2 == 1:
                            emit_hgrn(S, c, l, l // 2)
                        else:
                            emit_mamba(S, c, l, l // 2)
                    elif stg == "mamba":
                        emit_mamba(S, c, l, l // 2)
                    elif stg == "hgrn":
                        emit_hgrn(S, c, l, l // 2)
            emit_final(S, c, tile)
        c.sbuf_left = nc.sbuf_bytes_remaining
        S.emit()
        c.makespan = S.makespan
    nc._ctx = c
    return nc


def prep_common(inputs):
    f32 = np.float32
    A = lambda k: np.asarray(inputs[k], f32)
    g = A("ffn_w_gate").reshape(DEPTH, 2, KD, 128, KF, 128)
    u = A("ffn_w_up").reshape(DEPTH, 2, KD, 128, KF, 128)
    wgu = np.stack([g, u], axis=0)
    wgu = np.ascontiguousarray(wgu.transpose(1, 2, 5, 4, 0, 3, 6))
    wd = A("ffn_w_down").reshape(DEPTH, 2, KF, 128, KD, 128)
    wd = np.ascontiguousarray(wd.transpose(0, 1, 4, 3, 2, 5))
    nw = np.concatenate([A("norm_w").reshape(DEPTH * 3, KD, 128),
                         A("final_norm_w").reshape(1, KD, 128)], axis=0)
    nw = np.ascontiguousarray(nw.transpose(2, 0, 1).reshape(128, -1))
    out = {"wgu": wgu, "wd": wd, "nw": nw}
    hw = A("h_w_in").reshape(2, KD, 128, 4, NHD, 128)
    qf = hw[:, :, :, 0:2]
    out["hwqf"] = np.ascontiguousarray(qf.transpose(0, 4, 2, 3, 1, 5))
    gg = hw[:, :, :, 3].reshape(2, KD, 128, NHD // 2, 2, 128)
    out["hwg"] = np.ascontiguousarray(gg.transpose(0, 3, 2, 4, 1, 5))
    vv = hw[:, :, :, 2].reshape(2, KD, 128, NHD // 2, 2, 128)
    out["hwv"] = np.ascontiguousarray(vv.transpose(0, 3, 2, 4, 1, 5))
    wo = A("h_w_out").reshape(2, KD, 128, KD, 128)
    out["hwo"] = np.ascontiguousarray(wo.transpose(0, 3, 2, 1, 4))
    lb = A("h_lb_logits").reshape(DEPTH, NHD, 128)
    out["hlb"] = np.ascontiguousarray(lb.transpose(2, 0, 1).reshape(128, -1))
    out["hnw"] = np.ascontiguousarray(A("h_norm_w").T)
    mw = A("m_w_in")
    win = mw[:, :, 0:5120].reshape(2, KD, 128, 20, 2, 128)
    out["mwin"] = np.ascontiguousarray(win.transpose(0, 3, 2, 4, 1, 5))
    wdt = mw[:, :, 5120:5152].reshape(2, KD, 128, 32)
    out["mwdt"] = np.ascontiguousarray(wdt.transpose(2, 0, 1, 3))
    cw = A("m_conv_w").reshape(2, 4, 24, 128)
    cb = A("m_conv_b").reshape(2, 1, 24, 128)
    out["mconv"] = np.ascontiguousarray(np.concatenate([cw, cb], axis=1).transpose(3, 0, 2, 1))
    dcol = np.repeat(A("m_d"), 64, axis=1).reshape(2, 16, 128)
    out["mdcol"] = np.ascontiguousarray(dcol.transpose(2, 0, 1))
    out["mnw"] = np.ascontiguousarray(A("m_norm_w").reshape(2, 16, 128).transpose(2, 0, 1))
    out["mdtb"] = np.ascontiguousarray(A("m_dt_bias"))
    out["malog"] = np.ascontiguousarray(A("m_a_log"))
    mwo = A("m_w_out").reshape(2, 16, 128, KD, 128)
    out["mwo"] = np.ascontiguousarray(mwo.transpose(0, 3, 2, 1, 4))
    return out


_NC_CACHE = {}


def kernel(**inputs):
    x = np.asarray(inputs["x"], np.float32)
    B = x.shape[0]
    common = prep_common(inputs)
    if "nc" not in _NC_CACHE:
        _NC_CACHE["nc"] = build_program()
    nc = _NC_CACHE["nc"]
    in_maps = []
    for b in range(B):
        m = dict(common)
        m["xT"] = np.ascontiguousarray(x[b].T)
        in_maps.append(m)
    res = run_bass_kernel_spmd(nc, in_maps, core_ids=list(range(B)))
    out = np.stack([np.ascontiguousarray(r["outT"].T) for r in res.results], axis=0)
    return out.astype(np.float32)
```

```python
import heapq
import numpy as np
import concourse.bass as bass
import concourse.mybir as mybir
from concourse.bass_utils import run_bass_kernel_spmd

F32 = mybir.dt.float32
BF16 = mybir.dt.bfloat16
AF = mybir.ActivationFunctionType
ALU = mybir.AluOpType

ENGS = ("pe", "act", "dve", "pool", "sp")
HOP_NS = 250.0
WINDOW = 48


class Buf:
    __slots__ = ("name", "lw", "rd", "dsem", "dcnt", "dlast")

    def __init__(self, name):
        self.name = name
        self.lw = None
        self.rd = []
        self.dsem = None
        self.dcnt = 0
        self.dlast = None


class Op:
    __slots__ = ("eng", "fn", "deps", "marked", "dma", "idx", "cost", "seq", "fin", "pos", "dcount")


class Sched:
    def __init__(self, nc):
        self.nc = nc
        self.ops = {e: [] for e in ENGS}
        self.dma_bufs = []
        self.nseq = 0

    def _record(self, eng, fn, reads, writes, dma=None, cost=500.0):
        op = Op()
        op.eng = eng
        op.fn = fn
        op.marked = False
        op.dma = dma
        op.cost = cost
        op.seq = self.nseq
        self.nseq += 1
        deps = []
        for b in reads:
            if b.lw is not None:
                deps.append(b.lw)
        for b in writes:
            if b.lw is not None:
                deps.append(b.lw)
            deps.extend(b.rd)
        if dma is not None and dma.dlast is not None:
            deps.append(dma.dlast)
        seen = set()
        out = []
        for d in deps:
            if id(d[1]) in seen:
                continue
            seen.add(id(d[1]))
            out.append(d)
        op.deps = out
        self.ops[eng].append(op)
        if dma is not None:
            if dma.dsem is None:
                self.dma_bufs.append(dma)
                dma.dsem = True
            dma.dcnt += 16
            op.dcount = dma.dcnt
            tok = ("d", op)
            dma.dlast = tok
        else:
            op.dcount = 0
            tok = ("e", op)
        for b in reads:
            b.rd.append(tok)
        for b in writes:
            b.lw = tok
            b.rd = []
        return op

    DEFAULT_COST = {"pe": 150.0, "act": 700.0, "dve": 700.0, "pool": 1500.0, "sp": 100.0}

    def op(self, eng, fn, reads=(), writes=(), cost=None):
        if cost is None:
            cost = self.DEFAULT_COST[eng]
        return self._record(eng, fn, reads, writes, cost=cost)

    @staticmethod
    def fence(old_bufs, new_bufs):
        toks = {}
        for b in old_bufs:
            if b.lw is not None:
                toks[id(b.lw[1])] = b.lw
            for t in b.rd:
                toks[id(t[1])] = t
        toks = list(toks.values())
        for b in new_bufs:
            b.rd.extend(toks)

    def dma(self, eng, out, in_, reads=(), writes=(), sem=None, nbytes=65536):
        if sem is None:
            sem = (list(writes) + list(reads))[0]
        return self._record(eng, lambda e: e.dma_start(out=out, in_=in_), reads, writes, dma=sem,
                            cost=2000.0 + nbytes / 100.0)

    def _list_schedule(self):
        pend = {e: list(self.ops[e]) for e in ENGS}
        head = {e: 0 for e in ENGS}
        free = {e: 0.0 for e in ENGS}
        order = {e: [] for e in ENGS}
        for e in ENGS:
            for op in pend[e]:
                op.fin = None
        total = sum(len(v) for v in pend.values())
        done = 0
        issue = {"sp": 60.0, "pool": 700.0}
        while done < total:
            best = None
            for e in ENGS:
                lst = pend[e]
                h = head[e]
                n = len(lst)
                while h < n and lst[h] is None:
                    h += 1
                head[e] = h
                cnt = 0
                i = h
                fe = free[e]
                while i < n and cnt < WINDOW:
                    op = lst[i]
                    if op is not None:
                        cnt += 1
                        rdy = 0.0
                        ok = True
                        for d in op.deps:
                            f = d[1].fin
                            if f is None:
                                ok = False
                                break
                            if d[1].eng != e or d[0] == "d":
                                f += HOP_NS
                            if f > rdy:
                                rdy = f
                        if ok:
                            st = rdy if rdy > fe else fe
                            key = (st, op.seq)
                            if best is None or key < best[0]:
                                best = (key, e, i, op)
                            if rdy <= fe:
                                break
                    i += 1
            key, e, i, op = best
            st = key[0]
            if op.dma is not None:
                free[e] = st + issue.get(e, 100.0)
                op.fin = st + op.cost
            else:
                free[e] = st + op.cost
                op.fin = free[e]
            pend[e][i] = None
            order[e].append(op)
            done += 1
        self.makespan = max(free.values())
        return order

    def emit(self, reorder=True):
        nc = self.nc
        order = self._list_schedule() if reorder else self.ops
        for e in ENGS:
            for i, op in enumerate(order[e]):
                op.pos = i
        def needs_wait(op, d):
            if d[0] == "d":
                return True
            p = d[1]
            if p.eng == op.eng and op.dma is None and op.eng == "pe":
                return False
            return True
        for e in ENGS:
            for op in order[e]:
                for d in op.deps:
                    if d[0] == "e" and needs_wait(op, d):
                        d[1].marked = True
        count = {}
        for e in ENGS:
            cnt = 0
            for op in order[e]:
                if op.marked:
                    cnt += 1
                count[id(op)] = cnt
        from contextlib import ExitStack
        with ExitStack() as st:
            esem = {e: st.enter_context(nc.semaphore("sem_" + e)) for e in ENGS}
            for i, b in enumerate(self.dma_bufs):
                b.dsem = st.enter_context(nc.semaphore("dsem%d" % i))
            block = st.enter_context(nc.Block())

            def run(e, eng):
                seen = {}
                for op in order[e]:
                    for d in op.deps:
                        if not needs_wait(op, d):
                            continue
                        p = d[1]
                        if d[0] == "e":
                            sem = esem[p.eng]
                            val = count[id(p)]
                        else:
                            sem = p.dma.dsem
                            val = p.dcount
                        k = id(sem)
                        if seen.get(k, 0) >= val:
                            continue
                        seen[k] = val
                        eng.wait_ge(sem, val)
                    ins = op.fn(eng)
                    if op.dma is not None:
                        ins.then_inc(op.dma.dsem, 16)
                    elif op.marked:
                        ins.then_inc(esem[e], 1)

            final = [(b.dsem, b.dcnt) for b in self.dma_bufs]

            @block.tensor
            def _(eng):
                run("pe", eng)

            @block.scalar
            def _(eng):
                run("act", eng)

            @block.vector
            def _(eng):
                run("dve", eng)

            @block.gpsimd
            def _(eng):
                run("pool", eng)

            @block.sync
            def _(eng):
                run("sp", eng)
                for sem, cnt in final:
                    eng.wait_ge(sem, cnt)


D = 1024
KD = D // 128
DFF = 2816
KF = DFF // 128
SEQ = 4096
DEPTH = 4
EPS = 1e-6
T = 1024
NH = T // 512
TM = 512
F32R = mybir.dt.float32r
NHD = 8
HC = 64
NCB = 128 // HC


class Ring:
    def __init__(self, name, aps):
        self.aps = aps
        self.bufs = [Buf("%s%d" % (name, i)) for i in range(len(aps))]
        self.i = 0

    def next(self):
        i = self.i
        self.i = (i + 1) % len(self.aps)
        return self.aps[i], self.bufs[i]


class Ctx:
    pass


def _n(ap):
    n = 1
    for s in ap.shape[1:]:
        n *= int(s)
    return n


def _mm(S, out, lhsT, rhs, start, stop, reads, writes):
    n = _n(rhs)
    cost = 25.0 + max(n, 64) / 2.4 * (4.0 if lhsT.dtype == F32 else 1.0)
    S.op("pe", lambda e: e.matmul(out, lhsT=lhsT, rhs=rhs, start=start, stop=stop),
         reads=reads, writes=writes, cost=cost)


def _act(S, out, in_, func, reads, writes, **kw):
    S.op("act", lambda e: e.activation(out=out, in_=in_, func=func, **kw), reads=reads, writes=writes,
         cost=230.0 + 0.95 * _n(out))


def _tt(S, eng, out, in0, in1, op, reads, writes):
    n = _n(out)
    cost = (120.0 + 1.4 * n) if eng == "dve" else (350.0 + 2.2 * n)
    S.op(eng, lambda e: e.tensor_tensor(out=out, in0=in0, in1=in1, op=op), reads=reads, writes=writes, cost=cost)


def emit_rmsnorm(S, c, wcol):
    banks = [c.bank() for _ in range(NH)]
    for k in range(KD):
        sq, bsq = c.sq_ring.next()
        _act(S, sq, c.xT[:, k, :], AF.Square, [c.BxT[k]], [bsq])
        for h in range(NH):
            ps, bps = banks[h]
            _mm(S, ps, c.ones_bf[:, :], sq[:, h * 512:(h + 1) * 512], k == 0, k == KD - 1,
                [bsq, c.Bconst], [bps])
    for h in range(NH):
        ps, bps = banks[h]
        sl = slice(h * 512, (h + 1) * 512)
        _act(S, c.rstd[:, sl], ps, AF.Ln, [bps, c.Bconst], [c.Brstd[h]], scale=1.0 / D, bias=c.eps_col[:, 0:1])
        _act(S, c.rstd[:, sl], c.rstd[:, sl], AF.Exp, [c.Brstd[h]], [c.Brstd[h]], scale=-0.5)
    for k in range(KD):
        for h in range(NH):
            sl = slice(h * 512, (h + 1) * 512)
            S.op("dve", lambda e, k=k, sl=sl: e.scalar_tensor_tensor(
                out=c.uT[:, k, sl], in0=c.xT[:, k, sl], scalar=c.nw[:, wcol + k:wcol + k + 1],
                in1=c.rstd[:, sl], op0=ALU.mult, op1=ALU.mult),
                reads=[c.BxT[k], c.Brstd[h], c.Bconst], writes=[c.BuT[k][h]])


def enter_phase(S, c, name):
    new = c.phase_bufs[name]
    if c.cur_phase is not None and c.cur_phase != name:
        S.fence(c.phase_bufs[c.cur_phase], new)
    c.cur_phase = name
    if name == "ffn":
        c.bank = c.bank8.next
    else:
        c.bank = c.bank4.next


def emit_ffn(S, c, l, j):
    enter_phase(S, c, "ffn")
    emit_rmsnorm(S, c, (l * 3 + (0 if j == 0 else 2)) * KD)
    for fc in range(KF):
        w, bw = c.gu_ring.next()
        S.dma("pool", w, c.d_wgu[l, j, fc], writes=[bw])
        bk = [[c.bank() for _ in range(NH)] for _ in range(2)]
        for k in range(KD):
            for g in range(2):
                for h in range(NH):
                    ps, bps = bk[g][h]
                    _mm(S, ps, w[:, g, k, :], c.uT[:, k, h * 512:(h + 1) * 512], k == 0, k == KD - 1,
                        [bw, c.BuT[k][h]], [bps])
        for h in range(NH):
            sl = slice(h * 512, (h + 1) * 512)
            sg, bsg = c.sg_ring.next()
            pg, bpg = bk[0][h]
            pu, bpu = bk[1][h]
            _act(S, sg, pg, AF.Silu, [bpg], [bsg])
            _tt(S, "dve", c.actT[:, fc, sl], pu, sg, ALU.mult, [bpu, bsg], [c.Bact[fc][h]])
    for dc in range(KD):
        w, bw = c.dn_ring.next()
        S.dma("pool", w, c.d_wd[l, j, dc], writes=[bw])
        bk = [c.bank() for _ in range(NH)]
        for k in range(KF):
            for h in range(NH):
                ps, bps = bk[h]
                _mm(S, ps, w[:, k, :], c.actT[:, k, h * 512:(h + 1) * 512], k == 0, k == KF - 1,
                    [bw, c.Bact[k][h]], [bps])
        for h in range(NH):
            sl = slice(h * 512, (h + 1) * 512)
            ps, bps = bk[h]
            S.op("dve", lambda e, ps=ps, dc=dc, sl=sl: e.scalar_tensor_tensor(
                out=c.xT[:, dc, sl], in0=ps, scalar=0.5, in1=c.xT[:, dc, sl],
                op0=ALU.mult, op1=ALU.add),
                reads=[bps, c.BxT[dc]], writes=[c.BxT[dc]])


def emit_final(S, c, tile):
    enter_phase(S, c, "ffn")
    wc = DEPTH * 3 * KD
    emit_rmsnorm(S, c, wc)
    for k in range(KD):
        for h in range(NH):
            sl = slice(h * 512, (h + 1) * 512)
            S.op("dve", lambda e, k=k, sl=sl: e.scalar_tensor_tensor(
                out=c.xT[:, k, sl], in0=c.xT[:, k, sl], scalar=c.nw[:, wc + k:wc + k + 1],
                in1=c.rstd[:, sl], op0=ALU.mult, op1=ALU.mult),
                reads=[c.BxT[k], c.Brstd[h], c.Bconst], writes=[c.BxT[k]])
        S.dma("sp", c.d_out[k * 128:(k + 1) * 128, tile * T:(tile + 1) * T], c.xT[:, k, :],
              reads=[c.BxT[k]])


def emit_hgrn(S, c, l, jm):
    enter_phase(S, c, "hgrn")
    emit_rmsnorm(S, c, (l * 3 + 1) * KD)
    h = c.h
    NB = TM // 128
    NCH = TM // HC
    for sub in range(T // TM):
        t0 = sub * TM
        tsl = slice(t0, t0 + TM)
        hh = sub % NH if TM == 512 else None
        rdu = lambda k: [c.BuT[k][(t0 // 512)]]
        for vp in range(NHD // 2):
            w, bw = c.gu_ring.next()
            S.dma("pool", w, c.d_hwv[jm, vp], writes=[bw])
            for b in range(NB):
                pv, bpv = c.bank()
                for i in range(2):
                    for k in range(KD):
                        _mm(S, pv[:, i * 128:(i + 1) * 128], c.uT[:, k, t0 + b * 128:t0 + (b + 1) * 128],
                            w[:, i, k, :], k == 0, k == KD - 1, rdu(k) + [bw], [bpv])
                S.op("act", lambda e, pv=pv, b=b, vp=vp: e.copy(out=h.vall[:, b, vp * 256:(vp + 1) * 256], in_=pv[:, 0:256]),
                     reads=[bpv], writes=[h.Bvall[b]])
        for hd in range(NHD):
            w, bw = c.gu_ring.next()
            S.dma("pool", w, c.d_hwqf[jm, hd], writes=[bw])
            pq, bpq = c.bank()
            pf, bpf = c.bank()
            for k in range(KD):
                _mm(S, pq, w[:, 0, k, :], c.uT[:, k, tsl], k == 0, k == KD - 1, [bw] + rdu(k), [bpq])
            for k in range(KD):
                _mm(S, pf, w[:, 1, k, :], c.uT[:, k, tsl], k == 0, k == KD - 1, [bw] + rdu(k), [bpf])
            qs, bqs = h.tmp.next()
            sg, bsg = h.tmp.next()
            lf, blf = h.tmp.next()
            kk, bkk = h.tmp.next()
            cs, bcs = h.tmp.next()
            eq, beq = h.tmp.next()
            ek, bek = h.tmp.next()
            _act(S, qs, pq, AF.Silu, [bpq], [bqs])
            _act(S, sg, pf, AF.Sigmoid, [bpf], [bsg])
            cb = (jm * 3) * NHD + hd
            _act(S, lf, sg, AF.Ln, [bsg, c.Bconst], [blf],
                 scale=h.hconst[:, cb + NHD:cb + NHD + 1], bias=h.hconst[:, cb:cb + 1])
            S.op("dve", lambda e, kk=kk, sg=sg, cb=cb: e.tensor_scalar(
                out=kk, in0=sg, scalar1=h.hconst[:, cb + 2 * NHD:cb + 2 * NHD + 1],
                scalar2=h.hconst[:, cb + NHD:cb + NHD + 1], op0=ALU.mult, op1=ALU.add),
                reads=[bsg, c.Bconst], writes=[bkk])
            S.op("dve", lambda e, cs=cs, lf=lf: e.tensor_tensor_scan(
                out=cs, data0=h.mask32[:, :], data1=lf, initial=0.0, op0=ALU.mult, op1=ALU.add),
                reads=[blf, c.Bconst], writes=[bcs])
            S.op("dve", lambda e, cs=cs: e.tensor_scalar(out=cs, in0=cs, scalar1=-80.0, scalar2=None, op0=ALU.max),
                 reads=[bcs], writes=[bcs], cost=650.0)
            _act(S, eq, cs, AF.Exp, [bcs], [beq])
            _act(S, ek, cs, AF.Exp, [bcs], [bek], scale=-1.0)
            _tt(S, "dve", h.qT[:, hd, :], qs, eq, ALU.mult, [bqs, beq], [h.BqT[hd]])
            _tt(S, "dve", kk, kk, ek, ALU.mult, [bkk, bek], [bkk])
            S.op("pool", lambda e, kk=kk, hd=hd: e.tensor_copy(out=h.kT[:, hd, :], in_=kk),
                 reads=[bkk], writes=[h.BkT[hd]])
            eq3 = eq.rearrange("p (c j) -> p c j", j=HC)
            kd, bkd = h.kd_ring.next()
            _tt(S, "pool", kd.rearrange("p (c j) -> p c j", j=HC), kk.rearrange("p (c j) -> p c j", j=HC),
                eq3[:, :, HC - 1:HC].broadcast_to([128, NCH, HC]), ALU.mult, [bkk, beq], [bkd])
            S.op("pool", lambda e, eq3=eq3, hd=hd: e.tensor_copy(
                out=h.elast[:, hd, :].rearrange("p (c o) -> p c o", o=1), in_=eq3[:, :, HC - 1:HC]),
                reads=[beq], writes=[h.Bel[hd]])
            for b in range(NB):
                pt, bpt = c.qbank()
                ptb = pt.bitcast(BF16)[:, 0:128]
                S.op("pe", lambda e, ptb=ptb, kd=kd, b=b: e.transpose(ptb, kd[:, b * 128:(b + 1) * 128], c.ident[:, :]),
                     reads=[bkd, c.Bconst], writes=[bpt])
                S.op("act", lambda e, ptb=ptb, b=b, hd=hd: e.copy(out=h.kdtm[:, b, hd, :], in_=ptb),
                     reads=[bpt], writes=[h.Bkdtm[b][hd]])
        for gp in range(NHD // 2):
            w, bw = c.gu_ring.next()
            S.dma("pool", w, c.d_hwg[jm, gp], writes=[bw])
            for i in range(2):
                hd = gp * 2 + i
                pg, bpg = c.bank()
                for k in range(KD):
                    _mm(S, pg, w[:, i, k, :], c.uT[:, k, tsl], k == 0, k == KD - 1, [bw] + rdu(k), [bpg])
                _act(S, h.sgT[:, hd, :], pg, AF.Silu, [bpg], [h.BsgT[hd]])
        for b in range(NB):
            tok = slice(t0 + b * 128, t0 + (b + 1) * 128)
            bsl = slice(b * 128, (b + 1) * 128)
            for cc in range(NCB):
                for half in range(2):
                    hs = slice(half * 512, (half + 1) * 512)
                    eng = "dve" if (cc + half) % 2 == 0 else "act"
                    if eng == "dve":
                        S.op("dve", lambda e, hs=hs, cc=cc, b=b: e.tensor_scalar(
                            out=h.vm[:, cc, hs], in0=h.vall[:, b, hs], scalar1=h.cmask[:, cc:cc + 1], scalar2=None,
                            op0=ALU.mult), reads=[h.Bvall[b], c.Bconst], writes=[h.Bvm[cc][half]])
                    else:
                        _act(S, h.vm[:, cc, hs], h.vall[:, b, hs], AF.Copy, [h.Bvall[b], c.Bconst], [h.Bvm[cc][half]],
                             scale=h.cmask[:, cc:cc + 1])
            for hb in range(2):
                pa, bpa = c.bank()
                for i in range(4):
                    hd = hb * 4 + i
                    _mm(S, pa[:, i * 128:(i + 1) * 128], h.kT[:, hd, bsl], h.qT[:, hd, bsl], True, True,
                        [h.BkT[hd], h.BqT[hd]], [bpa])
                _tt(S, "dve", h.attn[:, hb * 4:(hb + 1) * 4, :], pa.rearrange("p (i l) -> p i l", i=4),
                    h.maskbd[:, :].unsqueeze(1).broadcast_to([128, 4, 128]), ALU.mult,
                    [bpa, c.Bconst], [h.Battn[hb]])
            pos = []
            for hb in range(2):
                po, bpo = c.bank()
                pos.append((po, bpo))
                for i in range(4):
                    hd = hb * 4 + i
                    _mm(S, po[:, i * 128:(i + 1) * 128], h.vall[:, b, hd * 128:(hd + 1) * 128], h.attn[:, hd, :],
                        True, True, [h.Bvall[b], h.Battn[hb]], [bpo])
            pis = [c.bank() for _ in range(2)]
            for cc in range(NCB):
                ch = (t0 + b * 128) // HC % (T // HC)
                chunk_in_sub = b * NCB + cc
                for hd in range(NHD):
                    par = h.par[jm][hd]
                    pi, bpi = pis[hd // 4]
                    i = hd % 4
                    _mm(S, pi[:, i * 128 + cc * HC:i * 128 + (cc + 1) * HC], h.sbf[par][:, jm, hd, :],
                        h.qT[:, hd, b * 128 + cc * HC:b * 128 + (cc + 1) * HC], True, True,
                        [h.Bsbf[par][jm][hd], h.BqT[hd]], [bpi])
                    pu, bpu = c.qbank()
                    _mm(S, pu, h.kdtm[:, b, hd, :], h.vm[:, cc, hd * 128:(hd + 1) * 128], True, True,
                        [h.Bkdtm[b][hd], h.Bvm[cc][hd // 4]], [bpu])
                    S.op("dve", lambda e, pu=pu, hd=hd, chunk_in_sub=chunk_in_sub: e.scalar_tensor_tensor(
                        out=h.S[:, jm, hd, :], in0=h.S[:, jm, hd, :],
                        scalar=h.elast[:, hd, chunk_in_sub:chunk_in_sub + 1], in1=pu,
                        op0=ALU.mult, op1=ALU.add),
                        reads=[bpu, h.BS[jm][hd], h.Bel[hd]], writes=[h.BS[jm][hd]])
                    S.op("act", lambda e, hd=hd, par=par: e.copy(out=h.sbf[1 - par][:, jm, hd, :], in_=h.S[:, jm, hd, :]),
                         reads=[h.BS[jm][hd]], writes=[h.Bsbf[1 - par][jm][hd]])
                    h.par[jm][hd] = 1 - par
            for hb in range(2):
                pi, bpi = pis[hb]
                po, bpo = pos[hb]
                oi, boi = h.tmp.next()
                osum, bos = h.tmp.next()
                S.op("act", lambda e, oi=oi, pi=pi: e.copy(out=oi, in_=pi), reads=[bpi], writes=[boi])
                _tt(S, "dve", osum, po, oi, ALU.add, [bpo, boi], [bos])
                osq, bosq = c.sg_ring.next()
                _act(S, osq, osum, AF.Square, [bos], [bosq])
                pss, bpss = c.bank()
                _mm(S, pss, c.ones_bf[:, :], osq, True, True, [bosq, c.Bconst], [bpss])
                rs, brs = h.tmp.next()
                _act(S, rs, pss, AF.Ln, [bpss, c.Bconst], [brs], scale=1.0 / 128, bias=c.eps_col[:, 0:1])
                _act(S, rs, rs, AF.Exp, [brs], [brs], scale=-0.5)
                _tt(S, "dve", osum, osum, rs, ALU.mult, [bos, brs], [bos])
                S.op("dve", lambda e, osum=osum, hb=hb, bsl=bsl: e.scalar_tensor_tensor(
                    out=h.ogT[:, hb * 4:(hb + 1) * 4, bsl], in0=osum.rearrange("p (i l) -> p i l", i=4),
                    scalar=h.hnw[:, jm:jm + 1], in1=h.sgT[:, hb * 4:(hb + 1) * 4, bsl],
                    op0=ALU.mult, op1=ALU.mult),
                    reads=[bos, c.Bconst] + [h.BsgT[hb * 4 + i] for i in range(4)],
                    writes=[h.BsgT[hb * 4 + i] for i in range(4)])
        for dc in range(KD):
            w, bw = c.dn_ring.next()
            S.dma("pool", w[:, 0:KD, :], c.d_hwo[jm, dc], writes=[bw])
            ps, bps = c.bank()
            for k in range(KD):
                _mm(S, ps, w[:, k, :], h.ogT[:, k, :], k == 0, k == KD - 1,
                    [bw, h.BsgT[k]], [bps])
            _tt(S, "dve", c.xT[:, dc, tsl], ps, c.xT[:, dc, tsl], ALU.add, [bps, c.BxT[dc]], [c.BxT[dc]])


def emit_hgrn_consts(S, c):
    h = c.h
    ex = h.lbtmp
    S.dma("sp", ex[:, 0:4 * NHD], c.d_hlb, writes=[c.Bconst])
    _act(S, ex[:, 0:4 * NHD], ex[:, 0:4 * NHD], AF.Exp, [c.Bconst], [c.Bconst])
    den = ex[:, 4 * NHD:5 * NHD]
    e = lambda i: ex[:, i * NHD:(i + 1) * NHD]
    _tt(S, "dve", den, e(0), e(1), ALU.add, [c.Bconst], [c.Bconst])
    _tt(S, "dve", den, den, e(2), ALU.add, [c.Bconst], [c.Bconst])
    _tt(S, "dve", den, den, e(3), ALU.add, [c.Bconst], [c.Bconst])
    rden = ex[:, 5 * NHD:6 * NHD]
    S.op("dve", lambda e_: e_.reciprocal(out=rden, in_=den), reads=[c.Bconst], writes=[c.Bconst])
    s123 = ex[:, 6 * NHD:7 * NHD]
    _tt(S, "dve", s123, e(1), e(2), ALU.add, [c.Bconst], [c.Bconst])
    _tt(S, "dve", s123, s123, e(3), ALU.add, [c.Bconst], [c.Bconst])
    for jm, num in ((0, e(1)), (1, s123)):
        lb = h.hconst[:, (jm * 3) * NHD:(jm * 3 + 1) * NHD]
        oml = h.hconst[:, (jm * 3 + 1) * NHD:(jm * 3 + 2) * NHD]
        noml = h.hconst[:, (jm * 3 + 2) * NHD:(jm * 3 + 3) * NHD]
        _tt(S, "dve", lb, num, rden, ALU.mult, [c.Bconst], [c.Bconst])
        S.op("dve", lambda e_, lb=lb, oml=oml: e_.tensor_scalar(out=oml, in0=lb, scalar1=-1.0, scalar2=1.0,
                                                                 op0=ALU.mult, op1=ALU.add),
             reads=[c.Bconst], writes=[c.Bconst])
        S.op("dve", lambda e_, noml=noml, oml=oml: e_.tensor_scalar(out=noml, in0=oml, scalar1=-1.0, scalar2=None,
                                                                    op0=ALU.mult),
             reads=[c.Bconst], writes=[c.Bconst])
    S.op("pool", lambda e_: e_.memset(h.mask32[:, :], 1.0), writes=[c.Bconst])
    S.op("pool", lambda e_: e_.memset(h.mask32[:, :].rearrange("p (c j) -> p c j", j=HC)[:, :, 0:1], 0.0),
         reads=[c.Bconst], writes=[c.Bconst])
    S.op("pool", lambda e_: e_.memset(h.cmask[:, :], 1.0), writes=[c.Bconst])
    for cc in range(NCB):
        S.op("pool", lambda e_, cc=cc: e_.affine_select(
            out=h.cmask[:, cc:cc + 1], in_=h.cmask[:, cc:cc + 1], pattern=[[0, 1]], compare_op=ALU.is_ge,
            fill=0.0, base=-HC * cc, channel_multiplier=1), reads=[c.Bconst], writes=[c.Bconst])
        S.op("pool", lambda e_, cc=cc: e_.affine_select(
            out=h.cmask[:, cc:cc + 1], in_=h.cmask[:, cc:cc + 1], pattern=[[0, 1]], compare_op=ALU.is_ge,
            fill=0.0, base=HC * cc + HC - 1, channel_multiplier=-1), reads=[c.Bconst], writes=[c.Bconst])
    S.op("pool", lambda e_: e_.memset(h.maskbd[:, :], 1.0), writes=[c.Bconst])
    S.op("pool", lambda e_: e_.affine_select(out=h.maskbd[:, :], in_=h.maskbd[:, :], pattern=[[1, 128]],
                                             compare_op=ALU.is_ge, fill=0.0, base=0, channel_multiplier=-1),
         reads=[c.Bconst], writes=[c.Bconst])
    for cc in range(NCB):
        S.op("dve", lambda e_, cc=cc: e_.tensor_scalar(
            out=h.maskbd[:, cc * HC:(cc + 1) * HC], in0=h.maskbd[:, cc * HC:(cc + 1) * HC],
            scalar1=h.cmask[:, cc:cc + 1], scalar2=None, op0=ALU.mult), reads=[c.Bconst], writes=[c.Bconst])
    S.op("pool", lambda e_: e_.memset(c.ident[:, :], 0.0), writes=[c.Bconst])
    S.op("pool", lambda e_: e_.affine_select(out=c.ident[:, :], in_=c.ident[:, :], pattern=[[-1, 128]],
                                             compare_op=ALU.not_equal, fill=1.0, base=0, channel_multiplier=1),
         reads=[c.Bconst], writes=[c.Bconst])
    S.op("pool", lambda e_: e_.memset(h.S[:], 0.0), writes=[b for r in h.BS for b in r])
    for par in range(2):
        S.op("pool", lambda e_, par=par: e_.memset(h.sbf[par][:], 0.0), writes=[b for r in h.Bsbf[par] for b in r])
    S.dma("sp", h.hnw[:, :], c.d_hnw, writes=[c.Bconst])


MT_ = 256
MG = 4
MHG = 8


def _pad_ap(ap2):
    return bass.AP(ap2.tensor, ap2.offset, [[ap2.ap[0][0], 128], [192, 2], [1, 64]])


def emit_mamba(S, c, l, jm):
    enter_phase(S, c, "mamba")
    emit_rmsnorm(S, c, (l * 3 + 1) * KD)
    m = c.m
    NCK = MT_ // 128
    for xpg_, bx_ in zip(m.xpad_ring.aps, m.xpad_ring.bufs):
        S.op("pool", lambda e, xpg_=xpg_: e.memset(xpg_, 0.0), writes=[bx_])
    S.op("pool", lambda e: e.memset(m.ppad[:], 0.0), writes=m.Bppad)
    for g in range(MG):
        for i in range(MHG // 2):
            ci = g * 4 + i
            S.op("act", lambda e, ci=ci: e.copy(out=_pad_ap(m.ppad[:, 2 * ci:2 * ci + 2, :].rearrange("p a b -> p (a b)")),
                                                  in_=m.prev[:, jm, ci * 128:(ci + 1) * 128].rearrange("p (a b) -> p a b", a=2)),
                 reads=[m.Bprev[jm][g]], writes=[m.Bppad[g]])
    for sub in range(T // MT_):
        t0 = sub * MT_
        tsl = slice(t0, t0 + MT_)
        rdu = lambda k: [c.BuT[k][t0 // 512]]
        for ck in range(NCK):
            tok = slice(t0 + ck * 128, t0 + (ck + 1) * 128)
            pd, bpd = c.qbank()
            for k in range(KD):
                _mm(S, pd[:, 0:32], c.uT[:, k, tok], m.wdt[:, jm, k, :], k == 0, k == KD - 1, rdu(k) + [m.Bwdt], [bpd])
            dtc, dAc, ddc = m.dt[:, ck, :], m.dA[:, ck, :], m.dtdec[:, ck, :]
            _tt(S, "dve", dtc, pd[:, 0:32], m.dtb[:, jm, :], ALU.add, [bpd, c.Bconst], [m.Bdt[ck]])
            _act(S, dtc, dtc, AF.Exp, [m.Bdt[ck]], [m.Bdt[ck]])
            _act(S, dtc, dtc, AF.Ln, [m.Bdt[ck]], [m.Bdt[ck]], bias=1.0)
            _tt(S, "dve", dAc, dtc, m.Aneg[:, jm, :], ALU.mult, [m.Bdt[ck], c.Bconst], [m.BdA[ck]])
            pr_, bpr = c.qbank()
            _mm(S, pr_[:, 0:32], m.Umat[:, :], dAc, True, True, [m.BdA[ck], c.Bconst], [bpr])
            _act(S, ddc, pr_[:, 0:32], AF.Exp, [bpr], [m.Bdtdec[ck]])
            _tt(S, "dve", ddc, ddc, dtc, ALU.mult, [m.Bdtdec[ck], m.Bdt[ck]], [m.Bdtdec[ck]])
        for pr in range(20):
            w, bw = c.gu_ring.next()
            S.dma("pool", w, c.d_mwin[jm, pr], writes=[bw])
            for i in range(2):
                ch = pr * 2 + i
                ps, bps = c.bank()
                for k in range(KD):
                    _mm(S, ps[:, 0:MT_], w[:, i, k, :], c.uT[:, k, tsl], k == 0, k == KD - 1, [bw] + rdu(k), [bps])
                if ch < 16:
                    _act(S, m.siluz[:, ch, :], ps[:, 0:MT_], AF.Silu, [bps], [m.Bsz[ch]])
                    continue
                ci = ch - 16
                raw, braw = m.raw_ring.next()
                acc, bacc = m.acc_ring.next()
                S.op("pool", lambda e, raw=raw, ci=ci: e.tensor_copy(out=raw[:, 0:3], in_=m.tails[:, jm, ci, :]),
                     reads=[m.Btail[jm][ci]], writes=[braw])
                S.op("act", lambda e, raw=raw, ps=ps: e.copy(out=raw[:, 3:3 + MT_], in_=ps[:, 0:MT_]),
                     reads=[bps], writes=[braw])
                S.op("pool", lambda e, raw=raw, ci=ci: e.tensor_copy(out=m.tails[:, jm, ci, :], in_=raw[:, MT_:MT_ + 3]),
                     reads=[braw], writes=[m.Btail[jm][ci]])
                cw = lambda tap, ci=ci: m.mconv[:, jm, ci, tap:tap + 1]
                S.op("dve", lambda e, raw=raw, acc=acc, cw=cw: e.tensor_scalar(
                    out=acc, in0=raw[:, 0:MT_], scalar1=cw(0), scalar2=None, op0=ALU.mult),
                    reads=[braw, c.Bconst], writes=[bacc])
                for tap in range(1, 4):
                    S.op("dve", lambda e, raw=raw, acc=acc, cw=cw, tap=tap: e.scalar_tensor_tensor(
                        out=acc, in0=raw[:, tap:tap + MT_], scalar=cw(tap), in1=acc, op0=ALU.mult, op1=ALU.add),
                        reads=[braw, bacc, c.Bconst], writes=[bacc])
                if ci < 16:
                    dst, bdst = m.xc[:, ci, :], m.Bxc[ci]
                elif ci < 20:
                    dst, bdst = m.BT[:, ci - 16, :], m.BBT[ci - 16]
                else:
                    dst, bdst = m.CT[:, ci - 20, :], m.BCT[ci - 20]
                _act(S, dst, acc, AF.Silu, [bacc, c.Bconst], [bdst], bias=cw(4))
        for ck in range(NCK):
            csl = slice(ck * 128, (ck + 1) * 128)
            for g in range(MG):
                hs = slice(g * MHG, (g + 1) * MHG)
                pt, bpt = c.bank()
                ptb = pt.bitcast(BF16)[:, 0:512]
                for i in range(4):
                    ci = g * 4 + i
                    S.op("pe", lambda e, ptb=ptb, ci=ci, i=i, csl=csl: e.transpose(
                        ptb[:, i * 128:(i + 1) * 128], m.xc[:, ci, csl], c.ident[:, :]),
                        reads=[m.Bxc[ci], c.Bconst], writes=[bpt], cost=120.0)
                pt4 = ptb.rearrange("p (c a b) -> p c a b", c=4, a=2)
                xpg, bxpg = m.xpad_ring.next()
                xp = xpg.rearrange("p a b -> p (a b)")
                xp4 = bass.AP(xp.tensor, xp.offset, [[xp.ap[0][0], 128], [256, 4], [192, 2], [1, 64]])
                _tt(S, "dve", xp4, pt4,
                    m.dt[:, ck, hs].rearrange("p (c a) -> p c a", a=2).unsqueeze(3).broadcast_to([128, 4, 2, 64]),
                    ALU.mult, [bpt, m.Bdt[ck]], [bxpg])
                xddg, bxddg = m.xdd_ring.next()
                _tt(S, "dve", xddg.rearrange("p (c a b) -> p c a b", c=4, a=2), pt4,
                    m.dtdec[:, ck, hs].rearrange("p (c a) -> p c a", a=2).unsqueeze(3).broadcast_to([128, 4, 2, 64]),
                    ALU.mult, [bpt, m.Bdtdec[ck]], [bxddg])
                ptB, bptB = c.qbank()
                ptBb = ptB.bitcast(BF16)[:, 0:128]
                S.op("pe", lambda e, ptBb=ptBb, g=g, csl=csl: e.transpose(ptBb, m.BT[:, g, csl], c.ident[:, :]),
                     reads=[m.BBT[g], c.Bconst], writes=[bptB], cost=120.0)
                btmg, bbtmg = m.btm_ring.next()
                S.op("act", lambda e, ptBb=ptBb, btmg=btmg: e.copy(out=btmg, in_=ptBb), reads=[bptB], writes=[bbtmg],
                     cost=350.0)
                R, bR = m.R_ring.next()
                _tt(S, "dve", R, m.Tmat[:, :].unsqueeze(1).broadcast_to([128, MHG, 128]),
                    m.dA[:, ck, hs].unsqueeze(2).broadcast_to([128, MHG, 128]), ALU.mult, [m.BdA[ck], c.Bconst], [bR])
                R2 = R.rearrange("p h l -> p (h l)")
                LT, bLT = m.LT_ring.next()
                ecs, becs = m.ecs_ring.next()
                for half in range(2):
                    hsl = slice(half * 512, (half + 1) * 512)
                    p1, bp1 = c.bank()
                    _mm(S, p1, m.UmatR[:, :], R2[:, hsl], True, True, [bR, c.Bconst], [bp1])
                    _act(S, LT.rearrange("p h l -> p (h l)")[:, hsl], p1, AF.Exp, [bp1], [bLT])
                    p2, bp2 = c.bank()
                    _mm(S, p2, m.onesR[:, :], R2[:, hsl], True, True, [bR, c.Bconst], [bp2])
                    _act(S, ecs.rearrange("p h l -> p (h l)")[:, hsl], p2, AF.Exp, [bp2], [becs])
                pg, bpg = c.qbank()
                _mm(S, pg, m.BT[:, g, csl], m.CT[:, g, csl], True, True, [m.BBT[g], m.BCT[g]], [bpg])
                Gm, bGm = m.Gm_ring.next()
                _tt(S, "dve", Gm, pg, m.maskc[:, :], ALU.mult, [bpg, c.Bconst], [bGm])
                MTt, bMT = m.MT_ring.next()
                _tt(S, "pool", MTt, LT, Gm.unsqueeze(1).broadcast_to([128, MHG, 128]), ALU.mult, [bLT, bGm], [bMT])
                Cp, bCp = m.Cp_ring.next()
                _tt(S, "pool", Cp, ecs, m.CT[:, g, csl].unsqueeze(1).broadcast_to([128, MHG, 128]), ALU.mult,
                    [becs, m.BCT[g]], [bCp])
                py, bpy = c.bank()
                for i in range(4):
                    ci = g * 4 + i
                    o = py[:, i * 128:(i + 1) * 128]
                    for a in range(2):
                        _mm(S, o, xpg[:, 2 * i + a, :], MTt[:, 2 * i + a, :], a == 0, False, [bxpg, bMT], [bpy])
                    for a in range(2):
                        _mm(S, o, m.ppad[:, 2 * ci + a, :], Cp[:, 2 * i + a, :], False, a == 1, [m.Bppad[g], bCp], [bpy])
                pu, bpu = c.bank()
                _mm(S, pu, btmg, xddg, True, True, [bbtmg, bxddg], [bpu])
                pv = m.prev[:, jm, g * 512:(g + 1) * 512]
                _tt(S, "dve", pv.rearrange("p (h q) -> p h q", h=MHG), pv.rearrange("p (h q) -> p h q", h=MHG),
                    ecs[:, :, 127:128].broadcast_to([128, MHG, 64]), ALU.mult, [m.Bprev[jm][g], becs], [m.Bprev[jm][g]])
                _tt(S, "dve", pv, pv, pu, ALU.add, [m.Bprev[jm][g], bpu], [m.Bprev[jm][g]])
                for i in range(4):
                    ci = g * 4 + i
                    S.op("act", lambda e, ci=ci: e.copy(
                        out=_pad_ap(m.ppad[:, 2 * ci:2 * ci + 2, :].rearrange("p a b -> p (a b)")),
                        in_=m.prev[:, jm, ci * 128:(ci + 1) * 128].rearrange("p (a b) -> p a b", a=2)),
                        reads=[m.Bprev[jm][g]], writes=[m.Bppad[g]])
                yg, byg = m.yg_ring.next()
                for i in range(4):
                    ci = g * 4 + i
                    S.op("dve", lambda e, i=i, ci=ci, yg=yg, py=py, csl=csl: e.scalar_tensor_tensor(
                        out=yg[:, i, :], in0=m.xc[:, ci, csl], scalar=m.dcol[:, jm, ci:ci + 1],
                        in1=py[:, i * 128:(i + 1) * 128], op0=ALU.mult, op1=ALU.add),
                        reads=[m.Bxc[ci], bpy, c.Bconst], writes=[byg])
                _tt(S, "dve", yg, yg, m.siluz[:, g * 4:(g + 1) * 4, csl], ALU.mult,
                    [byg] + [m.Bsz[g * 4 + i] for i in range(4)], [byg])
                ysq, bysq = c.sg_ring.next()
                _act(S, ysq, yg.rearrange("p i l -> p (i l)"), AF.Square, [byg], [bysq])
                pn, bpn = c.qbank()
                for i in range(4):
                    _mm(S, pn, c.ones_bf[:, :], ysq[:, i * 128:(i + 1) * 128], i == 0, i == 3, [bysq, c.Bconst], [bpn])
                rs, brs = m.rs_ring.next()
                _act(S, rs, pn, AF.Ln, [bpn, c.Bconst], [brs], scale=1.0 / 512, bias=c.eps_col[:, 0:1])
                _act(S, rs, rs, AF.Exp, [brs], [brs], scale=-0.5)
                for i in range(4):
                    ci = g * 4 + i
                    S.op("dve", lambda e, i=i, ci=ci, yg=yg, rs=rs, csl=csl: e.scalar_tensor_tensor(
                        out=m.ynT[:, ci, csl], in0=yg[:, i, :], scalar=m.mnw[:, jm, ci:ci + 1], in1=rs,
                        op0=ALU.mult, op1=ALU.mult),
                        reads=[byg, brs, c.Bconst], writes=[m.Byn[ci]])
        for dc in range(KD):
            w, bw = c.dn_ring.next()
            S.dma("pool", w[:, 0:16, :], c.d_mwo[jm, dc], writes=[bw])
            ps, bps = c.bank()
            for k in range(16):
                _mm(S, ps[:, 0:MT_], w[:, k, :], m.ynT[:, k, :], k == 0, k == 15, [bw, m.Byn[k]], [bps])
            _tt(S, "dve", c.xT[:, dc, tsl], ps[:, 0:MT_], c.xT[:, dc, tsl], ALU.add, [bps, c.BxT[dc]], [c.BxT[dc]])


def emit_mamba_consts(S, c):
    m = c.m
    B = [c.Bconst]
    S.dma("sp", m.mconv[:], c.d_mconv, writes=B)
    S.dma("sp", m.dcol[:], c.d_mdcol, writes=B)
    S.dma("sp", m.mnw[:], c.d_mnw, writes=B)
    S.dma("pool", m.wdt[:], c.d_mwdt, writes=[m.Bwdt])
    for j in range(2):
        S.dma("sp", m.dtb[:, j, :], c.d_mdtb[j:j + 1, :].partition_broadcast(128), writes=B)
        S.dma("sp", m.Aneg[:, j, :], c.d_malog[j:j + 1, :].partition_broadcast(128), writes=B)
    _act(S, m.Aneg[:], m.Aneg[:], AF.Exp, B, B)
    S.op("dve", lambda e: e.tensor_scalar(out=m.Aneg[:], in0=m.Aneg[:], scalar1=-1.0, scalar2=None, op0=ALU.mult),
         reads=B, writes=B)
    S.op("pool", lambda e: e.memset(m.onesf[:, :], 1.0), writes=B)
    S.op("pool", lambda e: e.memset(m.Tmat[:, :], 1.0), writes=B)
    S.op("pool", lambda e: e.affine_select(out=m.Tmat[:, :], in_=m.Tmat[:, :], pattern=[[1, 128]],
                                           compare_op=ALU.is_ge, fill=0.0, base=0, channel_multiplier=-1),
         reads=B, writes=B)
    S.op("pool", lambda e: e.memset(m.Umat[:, :], 1.0), writes=B)
    S.op("pool", lambda e: e.affine_select(out=m.Umat[:, :], in_=m.Umat[:, :], pattern=[[-1, 128]],
                                           compare_op=ALU.is_gt, fill=0.0, base=0, channel_multiplier=1),
         reads=B, writes=B)
    S.op("dve", lambda e: e.tensor_copy(out=m.UmatR[:, :], in_=m.Umat[:, :]), reads=B, writes=B)
    S.op("dve", lambda e: e.tensor_copy(out=m.onesR[:, :], in_=m.onesf[:, :]), reads=B, writes=B)
    S.op("pool", lambda e: e.memset(m.prev[:], 0.0), writes=[b for r in m.Bprev for b in r])
    S.op("pool", lambda e: e.memset(m.tails[:], 0.0), writes=[b for r in m.Btail for b in r])


def build_program(seq=SEQ, depth=DEPTH, plan=None):
    nc = bass.Bass("TRN2", target_bir_lowering=False)
    c = Ctx()
    c.nc = nc
    ntiles = seq // T
    dt = lambda name, shape: nc.dram_tensor(name, shape, F32, kind="ExternalInput").ap()
    c.d_x = dt("xT", [D, seq])
    c.d_wgu = dt("wgu", [DEPTH, 2, KF, 128, 2, KD, 128])
    c.d_wd = dt("wd", [DEPTH, 2, KD, 128, KF, 128])
    NWC = (DEPTH * 3 + 1) * KD
    c.d_nw = dt("nw", [128, NWC])
    c.d_hwqf = dt("hwqf", [2, NHD, 128, 2, KD, 128])
    c.d_hwg = dt("hwg", [2, NHD // 2, 128, 2, KD, 128])
    c.d_hwv = dt("hwv", [2, NHD // 2, 128, 2, KD, 128])
    c.d_hwo = dt("hwo", [2, KD, 128, KD, 128])
    c.d_hlb = dt("hlb", [128, 4 * NHD])
    c.d_hnw = dt("hnw", [128, 2])
    c.d_mwin = dt("mwin", [2, 20, 128, 2, KD, 128])
    c.d_mwdt = dt("mwdt", [128, 2, KD, 32])
    c.d_mconv = dt("mconv", [128, 2, 24, 5])
    c.d_mdcol = dt("mdcol", [128, 2, 16])
    c.d_mnw = dt("mnw", [128, 2, 16])
    c.d_mdtb = dt("mdtb", [2, 32])
    c.d_malog = dt("malog", [2, 32])
    c.d_mwo = dt("mwo", [2, KD, 128, 16, 128])
    c.d_out = nc.dram_tensor("outT", [D, seq], F32, kind="ExternalOutput").ap()
    from contextlib import ExitStack
    with ExitStack() as st:
        sb = lambda n, s, d: st.enter_context(nc.sbuf_tensor(n, s, d))
        c.xT = sb("xT_sb", [128, KD, T], F32)
        c.uT = sb("uT_sb", [128, KD, T], BF16)
        c.rstd = sb("rstd_sb", [128, T], F32)
        c.nw = sb("nw_sb", [128, NWC], F32)
        c.ones_bf = sb("ones_bf", [128, 128], BF16)
        c.ident = sb("ident_bf", [128, 128], BF16)
        c.eps_col = sb("eps_col", [128, 1], F32)
        gu = sb("gu_ring", [128, 4, 2, KD, 128], BF16)
        dn = sb("dn_ring", [128, 2, KF, 128], BF16)
        sq = sb("sq_ring", [128, 2, T], BF16)
        sg = sb("sg_ring", [128, 3, 512], BF16)
        c.gu_ring = Ring("gu", [gu[:, i] for i in range(4)])
        c.dn_ring = Ring("dn", [dn[:, i] for i in range(2)])
        c.sq_ring = Ring("sq", [sq[:, i] for i in range(2)])
        c.sg_ring = Ring("sg", [sg[:, i] for i in range(3)])
        h = c.h = Ctx()
        h.S = sb("h_S", [128, 2, NHD, 128], F32)
        h.sbf = [sb("h_sbf%d" % i, [128, 2, NHD, 128], BF16) for i in range(2)]
        h.hconst = sb("h_const", [128, 2 * 3 * NHD], F32)
        h.lbtmp = sb("h_lbtmp", [128, 7 * NHD], F32)
        h.hnw = sb("h_nw", [128, 2], F32)
        h.mask32 = sb("h_mask32", [128, TM], F32)
        h.cmask = sb("h_cmask", [128, NCB], F32)
        h.maskbd = sb("h_maskbd", [128, 128], F32)
        h.BS = [[Buf("hS%d_%d" % (j, i)) for i in range(NHD)] for j in range(2)]
        h.Bsbf = [[[Buf("hsbf%d_%d_%d" % (p, j, i)) for i in range(NHD)] for j in range(2)] for p in range(2)]
        h.par = [[0] * NHD for _ in range(2)]
        ARENA = 76 * 1024
        arena = sb("arena", [128, ARENA], mybir.dt.uint8)
        off = [0]

        def carve(shape, dtype, reset=False):
            if reset:
                off[0] = 0
            n = int(np.prod(shape)) * (2 if dtype == BF16 else 4)
            ap = arena[:, off[0]:off[0] + n].bitcast(dtype)
            off[0] += n
            assert off[0] <= ARENA, (off[0], ARENA)
            if len(shape) > 1:
                names = " ".join("a%d" % i for i in range(len(shape)))
                kw = {"a%d" % i: shape[i] for i in range(len(shape))}
                ap = ap.rearrange("p (%s) -> p %s" % (names, names), **kw)
            return ap

        c.actT = carve([KF, T], BF16, reset=True)
        NB = TM // 128
        h.qT = carve([NHD, TM], BF16, reset=True)
        h.kT = carve([NHD, TM], BF16)
        h.sgT = carve([NHD, TM], BF16)
        h.ogT = h.sgT
        h.vall = carve([NB, 1024], BF16)
        h.vm = carve([NCB, 1024], BF16)
        h.kdtm = carve([NB, NHD, 128], BF16)
        h.attn = carve([NHD, 128], BF16)
        h.elast = carve([NHD, TM // HC], F32)
        tmp = carve([13, TM], F32)
        kd = carve([2, TM], BF16)
        h.tmp = Ring("htmp", [tmp[:, i, :] for i in range(13)])
        h.kd_ring = Ring("hkd", [kd[:, i, :] for i in range(2)])
        h.BqT = [Buf("hq%d" % i) for i in range(NHD)]
        h.BkT = [Buf("hk%d" % i) for i in range(NHD)]
        h.BsgT = [Buf("hsg%d" % i) for i in range(NHD)]
        h.Bvall = [Buf("hvall%d" % i) for i in range(NB)]
        h.Bvm = [[Buf("hvm%d_%d" % (i, j)) for j in range(2)] for i in range(NCB)]
        h.Bkdtm = [[Buf("hkdtm%d_%d" % (b, i)) for i in range(NHD)] for b in range(NB)]
        h.Battn = [Buf("hattn%d" % i) for i in range(2)]
        h.Bel = [Buf("hel%d" % i) for i in range(NHD)]


        m = c.m = Ctx()
        m.prev = sb("m_prev", [128, 2, 2048], F32)
        m.tails = sb("m_tails", [128, 2, 24, 3], F32)
        m.mconv = sb("m_conv", [128, 2, 24, 5], F32)
        m.dcol = sb("m_dcol", [128, 2, 16], F32)
        m.mnw = sb("m_nw", [128, 2, 16], F32)
        m.wdt = sb("m_wdt", [128, 2, KD, 32], BF16)
        m.dtb = sb("m_dtb", [128, 2, 32], F32)
        m.Aneg = sb("m_Aneg", [128, 2, 32], F32)
        m.onesf = sb("m_onesf", [128, 128], F32)
        m.Tmat = sb("m_Tmat", [128, 128], F32)
        m.Umat = sb("m_Umat", [128, 128], F32)
        m.UmatR = sb("m_UmatR", [128, 128], F32R)
        m.onesR = sb("m_onesR", [128, 128], F32R)
        m.Rsb = sb("m_R", [128, 1, MHG, 128], F32R)
        m.maskc = m.Tmat
        m.siluz = carve([16, MT_], BF16, reset=True)
        m.ynT = m.siluz
        m.xc = carve([16, MT_], BF16)
        m.BT = carve([MG, MT_], BF16)
        m.CT = carve([MG, MT_], BF16)
        m.ppad = carve([32, 128], BF16)
        NCK_ = MT_ // 128
        m.dt = carve([NCK_, 32], F32)
        m.dA = carve([NCK_, 32], F32)
        m.dtdec = carve([NCK_, 32], F32)
        mk = lambda name, n, shape, dtype: Ring(name, [carve(shape, dtype) for _ in range(n)])
        m.xpad_ring = mk("mxpad", 3, [MHG, 128], BF16)
        m.xdd_ring = mk("mxdd", 3, [512], BF16)
        m.btm_ring = mk("mbtm", 3, [128], BF16)
        m.R_ring = Ring("mR", [m.Rsb[:, 0]])
        m.LT_ring = mk("mLT", 2, [MHG, 128], BF16)
        m.ecs_ring = mk("mecs", 2, [MHG, 128], F32)
        m.Gm_ring = mk("mGm", 2, [128], BF16)
        m.MT_ring = mk("mMT", 2, [MHG, 128], BF16)
        m.Cp_ring = mk("mCp", 2, [MHG, 128], BF16)
        m.yg_ring = mk("myg", 2, [4, 128], F32)
        m.rs_ring = mk("mrs", 2, [128], F32)
        m.raw_ring = mk("mraw", 2, [MT_ + 4], F32)
        m.acc_ring = mk("macc", 2, [MT_], F32)
        m.Bsz = [Buf("msz%d" % i) for i in range(16)]
        m.Bxc = [Buf("mxc%d" % i) for i in range(16)]
        m.BBT = [Buf("mBT%d" % i) for i in range(MG)]
        m.BCT = [Buf("mCT%d" % i) for i in range(MG)]
        m.Byn = m.Bsz
        m.Bwdt = Buf("mwdt")
        m.Bppad = [Buf("mppad%d" % i) for i in range(MG)]
        m.Bprev = [[Buf("mprev%d_%d" % (j, g)) for g in range(MG)] for j in range(2)]
        m.Btail = [[Buf("mtail%d_%d" % (j, i)) for i in range(24)] for j in range(2)]
        m.Bdt = [Buf("mdt%d" % i) for i in range(NCK_)]
        m.BdA = [Buf("mdA%d" % i) for i in range(NCK_)]
        m.Bdtdec = [Buf("mdtdec%d" % i) for i in range(NCK_)]

        psum = st.enter_context(nc.psum_tensor("psum", [128, 8, 512], F32))
        c.bank8 = Ring("bank", [psum[:, i, :] for i in range(8)])
        c.bank4 = Ring("bankm", [psum[:, i, :] for i in range(4)])
        c.qbank_ring = Ring("qbank", [psum[:, 4 + i, 0:128] for i in range(4)])
        c.qbank = c.qbank_ring.next
        c.bank = c.bank8.next
        c.BxT = [Buf("xT%d" % k) for k in range(KD)]
        c.BuT = [[Buf("uT%d_%d" % (k, hh)) for hh in range(NH)] for k in range(KD)]
        c.Bact = [[Buf("act%d_%d" % (k, hh)) for hh in range(NH)] for k in range(KF)]
        c.Brstd = [Buf("rstd%d" % hh) for hh in range(NH)]
        c.Bconst = Buf("const")
        c.Bout = Buf("out")
        flat = lambda x: [b for r in x for b in (flat(r) if isinstance(r, list) else [r])]
        c.phase_bufs = {
            "ffn": flat(c.Bact) + c.bank8.bufs,
            "hgrn": flat([h.BqT, h.BkT, h.BsgT, h.Bvall, h.Bvm, h.Bkdtm, h.Battn, h.Bel])
                    + h.tmp.bufs + h.kd_ring.bufs + c.bank4.bufs + c.qbank_ring.bufs,
            "mamba": flat([m.Bsz, m.Bxc, m.BBT, m.BCT, m.Bppad, m.Bdt, m.BdA, m.Bdtdec])
                     + sum([r.bufs for r in (m.xpad_ring, m.xdd_ring, m.btm_ring,
                                             m.R_ring, m.LT_ring, m.ecs_ring, m.Gm_ring, m.MT_ring, m.Cp_ring,
                                             m.yg_ring, m.rs_ring, m.raw_ring, m.acc_ring)], [])
                     + c.bank4.bufs + c.qbank_ring.bufs,
        }
        c.cur_phase = None
        S = Sched(nc)
        S.op("pool", lambda e: e.memset(c.ones_bf[:], 1.0), writes=[c.Bconst])
        S.op("pool", lambda e: e.memset(c.eps_col[:], EPS), writes=[c.Bconst])
        S.dma("sp", c.nw[:], c.d_nw, writes=[c.Bconst])
        emit_hgrn_consts(S, c)
        emit_mamba_consts(S, c)
        if plan is None:
            plan = ["ffn0", "mix", "ffn1"]
        for tile in range(ntiles):
            for k in range(KD):
                S.dma("sp", c.xT[:, k, :], c.d_x[k * 128:(k + 1) * 128, tile * T:(tile + 1) * T],
                      writes=[c.BxT[k]])
            for l in range(depth):
                for stg in plan:
                    if stg == "ffn0":
                        emit_ffn(S, c, l, 0)
                    elif stg == "ffn1":
                        emit_ffn(S, c, l, 1)
                    elif stg == "mix":
                        if l % 2 == 1:
                            emit_hgrn(S, c, l, l // 2)
                        else:
                            emit_mamba(S, c, l, l // 2)
                    elif stg == "mamba":
                        emit_mamba(S, c, l, l // 2)
                    elif stg == "hgrn":
                        emit_hgrn(S, c, l, l // 2)
            emit_final(S, c, tile)
        c.sbuf_left = nc.sbuf_bytes_remaining
        S.emit()
        c.makespan = S.makespan
    nc._ctx = c
    return nc


def prep_common(inputs):
    f32 = np.float32
    A = lambda k: np.asarray(inputs[k], f32)
    g = A("ffn_w_gate").reshape(DEPTH, 2, KD, 128, KF, 128)
    u = A("ffn_w_up").reshape(DEPTH, 2, KD, 128, KF, 128)
    wgu = np.stack([g, u], axis=0)
    wgu = np.ascontiguousarray(wgu.transpose(1, 2, 5, 4, 0, 3, 6))
    wd = A("ffn_w_down").reshape(DEPTH, 2, KF, 128, KD, 128)
    wd = np.ascontiguousarray(wd.transpose(0, 1, 4, 3, 2, 5))
    nw = np.concatenate([A("norm_w").reshape(DEPTH * 3, KD, 128),
                         A("final_norm_w").reshape(1, KD, 128)], axis=0)
    nw = np.ascontiguousarray(nw.transpose(2, 0, 1).reshape(128, -1))
    out = {"wgu": wgu, "wd": wd, "nw": nw}
    hw = A("h_w_in").reshape(2, KD, 128, 4, NHD, 128)
    qf = hw[:, :, :, 0:2]
    out["hwqf"] = np.ascontiguousarray(qf.transpose(0, 4, 2, 3, 1, 5))
    gg = hw[:, :, :, 3].reshape(2, KD, 128, NHD // 2, 2, 128)
    out["hwg"] = np.ascontiguousarray(gg.transpose(0, 3, 2, 4, 1, 5))
    vv = hw[:, :, :, 2].reshape(2, KD, 128, NHD // 2, 2, 128)
    out["hwv"] = np.ascontiguousarray(vv.transpose(0, 3, 2, 4, 1, 5))
    wo = A("h_w_out").reshape(2, KD, 128, KD, 128)
    out["hwo"] = np.ascontiguousarray(wo.transpose(0, 3, 2, 1, 4))
    lb = A("h_lb_logits").reshape(DEPTH, NHD, 128)
    out["hlb"] = np.ascontiguousarray(lb.transpose(2, 0, 1).reshape(128, -1))
    out["hnw"] = np.ascontiguousarray(A("h_norm_w").T)
    mw = A("m_w_in")
    win = mw[:, :, 0:5120].reshape(2, KD, 128, 20, 2, 128)
    out["mwin"] = np.ascontiguousarray(win.transpose(0, 3, 2, 4, 1, 5))
    wdt = mw[:, :, 5120:5152].reshape(2, KD, 128, 32)
    out["mwdt"] = np.ascontiguousarray(wdt.transpose(2, 0, 1, 3))
    cw = A("m_conv_w").reshape(2, 4, 24, 128)
    cb = A("m_conv_b").reshape(2, 1, 24, 128)
    out["mconv"] = np.ascontiguousarray(np.concatenate([cw, cb], axis=1).transpose(3, 0, 2, 1))
    dcol = np.repeat(A("m_d"), 64, axis=1).reshape(2, 16, 128)
    out["mdcol"] = np.ascontiguousarray(dcol.transpose(2, 0, 1))
    out["mnw"] = np.ascontiguousarray(A("m_norm_w").reshape(2, 16, 128).transpose(2, 0, 1))
    out["mdtb"] = np.ascontiguousarray(A("m_dt_bias"))
    out["malog"] = np.ascontiguousarray(A("m_a_log"))
    mwo = A("m_w_out").reshape(2, 16, 128, KD, 128)
    out["mwo"] = np.ascontiguousarray(mwo.transpose(0, 3, 2, 1, 4))
    return out


_NC_CACHE = {}


def kernel(**inputs):
    x = np.asarray(inputs["x"], np.float32)
    B = x.shape[0]
    common = prep_common(inputs)
    if "nc" not in _NC_CACHE:
        _NC_CACHE["nc"] = build_program()
    nc = _NC_CACHE["nc"]
    in_maps = []
    for b in range(B):
        m = dict(common)
        m["xT"] = np.ascontiguousarray(x[b].T)
        in_maps.append(m)
    res = run_bass_kernel_spmd(nc, in_maps, core_ids=list(range(B)))
    out = np.stack([np.ascontiguousarray(r["outT"].T) for r in res.results], axis=0)
    return out.astype(np.float32)
```
